# Optimizing a Trainium2 kernel written in Bass

```python
import math
import jax, jax.numpy as jnp
from jax import lax
import numpy as np

D_MODEL = 1024
BATCH = 8
SEQ = 4096
DEPTH = 2

MEM_LEN = 256
CHUNK = 64
NORM_EPS = 1e-6

GLA_HEADS = 4
GLA_DK = 64
GLA_DV = 96
GLA_RANK = 16
GLA_GATE_NORMALIZER = 16.0
GDN_HEADS = 4
GDN_DK = 96
GDN_DV = 96
GDN_CONV = 5
XA_HEADS = 4
XA_DH = 64

GLA_W = GLA_HEADS * GLA_DV
GDN_W = GDN_HEADS * GDN_DV
XA_W = XA_HEADS * XA_DH
MIX_W = GLA_W + GDN_W + XA_W
GDN_QKV_W = 2 * GDN_HEADS * GDN_DK + GDN_W

IN_SIZES = (
    GLA_HEADS * GLA_DK,
    GLA_HEADS * GLA_DK,
    GLA_W,
    GLA_W,
    2 * GLA_RANK,
    GDN_QKV_W,
    GDN_W,
    2 * GDN_HEADS,
    2 * GDN_HEADS,
    XA_W,
    XA_W,
)
IN_W = sum(IN_SIZES)

kernel_name = "hymba_gla_gdn_memxattn_encoder"


def _rmsnorm(x, w):
    xf = x.astype(jnp.float32)
    y = xf * lax.rsqrt(jnp.mean(xf * xf, axis=-1, keepdims=True) + NORM_EPS)
    return y * w.astype(jnp.float32)


def _l2norm(x):
    return x * lax.rsqrt(jnp.sum(x * x, axis=-1, keepdims=True) + NORM_EPS)


def _split_cols(t, sizes):
    out, start = [], 0
    for s in sizes:
        out.append(t[..., start:start + s])
        start += s
    return out


def _to_chunks(t):
    b, tl = t.shape[0], t.shape[1]
    t = t.reshape((b, tl // CHUNK, CHUNK) + t.shape[2:])
    return jnp.moveaxis(t, 3, 1)


def _from_chunks(t):
    t = jnp.moveaxis(t, 1, 3)
    b, n, c = t.shape[:3]
    return t.reshape((b, n * c) + t.shape[3:])


def _run_bidir(fn, fwd_args, bwd_args):
    o_f = _from_chunks(fn(*[_to_chunks(a) for a in fwd_args]))
    o_b = _from_chunks(fn(*[_to_chunks(a[:, ::-1]) for a in bwd_args]))[:, ::-1]
    return o_f + o_b


def _gla_chunked(q, k, v, g):
    c = q.shape[-2]
    b = jnp.cumsum(g, axis=-2)
    q_e = q * jnp.exp(b)
    k_e = k * jnp.exp(-b)
    incl = jnp.tril(jnp.ones((c, c), dtype=bool))
    a = jnp.where(incl, jnp.einsum('bhncd,bhnsd->bhncs', q_e, k_e), 0.0)
    o_intra = jnp.einsum('bhncs,bhnsv->bhncv', a, v)
    b_last = b[..., -1:, :]
    chunk_state = jnp.einsum('bhncd,bhncv->bhndv', k * jnp.exp(b_last - b), v)
    chunk_decay = jnp.exp(b_last[..., 0, :])

    def step(s, inp):
        dec, st = inp
        return dec[..., None] * s + st, s

    bsz, h, _, dk = chunk_decay.shape
    s0 = jnp.zeros((bsz, h, dk, v.shape[-1]), jnp.float32)
    _, s_prev = lax.scan(step, s0, (jnp.moveaxis(chunk_decay, 2, 0), jnp.moveaxis(chunk_state, 2, 0)))
    s_prev = jnp.moveaxis(s_prev, 0, 2)
    return o_intra + jnp.einsum('bhncd,bhndv->bhncv', q_e, s_prev)


def _gated_delta_chunked(q, k, v, g, beta):
    c = q.shape[-2]
    gc = jnp.cumsum(g, axis=-1)
    incl = jnp.tril(jnp.ones((c, c), dtype=bool))
    strict = jnp.tril(jnp.ones((c, c), dtype=bool), k=-1)
    diff = gc[..., :, None] - gc[..., None, :]
    decay = jnp.where(incl, jnp.exp(jnp.where(incl, diff, 0.0)), 0.0)
    kk = jnp.einsum('bhncd,bhnsd->bhncs', k, k)
    a_strict = jnp.where(strict, kk * decay * beta[..., :, None], 0.0)
    t_mat = a_strict + jnp.eye(c, dtype=jnp.float32)
    rhs = jnp.concatenate([v * beta[..., None], k * (beta * jnp.exp(gc))[..., None]], axis=-1)
    sol = lax.linalg.triangular_solve(t_mat, rhs, left_side=True, lower=True, unit_diagonal=True)
    dv = v.shape[-1]
    u, w = sol[..., :dv], sol[..., dv:]
    qk = jnp.where(incl, jnp.einsum('bhncd,bhnsd->bhncs', q, k) * decay, 0.0)
    q_dec = q * jnp.exp(gc)[..., None]
    k_to_end = k * jnp.exp(gc[..., -1:] - gc)[..., None]
    chunk_decay = jnp.exp(gc[..., -1])

    def step(s, inp):
        u_c, w_c, q_c, qk_c, k_c, dec_c = inp
        v_new = u_c - jnp.einsum('bhcd,bhdv->bhcv', w_c, s)
        o = jnp.einsum('bhcd,bhdv->bhcv', q_c, s) + jnp.einsum('bhcs,bhsv->bhcv', qk_c, v_new)
        s = dec_c[..., None, None] * s + jnp.einsum('bhcd,bhcv->bhdv', k_c, v_new)
        return s, o

    bsz, h = q.shape[:2]
    s0 = jnp.zeros((bsz, h, k.shape[-1], dv), jnp.float32)
    xs = tuple(jnp.moveaxis(t, 2, 0) for t in (u, w, q_dec, qk, k_to_end, chunk_decay))
    _, o = lax.scan(step, s0, xs)
    return jnp.moveaxis(o, 0, 2)


def _centred_depthwise_conv(x, w):
    ch, kw = w.shape
    rhs = jnp.transpose(w, (1, 0))[:, None, :]
    return lax.conv_general_dilated(x, rhs.astype(x.dtype), window_strides=(1,),
                                    padding=[(kw // 2, kw // 2)],
                                    dimension_numbers=('NWC', 'WIO', 'NWC'),
                                    feature_group_count=ch)


def _layer(x, mem, norm_w, w_in, gla_w2, gla_b, gla_norm_w, gdn_conv_w, gdn_a_log, gdn_dt_bias,
           gdn_norm_w, mem_norm_w, xa_w_kv, xa_norm_w, w_out):
    bsz, tl, _ = x.shape
    h = _rmsnorm(x, norm_w)
    proj = jnp.einsum('btd,de->bte', h, w_in.astype(jnp.float32))
    (gla_q, gla_k, gla_v, gla_z, gla_lr, gdn_qkv, gdn_z, gdn_b, gdn_a, xa_q, xa_z) = _split_cols(proj, IN_SIZES)

    lr = gla_lr.reshape(bsz, tl, 2, GLA_RANK)
    gate_logits = jnp.einsum('btzr,zre->btze', lr, gla_w2.astype(jnp.float32)) + gla_b.astype(jnp.float32)
    log_alpha = (jax.nn.log_sigmoid(gate_logits) / GLA_GATE_NORMALIZER).reshape(bsz, tl, 2, GLA_HEADS, GLA_DK)
    q1 = gla_q.reshape(bsz, tl, GLA_HEADS, GLA_DK) * (GLA_DK ** -0.5)
    k1 = gla_k.reshape(bsz, tl, GLA_HEADS, GLA_DK)
    v1 = gla_v.reshape(bsz, tl, GLA_HEADS, GLA_DV)
    o1 = _run_bidir(_gla_chunked, (q1, k1, v1, log_alpha[:, :, 0]), (q1, k1, v1, log_alpha[:, :, 1]))
    o1 = _rmsnorm(o1, gla_norm_w) * jax.nn.silu(gla_z).reshape(bsz, tl, GLA_HEADS, GLA_DV)

    qkv = jax.nn.silu(_centred_depthwise_conv(gdn_qkv, gdn_conv_w.astype(jnp.float32)))
    q2, k2, v2 = _split_cols(qkv, (GDN_HEADS * GDN_DK, GDN_HEADS * GDN_DK, GDN_W))
    q2 = _l2norm(q2.reshape(bsz, tl, GDN_HEADS, GDN_DK)) * (GDN_DK ** -0.5)
    k2 = _l2norm(k2.reshape(bsz, tl, GDN_HEADS, GDN_DK))
    v2 = v2.reshape(bsz, tl, GDN_HEADS, GDN_DV)
    beta = jax.nn.sigmoid(gdn_b.reshape(bsz, tl, 2, GDN_HEADS))
    g = -jnp.exp(gdn_a_log.astype(jnp.float32)) * jax.nn.softplus(
        gdn_a.reshape(bsz, tl, 2, GDN_HEADS) + gdn_dt_bias.astype(jnp.float32))
    o2 = _run_bidir(_gated_delta_chunked,
                    (q2, k2, v2, g[:, :, 0], beta[:, :, 0]),
                    (q2, k2, v2, g[:, :, 1], beta[:, :, 1]))
    o2 = _rmsnorm(o2, gdn_norm_w) * jax.nn.silu(gdn_z).reshape(bsz, tl, GDN_HEADS, GDN_DV)

    m = _rmsnorm(mem, mem_norm_w)
    mkv = jnp.einsum('bmd,de->bme', m, xa_w_kv.astype(jnp.float32))
    mk = mkv[..., :XA_W].reshape(bsz, -1, XA_HEADS, XA_DH)
    mv = mkv[..., XA_W:].reshape(bsz, -1, XA_HEADS, XA_DH)
    q3 = xa_q.reshape(bsz, tl, XA_HEADS, XA_DH)
    scores = jnp.einsum('bthd,bmhd->bhtm', q3, mk) * (XA_DH ** -0.5)
    p = jax.nn.softmax(scores, axis=-1)
    o3 = jnp.einsum('bhtm,bmhd->bthd', p, mv)
    o3 = _rmsnorm(o3, xa_norm_w) * jax.nn.silu(xa_z).reshape(bsz, tl, XA_HEADS, XA_DH)

    o_cat = jnp.concatenate([o1.reshape(bsz, tl, GLA_W), o2.reshape(bsz, tl, GDN_W),
                             o3.reshape(bsz, tl, XA_W)], axis=-1)
    y = jnp.einsum('bte,ed->btd', o_cat, w_out.astype(jnp.float32))
    return x + y.astype(x.dtype)


def setup_inputs(seed: int = 0) -> dict:
    key = jax.random.key(seed)
    ks = jax.random.split(key, 20)
    f32 = jnp.float32
    x = jax.random.normal(ks[0], (BATCH, SEQ, D_MODEL), f32)
    mem = jax.random.normal(ks[1], (BATCH, MEM_LEN, D_MODEL), f32)
    norm_w = 1.0 + 0.02 * jax.random.normal(ks[2], (DEPTH, D_MODEL), f32)
    w_in = jax.random.normal(ks[3], (DEPTH, D_MODEL, IN_W), f32) * D_MODEL ** -0.5
    gla_w2 = jax.random.normal(ks[4], (DEPTH, 2, GLA_RANK, GLA_HEADS * GLA_DK), f32) * GLA_RANK ** -0.5
    gla_b = 0.1 * jax.random.normal(ks[5], (DEPTH, 2, GLA_HEADS * GLA_DK), f32)
    gla_norm_w = 1.0 + 0.02 * jax.random.normal(ks[6], (DEPTH, GLA_DV), f32)
    gdn_conv_w = jax.random.normal(ks[7], (DEPTH, GDN_QKV_W, GDN_CONV), f32) * GDN_CONV ** -0.5
    gdn_a_log = jnp.log(jax.random.uniform(ks[8], (DEPTH, 2, GDN_HEADS), f32, 1.0, 16.0))
    dt = jnp.exp(jax.random.uniform(ks[9], (DEPTH, 2, GDN_HEADS), f32,
                                    math.log(1e-3), math.log(1e-1)))
    gdn_dt_bias = dt + jnp.log(-jnp.expm1(-dt))
    gdn_norm_w = 1.0 + 0.02 * jax.random.normal(ks[10], (DEPTH, GDN_DV), f32)
    mem_norm_w = 1.0 + 0.02 * jax.random.normal(ks[11], (DEPTH, D_MODEL), f32)
    xa_w_kv = jax.random.normal(ks[12], (DEPTH, D_MODEL, 2 * XA_W), f32) * D_MODEL ** -0.5
    xa_norm_w = 1.0 + 0.02 * jax.random.normal(ks[13], (DEPTH, XA_DH), f32)
    w_out = jax.random.normal(ks[14], (DEPTH, MIX_W, D_MODEL), f32) * MIX_W ** -0.5
    final_norm_w = 1.0 + 0.02 * jax.random.normal(ks[15], (D_MODEL,), f32)
    return {"x": x, "mem": mem, "norm_w": norm_w, "w_in": w_in, "gla_w2": gla_w2, "gla_b": gla_b,
            "gla_norm_w": gla_norm_w, "gdn_conv_w": gdn_conv_w, "gdn_a_log": gdn_a_log,
            "gdn_dt_bias": gdn_dt_bias, "gdn_norm_w": gdn_norm_w, "mem_norm_w": mem_norm_w,
            "xa_w_kv": xa_w_kv, "xa_norm_w": xa_norm_w, "w_out": w_out, "final_norm_w": final_norm_w}


def reference(x, mem, norm_w, w_in, gla_w2, gla_b, gla_norm_w, gdn_conv_w, gdn_a_log, gdn_dt_bias,
              gdn_norm_w, mem_norm_w, xa_w_kv, xa_norm_w, w_out, final_norm_w):
    h = x
    for l in range(DEPTH):
        h = _layer(h, mem, norm_w[l], w_in[l], gla_w2[l], gla_b[l], gla_norm_w[l], gdn_conv_w[l],
                   gdn_a_log[l], gdn_dt_bias[l], gdn_norm_w[l], mem_norm_w[l], xa_w_kv[l],
                   xa_norm_w[l], w_out[l])
    return _rmsnorm(h, final_norm_w).astype(x.dtype)
```

```python
import contextlib
import numpy as np
import ml_dtypes
import concourse.bass as bass
import concourse.mybir as mybir
from concourse.bass_utils import run_bass_kernel_spmd

F32 = mybir.dt.float32
BF16 = mybir.dt.bfloat16
AF = mybir.ActivationFunctionType
ALU = mybir.AluOpType
AX = mybir.AxisListType

T = 4096
D = 1024
INW = 3376
NL = 2
MEM = 256
EPS = 1e-6
NCH = T // 64
import os as _osm
NDT = BF16 if _osm.environ.get('K_NDT', 'bf16') == 'bf16' else F32

EPOCH = 30000
N_DMA_SEMS = 32


class Sched:
    ENGS = ("pe", "act", "dve", "pool", "sp")

    def __init__(self, nc, stack):
        self.nc = nc
        self.stack = stack
        self.ops = {e: [] for e in self.ENGS}
        self.cnt = {e: 0 for e in self.ENGS}
        self.epoch = {e: 0 for e in self.ENGS}
        self.sems = {}
        for e in self.ENGS:
            self._new_eng_sem(e)
        self.dma_sems = []
        for i in range(2 * N_DMA_SEMS):
            s = stack.enter_context(nc.semaphore(f"dma{i}"))
            self.dma_sems.append([s, 0])
        self.dma_rr = {"sp": 0, "pool": 0}
        self.known = {e: {} for e in self.ENGS}
        self.last_w = {}
        self.readers = {}
        self.n_waits = 0
        self.n_ops = 0

    def _new_eng_sem(self, e):
        s = self.stack.enter_context(self.nc.semaphore(f"s_{e}_{self.epoch[e]}"))
        self.sems[(e, self.epoch[e])] = s

    def _deps(self, reads, writes):
        deps = []
        for b in reads:
            if b in self.last_w:
                deps.append(self.last_w[b])
        for b in writes:
            if b in self.last_w:
                deps.append(self.last_w[b])
            deps.extend(self.readers.get(b, {}).items())
        return deps

    def _emit_waits(self, eng, deps):
        need = {}
        for (sk, val) in deps:
            if sk[0] == eng and eng == "pe":
                continue
            if self.known[eng].get(sk, 0) >= val:
                continue
            if need.get(sk, 0) < val:
                need[sk] = val
        waits = []
        for sk, val in need.items():
            self.known[eng][sk] = val
            sem = self.sems[sk] if sk[0] in self.ENGS else self.dma_sems[sk[1]][0]
            waits.append((sem, val))
        return waits

    def _record(self, ev, reads, writes):
        for b in writes:
            self.last_w[b] = ev
            self.readers[b] = {}
        for b in reads:
            if b not in writes:
                d = self.readers.setdefault(b, {})
                if d.get(ev[0], 0) < ev[1]:
                    d[ev[0]] = ev[1]

    def op(self, eng, fn, reads=(), writes=()):
        waits = self._emit_waits(eng, self._deps(reads, writes))
        if self.cnt[eng] >= EPOCH:
            self.epoch[eng] += 1
            self.cnt[eng] = 0
            self._new_eng_sem(eng)
        self.cnt[eng] += 1
        sk = (eng, self.epoch[eng])
        sem = self.sems[sk]
        self.n_waits += len(waits)
        self.n_ops += 1

        def run(e, waits=waits, fn=fn, sem=sem):
            for (s, v) in waits:
                e.wait_ge(s, v)
            fn(e).then_inc(sem, 1)

        self.ops[eng].append(run)
        ev = (sk, self.cnt[eng])
        self._record(ev, reads, writes)
        return ev

    def dma(self, eng, fn, reads=(), writes=()):
        i = self.dma_rr[eng] + (0 if eng == "sp" else N_DMA_SEMS)
        self.dma_rr[eng] = (self.dma_rr[eng] + 1) % N_DMA_SEMS
        sem, cur = self.dma_sems[i]
        sk = ("dma", i)
        deps = self._deps(reads, writes)
        if cur > 0:
            deps.append((sk, cur))
        waits = self._emit_waits(eng, deps)
        val = cur + 16
        self.dma_sems[i][1] = val
        self.n_waits += len(waits)
        self.n_ops += 1

        def run(e, waits=waits, fn=fn, sem=sem):
            for (s, v) in waits:
                e.wait_ge(s, v)
            fn(e).then_inc(sem, 16)

        self.ops[eng].append(run)
        ev = (sk, val)
        self._record(ev, reads, writes)
        return ev

    def barrier(self):
        evs = [((e, self.epoch[e]), self.cnt[e]) for e in self.ENGS if self.cnt[e] > 0]
        evs += [(("dma", i), v) for i, (s, v) in enumerate(self.dma_sems) if v > 0]
        for eng in self.ENGS:
            waits = self._emit_waits(eng, evs)

            def run(e, waits=waits):
                for (s, v) in waits:
                    e.wait_ge(s, v)

            self.ops[eng].append(run)

    def wait_all(self, eng, bufs):
        deps = [self.last_w[b] for b in bufs if b in self.last_w]
        waits = self._emit_waits(eng, deps)

        def run(e, waits=waits):
            for (s, v) in waits:
                e.wait_ge(s, v)

        self.ops[eng].append(run)

    def finish(self):
        with self.nc.Block() as block:
            @block.tensor
            def _(e):
                for f in self.ops["pe"]:
                    f(e)

            @block.scalar
            def _(e):
                for f in self.ops["act"]:
                    f(e)

            @block.vector
            def _(e):
                for f in self.ops["dve"]:
                    f(e)

            @block.gpsimd
            def _(e):
                for f in self.ops["pool"]:
                    f(e)

            @block.sync
            def _(e):
                for f in self.ops["sp"]:
                    f(e)


def build(nlayers=NL, debug=False, passes=(0, 1, 2, 3, 4), parts=("gla", "gdn")):
    nc = bass.Bass("TRN2", target_bir_lowering=False)
    dram = lambda n, s, dt, k="ExternalInput": nc.dram_tensor(n, s, dt, kind=k).ap()
    SCR = "ExternalOutput" if debug else "Internal"
    x_in = dram("x", [T, D], F32)
    mem_in = dram("mem", [MEM, D], F32)
    w_in = dram("w_in", [NL, D, INW], F32)
    w_out = dram("w_out", [NL, D, D], F32)
    w_kv = dram("xa_w_kv", [NL, D, 512], F32)
    normw_pc = dram("normw_pc", [NL, 128, 8], F32)
    memnormw_pc = dram("memnormw_pc", [NL, 128, 8], F32)
    w2_in = dram("w2", [NL, 16, 2, 256], F32)
    bias_bc = dram("bias_bc", [NL, 128, 512], F32)
    ohnorm_bc = dram("ohnorm_bc", [NL, 128, 768], F32)
    xanorm_bc = dram("xanorm_bc", [NL, 128, 256], F32)
    convw_in = dram("convw", [NL, 96, 12, 5], F32)
    alog_bc = dram("alog_bc", [NL, 128, 8], F32)
    dtb_bc = dram("dtb_bc", [NL, 128, 8], F32)
    fnorm_bc = dram("fnorm_bc", [128, D], F32)
    ident_bf_in = dram("ident_bf", [128, 128], BF16)
    ident_f_in = dram("ident_f", [64, 64], F32)
    masks_in = dram("masks", [64, 5, 4, 64], F32)
    ones_f_in = dram("ones_f", [64, 96], F32)
    ones_bf_in = dram("ones_bf", [96, 96], BF16)
    out = dram("out", [T, D], F32, "ExternalOutput")
    KV = dram("s_kv", [T, 640], BF16, SCR)
    GATE = dram("s_gate", [T, 1024], BF16, SCR)
    GB = dram("s_gb", [T, 16], F32, SCR)
    LA = dram("s_la", [T, 512], F32, SCR)
    QKT = dram("s_qkt", [512, T], BF16, SCR)
    RAWT = dram("s_rawt", [12, 96, T], BF16, SCR)
    C2T = dram("s_c2t", [12, 96, T], BF16, SCR)
    OC3 = dram("s_oc3", [T, 256], BF16, SCR)
    OFB = [dram("s_of", [T, 768], F32, SCR), dram("s_ob", [T, 768], F32, SCR)]
    X1 = dram("s_x1", [T, D], F32, SCR)

    with contextlib.ExitStack() as st:
        S = Sched(nc, st)
        sb = lambda n, s, dt: st.enter_context(nc.sbuf_tensor("sb_" + n, s, dt))
        ident_bf = sb("ident_bf", [128, 128], BF16)
        ident_f = sb("ident_f", [64, 64], F32)
        masks = sb("masks", [64, 5, 4, 64], F32)
        masks5 = masks
        ones_f = sb("ones_f", [64, 96], F32)
        ones_bf = sb("ones_bf", [96, 96], BF16)
        fnorm = sb("fnorm", [128, D], F32)
        for (t_, d_, k_) in ((ident_bf, ident_bf_in, "c_idbf"), (ident_f, ident_f_in, "c_idf"),
                             (masks, masks_in, "c_masks"), (ones_f, ones_f_in, "c_onesf"),
                             (ones_bf, ones_bf_in, "c_onesbf"), (fnorm, fnorm_bc, "c_fnorm")):
            S.dma("sp", lambda e, t_=t_, d_=d_: e.dma_start(out=t_[:], in_=d_), writes=[k_])
        LE, GE, GT, LT = 0, 1, 2, 3

        w_out_sb = sb("w_out_sb", [128, 8, D], BF16)
        normw = sb("normw", [128, 8], F32)
        memnormw = sb("memnormw", [128, 8], F32)
        w2 = sb("w2sb", [16, 2, 256], F32)
        biasb = sb("biasb", [128, 512], F32)
        ohnorm = sb("ohnorm", [128, 768], F32)
        xanorm = sb("xanorm", [128, 256], F32)
        convw = sb("convw", [96, 12, 5], F32)
        nega = sb("nega", [128, 8], F32)
        dtb = sb("dtb", [128, 8], F32)
        mkT = sb("mkT", [64, 4, MEM], BF16)
        mv = sb("mv", [128, 2, 256], BF16)

        PB = [st.enter_context(nc.psum_tensor(f"pb{i}", [128, 512], F32)) for i in range(8)]

        def rms_rstd(eng_out, src_ap, n, key_src, key_out):
            S.op("act", lambda e: e.activation(out=eng_out, in_=src_ap, func=AF.Ln, scale=1.0 / n, bias=EPS),
                 reads=[key_src], writes=[key_out])
            S.op("act", lambda e: e.activation(out=eng_out, in_=eng_out, func=AF.Exp, scale=-0.5),
                 reads=[key_out], writes=[key_out])

        marks = []
        def mark(name):
            marks.append((name, len(S.ops['pe']), len(S.ops['act']), len(S.ops['dve']), len(S.ops['pool'])))
        def emit_layer(l):
            mark(f'L{l} start')
            xsrc = x_in if l == 0 else X1
            last = (l == nlayers - 1)
            if 0 in passes or 1 in passes:
                with contextlib.ExitStack() as ps:
                    sbp = lambda n, s, dt: ps.enter_context(nc.sbuf_tensor(f"sbA{l}_" + n, s, dt))
                    w_in_sb = sbp("w_in_sb", [128, 8, INW], BF16)
                    stage = [sbp(f"stage{i}", [128, 1688], F32) for i in range(2)]
                    xt = [sbp(f"xt{i}", [128, D], F32) for i in range(2)]
                    junk = sbp("junk", [128, D], BF16)
                    xs = [sbp(f"xs{i}", [128, D], BF16) for i in range(2)]
                    ss = sbp("ss", [128, 1], F32)
                    rstd = sbp("rstd", [128, 1], F32)
                    hTs = [sbp(f"hT{i}", [128, 8, 512], BF16) for i in range(2)]
                    hT = hTs[0]
                    kvo = [sbp(f"kvo{i}", [128, 640], BF16) for i in range(2)]
                    gate_o = [sbp(f"gateo{i}", [128, 1024], BF16) for i in range(2)]
                    gb_o = [sbp(f"gbo{i}", [128, 16], F32) for i in range(2)]
                    gtmp = sbp("gtmp", [128, 8], F32)
                    la_o = [sbp(f"lao{i}", [128, 512], F32) for i in range(2)]
                    fm_o = [sbp(f"fmo{i}", [128, 512], BF16) for i in range(2)]
                    lrT = sbp("lrT", [16, 2, 512], F32)
                    raw_o = sbp("raw_o", [96, 12, 512], BF16)
                    q3T = sbp("q3T", [64, 4, 512], BF16)
                    mx = [sbp(f"mx{i}", [128, 4], F32) for i in range(2)]
                    nb = [sbp(f"nb{i}", [128, 4], F32) for i in range(2)]
                    rs = [sbp(f"rs{i}", [128, 4], F32) for i in range(2)]
                    pexp = [sbp(f"pexp{i}", [128, 4, MEM], BF16) for i in range(2)]
                    pT = [sbp(f"pT{i}", [128, 8, 128], BF16) for i in range(2)]
                    o3 = [sbp(f"o3{i}", [128, 4, 64], F32) for i in range(2)]
                    o3sq = [sbp(f"o3sq{i}", [128, 4, 64], F32) for i in range(2)]
                    o3ss = [sbp(f"o3ss{i}", [128, 4], F32) for i in range(2)]
                    oc3_o = [sbp(f"oc3o{i}", [128, 256], BF16) for i in range(2)]
                    def token_tile_to_hT(src_dram, row0, slot, dst, dst_key, col0):
                        xk = ("xt", slot)
                        S.dma("sp", lambda e: e.dma_start(out=xt[slot][:], in_=src_dram[row0:row0 + 128, :]), writes=[xk])
                        S.op("act", lambda e: e.activation(out=junk[:], in_=xt[slot][:], func=AF.Square, accum_out=ss[:]),
                             reads=[xk], writes=["junk", "ss"])
                        rms_rstd(rstd[:], ss[:], D, "ss", "rstd")
                        S.op("dve", lambda e: e.tensor_scalar(out=xs[slot][:], in0=xt[slot][:], scalar1=rstd[:], scalar2=None,
                                                              op0=ALU.mult), reads=[xk, "rstd"], writes=[("xs", slot)])
                        ptr = PB[0][:, 0:512].bitcast(BF16).rearrange("p (c t) -> p c t", c=8)
                        for c in range(8):
                            S.op("pe", lambda e, c=c: e.transpose(out=ptr[:, c, :], in_=xs[slot][:, c * 128:(c + 1) * 128],
                                                                  identity=ident_bf[:]),
                                 reads=[("xs", slot), "c_idbf"], writes=["pb0"])
                        S.op("dve", lambda e: e.tensor_copy(out=dst[:, :, col0:col0 + 128], in_=ptr), reads=["pb0"],
                             writes=[dst_key])

                    if 0 in passes:
                        for (t_, d_, k_) in ((normw, normw_pc[l], "normw"), (memnormw, memnormw_pc[l], "memnormw"),
                                             (w2, w2_in[l], "w2"), (biasb, bias_bc[l], "biasb"),
                                             (ohnorm, ohnorm_bc[l], "ohnorm"), (xanorm, xanorm_bc[l], "xanorm"),
                                             (convw, convw_in[l], "convw"), (nega, alog_bc[l], "nega"),
                                             (dtb, dtb_bc[l], "dtb")):
                            S.dma("sp", lambda e, t_=t_, d_=d_: e.dma_start(out=t_[:], in_=d_), writes=[k_])
                        S.op("act", lambda e: e.activation(out=nega[:], in_=nega[:], func=AF.Exp), reads=["nega"], writes=["nega"])
                        S.op("dve", lambda e: e.tensor_scalar(out=nega[:], in0=nega[:], scalar1=-1.0, scalar2=None, op0=ALU.mult),
                             reads=["nega"], writes=["nega"])
                        for cc in range(16):
                            c, hf = divmod(cc, 2)
                            sl = cc % 2
                            S.dma("sp", lambda e, c=c, sl=sl, hf=hf: e.dma_start(out=stage[sl][:], in_=w_in[l, c * 128:(c + 1) * 128, hf * 1688:(hf + 1) * 1688]),
                                  writes=[("stage", sl)])
                            S.op("dve" if cc % 2 == 0 else "pool",
                                 lambda e, c=c, sl=sl, hf=hf: e.tensor_scalar(out=w_in_sb[:, c, hf * 1688:(hf + 1) * 1688], in0=stage[sl][:], scalar1=normw[:, c:c + 1],
                                                                       scalar2=None, op0=ALU.mult),
                                 reads=[("stage", sl), "normw"], writes=["w_in_sb"])
                        for c in range(8):
                            sl = c % 2
                            S.dma("sp", lambda e, c=c, sl=sl: e.dma_start(out=stage[sl][:, 0:D], in_=w_out[l, c * 128:(c + 1) * 128, :]),
                                  writes=[("stage", sl)])
                            S.op("act", lambda e, c=c, sl=sl: e.copy(out=w_out_sb[:, c, :], in_=stage[sl][:, 0:D]),
                                 reads=[("stage", sl)], writes=["w_out_sb"])
                        wkv = hT
                        for c in range(8):
                            sl = c % 2
                            S.dma("sp", lambda e, c=c, sl=sl: e.dma_start(out=stage[sl][:, 0:512], in_=w_kv[l, c * 128:(c + 1) * 128, :]),
                                  writes=[("stage", sl)])
                            S.op("dve", lambda e, c=c, sl=sl: e.tensor_scalar(out=wkv[:, c, :], in0=stage[sl][:, 0:512],
                                                                              scalar1=memnormw[:, c:c + 1], scalar2=None, op0=ALU.mult),
                                 reads=[("stage", sl), "memnormw"], writes=[("hT", 0)])
                        mT = raw_o[:, 0:4, :]
                        memT = sbp("memT", [128, 8, MEM], BF16)
                        for mt_ in range(2):
                            token_tile_to_hT(mem_in, mt_ * 128, mt_, memT, "memT", mt_ * 128)
                        for h in range(4):
                            pm = PB[1][0:64, 0:MEM]
                            for c in range(8):
                                S.op("pe", lambda e, c=c, h=h, pm=pm: e.matmul(pm, lhsT=wkv[:, c, h * 64:(h + 1) * 64], rhs=memT[:, c, :],
                                                                               start=(c == 0), stop=(c == 7)),
                                     reads=[("hT", 0), "memT"], writes=["pb1"])
                            S.op("act", lambda e, h=h, pm=pm: e.copy(out=mkT[:, h, :], in_=pm), reads=["pb1"], writes=["mkT"])
                        for mt_ in range(2):
                            pm = PB[1][:, 0:256]
                            for c in range(8):
                                S.op("pe", lambda e, c=c, mt_=mt_, pm=pm: e.matmul(pm, lhsT=memT[:, c, mt_ * 128:(mt_ + 1) * 128],
                                                                                   rhs=wkv[:, c, 256:512], start=(c == 0), stop=(c == 7)),
                                     reads=[("hT", 0), "memT"], writes=["pb1"])
                            S.op("act", lambda e, mt_=mt_, pm=pm: e.copy(out=mv[:, mt_, :], in_=pm), reads=["pb1"], writes=["mv"])

                    mark(f'L{l} p0 done')
                    if 1 in passes:
                        def do_mt(mt, hT, hk):
                            for sub in range(4):
                                tt = mt * 4 + sub
                                r0 = tt * 128
                                sl = tt % 2
                                if tt + 1 < T // 128:
                                    prep_tile(tt + 1)
                                hsl = lambda c: hT[:, c, sub * 128:(sub + 1) * 128]
                                groups = ((1, 256, 512), (2, 768, 512), (3, 2464, 400), (4, 3120, 256))
                                for (bk, c0, wd) in groups:
                                    for c in range(8):
                                        S.op("pe", lambda e, c=c, bk=bk, c0=c0, wd=wd, sub=sub: e.matmul(
                                            PB[bk][:, 0:wd], lhsT=hT[:, c, sub * 128:(sub + 1) * 128], rhs=w_in_sb[:, c, c0:c0 + wd],
                                            start=(c == 0), stop=(c == 7)), reads=[hk, "w_in_sb"], writes=[f"pb{bk}"])
                                S.op("act", lambda e, sl=sl: e.copy(out=kvo[sl][:, 0:512], in_=PB[1][:, 0:512]), reads=["pb1"],
                                     writes=[("kvo", sl)])
                                S.op("dve", lambda e, sl=sl: e.tensor_copy(out=kvo[sl][:, 512:640], in_=PB[2][:, 0:128]), reads=["pb2"],
                                     writes=[("kvo", sl)])
                                S.dma("pool", lambda e, sl=sl, r0=r0: e.dma_start(out=KV[r0:r0 + 128, :], in_=kvo[sl][:]),
                                      reads=[("kvo", sl)], writes=["KV"])
                                S.op("act", lambda e, sl=sl: e.activation(out=gate_o[sl][:, 0:384], in_=PB[2][:, 128:512], func=AF.Silu),
                                     reads=["pb2"], writes=[("gateo", sl)])
                                S.op("act", lambda e, sl=sl: e.activation(out=gate_o[sl][:, 384:768], in_=PB[3][:, 0:384], func=AF.Silu),
                                     reads=["pb3"], writes=[("gateo", sl)])
                                S.op("act", lambda e, sl=sl: e.activation(out=gate_o[sl][:, 768:1024], in_=PB[4][:, 0:256], func=AF.Silu),
                                     reads=["pb4"], writes=[("gateo", sl)])
                                S.dma("pool", lambda e, sl=sl, r0=r0: e.dma_start(out=GATE[r0:r0 + 128, :], in_=gate_o[sl][:]),
                                      reads=[("gateo", sl)], writes=["GATE"])
                                S.op("act", lambda e, sl=sl: e.activation(out=gb_o[sl][:, 0:8], in_=PB[3][:, 384:392], func=AF.Exp, scale=-1.0),
                                     reads=["pb3"], writes=[("gbo", sl)])
                                S.op("dve", lambda e, sl=sl: e.tensor_scalar(out=gb_o[sl][:, 0:8], in0=gb_o[sl][:, 0:8], scalar1=1.0, scalar2=None, op0=ALU.add),
                                     reads=[("gbo", sl)], writes=[("gbo", sl)])
                                S.op("dve", lambda e, sl=sl: e.reciprocal(out=gb_o[sl][:, 0:8], in_=gb_o[sl][:, 0:8]), reads=[("gbo", sl)], writes=[("gbo", sl)])
                                S.op("dve", lambda e: e.tensor_tensor(out=gtmp[:], in0=PB[3][:, 392:400], in1=dtb[:], op=ALU.add),
                                     reads=["pb3", "dtb"], writes=["gtmp"])
                                S.op("act", lambda e: e.activation(out=gtmp[:], in_=gtmp[:], func=AF.Exp), reads=["gtmp"], writes=["gtmp"])
                                S.op("act", lambda e: e.activation(out=gtmp[:], in_=gtmp[:], func=AF.Ln, bias=1.0), reads=["gtmp"],
                                     writes=["gtmp"])
                                S.op("dve", lambda e, sl=sl: e.tensor_tensor(out=gb_o[sl][:, 8:16], in0=gtmp[:], in1=nega[:], op=ALU.mult),
                                     reads=["gtmp", "nega"], writes=[("gbo", sl)])
                                S.dma("pool", lambda e, sl=sl, r0=r0: e.dma_start(out=GB[r0:r0 + 128, :], in_=gb_o[sl][:]),
                                      reads=[("gbo", sl)], writes=["GB"])
                            t0 = mt * 512
                            fmi = 0
                            for m in range(4):
                                bk = 5 + (m % 2)
                                for c in range(8):
                                    S.op("pe", lambda e, c=c, m=m, bk=bk: e.matmul(PB[bk][:, :], lhsT=w_in_sb[:, c, m * 128:(m + 1) * 128],
                                                                                   rhs=hT[:, c, :], start=(c == 0), stop=(c == 7)),
                                         reads=[hk, "w_in_sb"], writes=[f"pb{bk}"])
                                fs = fmi % 2
                                fmi += 1
                                S.op("act", lambda e, bk=bk, fs=fs, m=m: e.activation(out=fm_o[fs][:], in_=PB[bk][:, :], func=AF.Copy,
                                                                                      scale=(0.125 if m < 2 else 1.0)),
                                     reads=[f"pb{bk}"], writes=[("fmo", fs)])
                                S.dma("pool", lambda e, fs=fs, m=m, t0=t0: e.dma_start(out=QKT[m * 128:(m + 1) * 128, t0:t0 + 512],
                                                                                       in_=fm_o[fs][:]),
                                      reads=[("fmo", fs)], writes=["QKT"])
                            for z in range(2):
                                for c in range(8):
                                    S.op("pe", lambda e, c=c, z=z: e.matmul(PB[7][0:16, :], lhsT=w_in_sb[:, c, 1280 + z * 16:1296 + z * 16],
                                                                            rhs=hT[:, c, :], start=(c == 0), stop=(c == 7)),
                                         reads=[hk, "w_in_sb"], writes=["pb7"])
                                S.op("act", lambda e, z=z: e.copy(out=lrT[:, z, :], in_=PB[7][0:16, :]), reads=["pb7"], writes=["lrT"])
                            def gate_logits(sub):
                                tt = mt * 4 + sub
                                r0 = tt * 128
                                sl = tt % 2
                                for z in range(2):
                                    S.op("pe", lambda e, z=z, sub=sub: e.matmul(PB[7][:, z * 256:(z + 1) * 256],
                                                                                lhsT=lrT[:, z, sub * 128:(sub + 1) * 128], rhs=w2[:, z, :],
                                                                                start=True, stop=True),
                                         reads=["lrT", "w2"], writes=["pb7"])
                                S.op("dve", lambda e, sl=sl: e.tensor_tensor(out=la_o[sl][:], in0=PB[7][:, :], in1=biasb[:], op=ALU.add),
                                     reads=["pb7", "biasb"], writes=[("lao", sl)])
                                S.op("act", lambda e, sl=sl: e.activation(out=la_o[sl][:], in_=la_o[sl][:], func=AF.Exp, scale=-1.0),
                                     reads=[("lao", sl)], writes=[("lao", sl)])
                                S.op("act", lambda e, sl=sl: e.activation(out=la_o[sl][:], in_=la_o[sl][:], func=AF.Ln, bias=1.0),
                                     reads=[("lao", sl)], writes=[("lao", sl)])
                                S.op("dve", lambda e, sl=sl: e.tensor_scalar(out=la_o[sl][:], in0=la_o[sl][:], scalar1=-1.0 / 16.0,
                                                                             scalar2=None, op0=ALU.mult),
                                     reads=[("lao", sl)], writes=[("lao", sl)])
                                S.dma("pool", lambda e, sl=sl, r0=r0: e.dma_start(out=LA[r0:r0 + 128, :], in_=la_o[sl][:]),
                                      reads=[("lao", sl)], writes=["LA"])
                            for jj in range(12):
                                bk = 5 + (jj % 2)
                                for c in range(8):
                                    S.op("pe", lambda e, c=c, jj=jj, bk=bk: e.matmul(PB[bk][0:96, :],
                                                                                     lhsT=w_in_sb[:, c, 1312 + jj * 96:1312 + (jj + 1) * 96],
                                                                                     rhs=hT[:, c, :], start=(c == 0), stop=(c == 7)),
                                         reads=[hk, "w_in_sb"], writes=[f"pb{bk}"])
                                S.op("act" if jj % 2 == 0 else "dve",
                                     (lambda e, jj=jj, bk=bk: e.copy(out=raw_o[:, jj, :], in_=PB[bk][0:96, :])) if jj % 2 == 0 else
                                     (lambda e, jj=jj, bk=bk: e.tensor_copy(out=raw_o[:, jj, :], in_=PB[bk][0:96, :])),
                                     reads=[f"pb{bk}"], writes=["raw_o"])
                                if jj % 3 == 2:
                                    gate_logits(jj // 3)
                            S.dma("pool", lambda e, t0=t0: e.dma_start(out=RAWT[:, :, t0:t0 + 512].rearrange("j p c -> p j c"), in_=raw_o[:]),
                                  reads=["raw_o"], writes=["RAWT"])
                            for h in range(4):
                                bk = 5 + (h % 2)
                                for c in range(8):
                                    S.op("pe", lambda e, c=c, h=h, bk=bk: e.matmul(PB[bk][0:64, :],
                                                                                   lhsT=w_in_sb[:, c, 2864 + h * 64:2864 + (h + 1) * 64],
                                                                                   rhs=hT[:, c, :], start=(c == 0), stop=(c == 7)),
                                         reads=[hk, "w_in_sb"], writes=[f"pb{bk}"])
                                S.op("act", lambda e, h=h, bk=bk: e.copy(out=q3T[:, h, :], in_=PB[bk][0:64, :]), reads=[f"pb{bk}"],
                                     writes=["q3T"])
                            def xa_sub(sub):
                                tt = mt * 4 + sub
                                r0 = tt * 128
                                sl = tt % 2
                                xb = 5 if sl == 0 else 1
                                for h in range(4):
                                    bk = xb + (h // 2)
                                    S.op("pe", lambda e, h=h, bk=bk, sub=sub: e.matmul(PB[bk][:, (h % 2) * 256:(h % 2 + 1) * 256],
                                                                                       lhsT=q3T[:, h, sub * 128:(sub + 1) * 128], rhs=mkT[:, h, :],
                                                                                       start=True, stop=True),
                                         reads=["q3T", "mkT"], writes=[f"pb{bk}"])
                                yield
                                for hp in range(2):
                                    S.op("dve", lambda e, hp=hp: e.tensor_reduce(out=mx[sl][:, hp * 2:hp * 2 + 2],
                                                                                 in_=PB[xb + hp][:, :].rearrange("p (h m) -> p h m", h=2),
                                                                                 axis=AX.X, op=ALU.max),
                                         reads=[f"pb{xb + hp}"], writes=[("mx", sl)])
                                S.op("dve", lambda e: e.tensor_scalar(out=nb[sl][:], in0=mx[sl][:], scalar1=-0.125, scalar2=None, op0=ALU.mult),
                                     reads=[("mx", sl)], writes=[("nb", sl)])
                                for h in range(4):
                                    bk = xb + (h // 2)
                                    S.op("act", lambda e, h=h, bk=bk: e.activation(out=pexp[sl][:, h, :], in_=PB[bk][:, (h % 2) * 256:(h % 2 + 1) * 256],
                                                                                   func=AF.Exp, scale=0.125, bias=nb[sl][:, h:h + 1],
                                                                                   accum_out=rs[sl][:, h:h + 1]),
                                         reads=[f"pb{bk}", ("nb", sl)], writes=[("pexp", sl), ("rs", sl)])
                                yield
                                ptp = PB[xb + 2][:, :].bitcast(BF16).rearrange("p (c t) -> p c t", c=8)
                                for h in range(4):
                                    for mb in range(2):
                                        S.op("pe", lambda e, h=h, mb=mb: e.transpose(out=ptp[:, h * 2 + mb, :],
                                                                                     in_=pexp[sl][:, h, mb * 128:(mb + 1) * 128], identity=ident_bf[:]),
                                             reads=[("pexp", sl), "c_idbf"], writes=[f"pb{xb + 2}"])
                                S.op("dve", lambda e: e.tensor_copy(out=pT[sl][:], in_=ptp), reads=[f"pb{xb + 2}"], writes=[("pT", sl)])
                                yield
                                po = PB[xb][:, 0:256].rearrange("p (h d) -> p h d", h=4)
                                for h in range(4):
                                    for mb in range(2):
                                        S.op("pe", lambda e, h=h, mb=mb: e.matmul(po[:, h, :], lhsT=pT[sl][:, h * 2 + mb, :],
                                                                                  rhs=mv[:, mb, h * 64:(h + 1) * 64], start=(mb == 0), stop=(mb == 1)),
                                             reads=[("pT", sl), "mv"], writes=[f"pb{xb}"])
                                yield
                                S.op("dve", lambda e: e.reciprocal(out=rs[sl][:], in_=rs[sl][:]), reads=[("rs", sl)], writes=[("rs", sl)])
                                S.op("dve", lambda e: e.tensor_tensor(out=o3[sl][:], in0=po, in1=rs[sl][:].unsqueeze(2).broadcast_to([128, 4, 64]),
                                                                      op=ALU.mult), reads=[f"pb{xb}", ("rs", sl)], writes=[("o3", sl)])
                                S.op("pool", lambda e: e.tensor_tensor(out=o3sq[sl][:], in0=o3[sl][:], in1=o3[sl][:], op=ALU.mult), reads=[("o3", sl)],
                                     writes=[("o3sq", sl)])
                                S.op("dve", lambda e: e.tensor_reduce(out=o3ss[sl][:], in_=o3sq[sl][:], axis=AX.X, op=ALU.add), reads=[("o3sq", sl)],
                                     writes=[("o3ss", sl)])
                                yield
                                rms_rstd(o3ss[sl][:], o3ss[sl][:], 64, ("o3ss", sl), ("o3ss", sl))
                                S.op("dve", lambda e: e.tensor_tensor(out=o3[sl][:], in0=o3[sl][:], in1=o3ss[sl][:].unsqueeze(2).broadcast_to([128, 4, 64]),
                                                                      op=ALU.mult), reads=[("o3", sl), ("o3ss", sl)], writes=[("o3", sl)])
                                S.op("pool", lambda e: e.tensor_tensor(out=o3[sl][:], in0=o3[sl][:], in1=xanorm[:].rearrange("p (h d) -> p h d", h=4),
                                                                       op=ALU.mult), reads=[("o3", sl), "xanorm"], writes=[("o3", sl)])
                                S.dma("sp", lambda e, sl=sl, r0=r0: e.dma_start(out=oc3_o[sl][:], in_=GATE[r0:r0 + 128, 768:1024]),
                                      reads=["GATE"], writes=[("oc3o", sl)])
                                S.op("dve", lambda e, sl=sl: e.tensor_tensor(out=oc3_o[sl][:], in0=o3[sl][:].rearrange("p h d -> p (h d)"),
                                                                             in1=oc3_o[sl][:], op=ALU.mult),
                                     reads=[("o3", sl), ("oc3o", sl)], writes=[("oc3o", sl)])
                                S.dma("pool", lambda e, sl=sl, r0=r0: e.dma_start(out=OC3[r0:r0 + 128, :], in_=oc3_o[sl][:]),
                                      reads=[("oc3o", sl)], writes=["OC3"])
                            xa_act = []
                            xa_n = 0
                            while True:
                                while len(xa_act) < 2 and xa_n < 4:
                                    xa_act.append(xa_sub(xa_n))
                                    xa_n += 1
                                if not xa_act:
                                    break
                                for g_ in list(xa_act):
                                    if next(g_, "done") == "done":
                                        xa_act.remove(g_)

                        def prep_tile(tt_):
                            mt__, sub__ = divmod(tt_, 4)
                            token_tile_to_hT(xsrc, tt_ * 128, tt_ % 2, hTs[mt__ % 2], ("hT", mt__ % 2), sub__ * 128)
                        prep_tile(0)
                        for mt_ in range(T // 512):
                            do_mt(mt_, hTs[mt_ % 2], ("hT", mt_ % 2))
                    S.barrier()

            mark(f'L{l} p1 done')
            if 2 in passes:
                with contextlib.ExitStack() as ps:
                    sbp = lambda n, s, dt: ps.enter_context(nc.sbuf_tensor(f"sbB{l}_" + n, s, dt))
                    rawh = [sbp(f"rawh{i}", [96, 12, 516], BF16) for i in range(2)]
                    dg = sbp("dg", [96, 12, 5, 96], BF16)
                    for jj in range(12):
                        S.op("dve" if jj % 2 == 0 else "pool", lambda e, jj=jj: e.tensor_tensor(
                            out=dg[:, jj, :, :], in0=ident_bf[0:96, 0:96].unsqueeze(1).broadcast_to([96, 5, 96]),
                            in1=convw[:, jj, :].unsqueeze(2).broadcast_to([96, 5, 96]), op=ALU.mult),
                            reads=["c_idbf", "convw"], writes=["dg"])
                    sil = [sbp(f"sil{i}", [96, 512], F32) for i in range(4)]
                    sqb = [sbp(f"sqb{i}", [96, 512], BF16) for i in range(4)]
                    rsd = [sbp(f"rsd{i}", [96, 512], F32) for i in range(4)]
                    c2_o = [sbp(f"c2o{i}", [96, 12, 512], BF16) for i in range(2)]
                    for mt in range(T // 512):
                        t0 = mt * 512
                        rs_ = mt % 2
                        lo = max(t0 - 2, 0)
                        hi = min(t0 + 514, T)
                        rk = ("rawh", rs_)
                        if mt == 0:
                            S.op("pool", lambda e, rs_=rs_: e.memset(rawh[rs_][:, :, 0:2], 0.0), writes=[rk])
                        if mt == T // 512 - 1:
                            S.op("pool", lambda e, rs_=rs_: e.memset(rawh[rs_][:, :, 514:516], 0.0), writes=[rk])
                        S.dma("sp", lambda e, rs_=rs_, lo=lo, hi=hi, t0=t0: e.dma_start(
                            out=rawh[rs_][:, :, lo - (t0 - 2):hi - (t0 - 2)], in_=RAWT[:, :, lo:hi].rearrange("j p c -> p j c")),
                            reads=["RAWT"], writes=[rk])
                        for jj in range(12):
                            a_ = jj % 4
                            ak = f"pb{1 + a_}"
                            accp = PB[1 + a_][0:96, :]
                            for k in range(5):
                                S.op("pe", lambda e, jj=jj, a_=a_, rs_=rs_, k=k: e.matmul(PB[1 + a_][0:96, :], lhsT=dg[:, jj, k, :],
                                                                                         rhs=rawh[rs_][:, jj, k:k + 512], start=(k == 0), stop=(k == 4)),
                                     reads=[rk, "dg"], writes=[ak])
                            s_ = jj % 4
                            S.op("act", lambda e, a_=a_, s_=s_: e.activation(out=sil[s_][:], in_=PB[1 + a_][0:96, :], func=AF.Exp, scale=-1.0),
                                 reads=[ak], writes=[("sil", s_)])
                            S.op("dve", lambda e, s_=s_: e.tensor_scalar(out=sil[s_][:], in0=sil[s_][:], scalar1=1.0, scalar2=None, op0=ALU.add),
                                 reads=[("sil", s_)], writes=[("sil", s_)])
                            S.op("dve", lambda e, s_=s_: e.reciprocal(out=sil[s_][:], in_=sil[s_][:]), reads=[("sil", s_)], writes=[("sil", s_)])
                            if jj >= 8:
                                S.op("dve", lambda e, jj=jj, a_=a_, s_=s_, rs_=rs_: e.tensor_tensor(out=c2_o[rs_][:, jj, :], in0=sil[s_][:], in1=PB[1 + a_][0:96, :], op=ALU.mult),
                                     reads=[("sil", s_), ak], writes=[("c2o", rs_)])
                            else:
                                S.op("dve", lambda e, a_=a_, s_=s_: e.tensor_tensor(out=sil[s_][:], in0=sil[s_][:], in1=PB[1 + a_][0:96, :], op=ALU.mult),
                                     reads=[("sil", s_), ak], writes=[("sil", s_)])
                                S.op("pool", lambda e, s_=s_: e.tensor_tensor(out=sqb[s_][:], in0=sil[s_][:], in1=sil[s_][:], op=ALU.mult),
                                     reads=[("sil", s_)], writes=[("sqb", s_)])
                                bk = (5, 6, 7, 0)[s_]
                                S.op("pe", lambda e, s_=s_, bk=bk: e.matmul(PB[bk][0:96, :], lhsT=ones_bf[:], rhs=sqb[s_][:], start=True, stop=True),
                                     reads=[("sqb", s_), "c_onesbf"], writes=[f"pb{bk}"])
                                S.op("act", lambda e, s_=s_, bk=bk: e.activation(out=rsd[s_][:], in_=PB[bk][0:96, :], func=AF.Ln, bias=EPS),
                                     reads=[f"pb{bk}"], writes=[("rsd", s_)])
                                S.op("act", lambda e, s_=s_: e.activation(out=rsd[s_][:], in_=rsd[s_][:], func=AF.Exp, scale=-0.5),
                                     reads=[("rsd", s_)], writes=[("rsd", s_)])
                                qs = (96.0 ** -0.5) if jj < 4 else 1.0
                                S.op("dve", lambda e, s_=s_, jj=jj, qs=qs, rs_=rs_: e.scalar_tensor_tensor(
                                    out=c2_o[rs_][:, jj, :], in0=sil[s_][:], scalar=qs, in1=rsd[s_][:], op0=ALU.mult, op1=ALU.mult),
                                    reads=[("sil", s_), ("rsd", s_)], writes=[("c2o", rs_)])

                        S.dma("pool", lambda e, rs_=rs_, t0=t0: e.dma_start(out=C2T[:, :, t0:t0 + 512].rearrange("j p c -> p j c"),
                                                                          in_=c2_o[rs_][:]), reads=[("c2o", rs_)], writes=["C2T"])
                    S.barrier()

            mark(f'L{l} p2 done')
            if 3 in passes:
                with contextlib.ExitStack() as ps:
                    sbp = lambda n, s, dt: ps.enter_context(nc.sbuf_tensor(f"sbC{l}_" + n, s, dt))
                    two = range(2)
                    glaq = [[sbp(f"glaq{d}{i}", [64, 4, 256], BF16) for i in two] for d in two]
                    glak = [[sbp(f"glak{d}{i}", [64, 4, 256], BF16) for i in two] for d in two]
                    glakv = [[sbp(f"glakv{d}{i}", [64, 4, 640], BF16) for i in two] for d in two]
                    glala = [[sbp(f"glala{d}{i}", [64, 4, 256], F32) for i in two] for d in two]
                    gdnc2 = [[sbp(f"gdnc2{d}{i}", [96, 12, 256], BF16) for i in two] for d in two]
                    gdngb = [[sbp(f"gdngb{d}{i}", [64, 4, 16], F32) for i in two] for d in two]
                    ebT = [sbp(f"ebT{d}", [64, 4, 64], F32) for d in two]
                    enbT = [sbp(f"enbT{d}", [64, 4, 64], F32) for d in two]
                    erem = [sbp(f"erem{d}", [64, 256], F32) for d in two]
                    qeT = [sbp(f"qeT{d}", [64, 4, 64], BF16) for d in two]
                    keT = [sbp(f"keT{d}", [64, 4, 64], BF16) for d in two]
                    ktl = [sbp(f"ktl{d}", [64, 256], BF16) for d in two]
                    ATm = [sbp(f"ATm{d}", [64, 4, 64], BF16) for d in two]
                    gS32 = [sbp(f"gS32{d}", [64, 4, 96], F32) for d in two]
                    gSbf = [sbp(f"gSbf{d}", [64, 4, 96], BF16) for d in two]
                    go_sb = [sbp(f"go_sb{d}", [64, 384], F32) for d in two]
                    ebT8 = sbp("ebT8", [64, 8, 64], F32)
                    enbT8 = sbp("enbT8", [64, 8, 64], F32)
                    erem8 = sbp("erem8", [64, 512], F32)
                    qeT8 = sbp("qeT8", [64, 8, 64], BF16)
                    keT8 = sbp("keT8", [64, 8, 64], BF16)
                    ktl8 = sbp("ktl8", [64, 512], BF16)
                    ATm8 = sbp("ATm8", [64, 8, 64], BF16)
                    gS328 = sbp("gS328", [64, 8, 96], F32)
                    gSbf8 = sbp("gSbf8", [64, 8, 96], BF16)
                    S.op("pool", lambda e: e.memset(gS328[:], 0.0), writes=["gS328"])
                    S.op("pool", lambda e: e.memset(gSbf8[:], 0.0), writes=["gSbf8"])

                    gb8 = sbp("gb8", [64, 16], F32)
                    K8 = sbp("K8", [64, 8, 96], BF16)
                    V8 = sbp("V8", [64, 8, 96], BF16)
                    eg16 = sbp("eg16", [64, 16], F32)
                    dec96 = sbp("dec96m", [96, 8], F32)
                    R8 = sbp("R8", [64, 8, 64], F32)
                    kkN = sbp("kkN", [64, 8, 64], F32)
                    kkM = sbp("kkM", [64, 8, 64], F32)
                    qkI = sbp("qkI", [64, 8, 64], F32)
                    decT = sbp("decT", [64, 8, 64], F32)
                    decD = sbp("decD", [64, 8, 64], F32)
                    EGB8 = sbp("EGB8", [96, 8, 64], F32)
                    DB8 = sbp("DB8", [64, 8, 64], F32)
                    Nm = [sbp(f"Nm{i}", [64, 8, 64], NDT) for i in two]
                    Mm = [sbp(f"Mm{i}", [64, 8, 64], NDT) for i in two]
                    qkm8 = sbp("qkm8", [96, 8, 64], BF16)
                    Pm8 = sbp("Pm8", [64, 8, 64], NDT)
                    TiT8 = sbp("TiT8", [64, 8, 64], BF16)
                    vb8 = sbp("vb8", [64, 8, 96], BF16)
                    bg8 = sbp("bg8", [64, 8], F32)
                    kbg8 = sbp("kbg8", [64, 8, 96], BF16)
                    kte8 = sbp("kte8", [64, 8, 96], BF16)
                    qdT8 = sbp("qdT8", [96, 8, 64], BF16)
                    u8 = sbp("u8", [64, 8, 96], F32)
                    wT8 = sbp("wT8", [96, 8, 64], BF16)
                    vnew8 = sbp("vnew8", [96, 8, 96], BF16)
                    dS328 = sbp("dS328", [96, 8, 96], F32)
                    dSbf8 = sbp("dSbf8", [96, 8, 96], BF16)
                    do_sb = [sbp(f"do_sb{d}", [64, 384], F32) for d in two]
                    S.op("pool", lambda e: e.memset(dS328[:], 0.0), writes=["dS328"])
                    S.op("pool", lambda e: e.memset(qkm8[:], 0.0), writes=["qkm8"])
                    S.op("pool", lambda e: e.memset(vnew8[:], 0.0), writes=["vnew8"])
                    S.op("pool", lambda e: e.memset(dSbf8[:], 0.0), writes=["dSbf8"])

                    for d in two:
                        S.op("pool", lambda e, d=d: e.memset(gS32[d][:], 0.0), writes=[("gS32", d)])
                        S.op("pool", lambda e, d=d: e.memset(gSbf[d][:], 0.0), writes=[("gSbf", d)])

                    def load_group(d, grp):
                        gs = grp % 2
                        g0 = grp * 256
                        S.dma("sp", lambda e: e.dma_start(out=glaq[d][gs][:], in_=QKT[0:256, g0:g0 + 256].rearrange("(h p) c -> p h c", p=64)),
                              reads=["QKT"], writes=[("glaq", d, gs)])
                        S.dma("sp", lambda e: e.dma_start(out=glak[d][gs][:], in_=QKT[256:512, g0:g0 + 256].rearrange("(h p) c -> p h c", p=64)),
                              reads=["QKT"], writes=[("glak", d, gs)])
                        S.dma("sp", lambda e: e.dma_start(out=glakv[d][gs][:], in_=KV[g0:g0 + 256, :].rearrange("(n p) c -> p n c", p=64)),
                              reads=["KV"], writes=[("glakv", d, gs)])
                        S.dma("sp", lambda e: e.dma_start(out=glala[d][gs][:], in_=LA[g0:g0 + 256, d * 256:(d + 1) * 256].rearrange("(n p) c -> p n c", p=64)),
                              reads=["LA"], writes=[("glala", d, gs)])
                        S.dma("sp", lambda e: e.dma_start(out=gdnc2[d][gs][:], in_=C2T[:, :, g0:g0 + 256].rearrange("j p c -> p j c")),
                              reads=["C2T"], writes=[("gdnc2", d, gs)])
                        S.dma("sp", lambda e: e.dma_start(out=gdngb[d][gs][:], in_=GB[g0:g0 + 256, :].rearrange("(n p) c -> p n c", p=64)),
                              reads=["GB"], writes=[("gdngb", d, gs)])

                    def gla_chunk(d, n):
                        grp, ci = divmod(n, 4)
                        gs = grp % 2
                        qg, kg, kvg, lag = glaq[d][gs], glak[d][gs], glakv[d][gs], glala[d][gs]
                        kq_, kk_, kkv_, kla_ = ("glaq", d, gs), ("glak", d, gs), ("glakv", d, gs), ("glala", d, gs)
                        cs = slice(ci * 64, (ci + 1) * 64)
                        mI = LE if d == 0 else GE
                        mS = GT if d == 0 else LT
                        bT_ps = PB[0][0:64, 0:256].rearrange("p (h c) -> p h c", h=4)
                        rem_ps = PB[0][0:64, 256:512]
                        import os as _os
                        STG = int(_os.environ.get('K_STAGE', 99))
                        for h in range(4):
                            S.op("pe", lambda e, h=h: e.matmul(bT_ps[:, h, :], lhsT=lag[:, ci, h * 64:(h + 1) * 64], rhs=masks[:, mI, 0, :],
                                                               start=True, stop=True), reads=[kla_, "c_masks"], writes=["pb0"])
                        S.op("pe", lambda e: e.matmul(rem_ps, lhsT=masks[:, mS, 0, :], rhs=lag[:, ci, :], start=True, stop=True),
                             reads=[kla_, "c_masks"], writes=["pb0"])
                        yield
                        S.op("act", lambda e: e.activation(out=ebT[d][:], in_=bT_ps, func=AF.Exp), reads=["pb0"], writes=[("ebT", d)])
                        S.op("act", lambda e: e.activation(out=enbT[d][:], in_=bT_ps, func=AF.Exp, scale=-1.0), reads=["pb0"], writes=[("enbT", d)])
                        S.op("act", lambda e: e.activation(out=erem[d][:], in_=rem_ps, func=AF.Exp), reads=["pb0"], writes=[("erem", d)])
                        yield
                        S.op("dve", lambda e: e.tensor_tensor(out=qeT[d][:], in0=qg[:, :, cs], in1=ebT[d][:], op=ALU.mult),
                             reads=[kq_, ("ebT", d)], writes=[("qeT", d)])
                        S.op("pool", lambda e: e.tensor_tensor(out=keT[d][:], in0=kg[:, :, cs], in1=enbT[d][:], op=ALU.mult),
                             reads=[kk_, ("enbT", d)], writes=[("keT", d)])
                        S.op("dve", lambda e: e.tensor_tensor(out=ktl[d][:], in0=kvg[:, ci, 0:256], in1=erem[d][:], op=ALU.mult),
                             reads=[kkv_, ("erem", d)], writes=[("ktl", d)])
                        yield
                        AT_ps = PB[1][0:64, 0:256].rearrange("p (h c) -> p h c", h=4)
                        for h in range(4):
                            S.op("pe", lambda e, h=h: e.matmul(AT_ps[:, h, :], lhsT=keT[d][:, h, :], rhs=qeT[d][:, h, :], start=True, stop=True),
                                 reads=[("keT", d), ("qeT", d)], writes=["pb1"])
                        yield
                        S.op("dve", lambda e: e.tensor_tensor(out=ATm[d][:], in0=AT_ps, in1=masks[:, mI, :, :], op=ALU.mult),
                             reads=["pb1", "c_masks"], writes=[("ATm", d)])
                        yield
                        o_ps = PB[0][0:64, 0:384].rearrange("p (h v) -> p h v", h=4)
                        for h in range(4):
                            S.op("pe", lambda e, h=h: e.matmul(o_ps[:, h, :], lhsT=ATm[d][:, h, :], rhs=kvg[:, ci, 256 + h * 96:256 + (h + 1) * 96],
                                                               start=True, stop=False), reads=[("ATm", d), kkv_], writes=["pb0"])
                            S.op("pe", lambda e, h=h: e.matmul(o_ps[:, h, :], lhsT=qeT[d][:, h, :], rhs=gSbf[d][:, h, :], start=False, stop=True),
                                 reads=[("qeT", d), ("gSbf", d)], writes=["pb0"])
                        S.op("act", lambda e: e.copy(out=go_sb[d][:], in_=PB[0][0:64, 0:384]), reads=["pb0"], writes=[("go_sb", d)])
                        S.dma("pool", lambda e: e.dma_start(out=OFB[d][n * 64:(n + 1) * 64, 0:384], in_=go_sb[d][:]),
                              reads=[("go_sb", d)], writes=[("OFB", d)])
                        yield
                        kv_ps = PB[1][0:64, 0:384].rearrange("p (h v) -> p h v", h=4)
                        for h in range(4):
                            S.op("pe", lambda e, h=h: e.matmul(kv_ps[:, h, :], lhsT=ktl[d][:, h * 64:(h + 1) * 64],
                                                               rhs=kvg[:, ci, 256 + h * 96:256 + (h + 1) * 96], start=True, stop=True),
                                 reads=[("ktl", d), kkv_], writes=["pb1"])
                        dcol = 63 if d == 0 else 0
                        S.op("pool", lambda e: e.tensor_tensor(out=gS32[d][:], in0=gS32[d][:],
                                                               in1=ebT[d][:, :, dcol:dcol + 1].broadcast_to([64, 4, 96]), op=ALU.mult),
                             reads=[("gS32", d), ("ebT", d)], writes=[("gS32", d)])
                        S.op("dve", lambda e: e.tensor_tensor(out=gS32[d][:], in0=gS32[d][:], in1=kv_ps, op=ALU.add),
                             reads=[("gS32", d), "pb1"], writes=[("gS32", d)])
                        S.op("act", lambda e: e.copy(out=gSbf[d][:], in_=gS32[d][:]), reads=[("gS32", d)], writes=[("gSbf", d)])

                    def gla_step(ns):
                        two_ = range(2)
                        info = []
                        for d in two_:
                            grp, ci = divmod(ns[d], 4)
                            gs = grp % 2
                            info.append((glaq[d][gs], glak[d][gs], glakv[d][gs], glala[d][gs], ("glaq", d, gs), ("glak", d, gs), ("glakv", d, gs),
                                         ("glala", d, gs), slice(ci * 64, (ci + 1) * 64), ci))
                        mIl, mSl = (LE, GE), (GT, LT)
                        mI2 = masks5[:, 0:2, :, :].rearrange("p t r c -> p (t r) c")
                        v64 = lambda t_: t_[0:64, :].rearrange("p (u c) -> p u c", u=8)
                        bT_ps = v64(PB[0])
                        for d in two_:
                            lag, kla_, ci = info[d][3], info[d][7], info[d][9]
                            for h in range(4):
                                u = d * 4 + h
                                S.op("pe", lambda e, u=u, h=h, lag=lag, ci=ci, d=d: e.matmul(bT_ps[:, u, :], lhsT=lag[:, ci, h * 64:(h + 1) * 64],
                                                                                              rhs=masks5[:, mIl[d], 0, :], start=True, stop=True),
                                     reads=[kla_, "c_masks"], writes=["pb0"])
                            S.op("pe", lambda e, lag=lag, ci=ci, d=d: e.matmul(PB[1][0:64, d * 256:(d + 1) * 256], lhsT=masks5[:, mSl[d], 0, :], rhs=lag[:, ci, :],
                                                                              start=True, stop=True), reads=[kla_, "c_masks"], writes=["pb1"])
                        S.op("act", lambda e: e.activation(out=ebT8[:], in_=bT_ps, func=AF.Exp), reads=["pb0"], writes=["ebT8"])
                        S.op("act", lambda e: e.activation(out=enbT8[:], in_=bT_ps, func=AF.Exp, scale=-1.0), reads=["pb0"], writes=["enbT8"])
                        S.op("act", lambda e: e.activation(out=erem8[:], in_=PB[1][0:64, :], func=AF.Exp), reads=["pb1"], writes=["erem8"])
                        yield
                        for d in two_:
                            qg, kg, kvg, kq_, kk_, kkv_, cs, ci = info[d][0], info[d][1], info[d][2], info[d][4], info[d][5], info[d][6], info[d][8], info[d][9]
                            S.op("dve", lambda e, d=d, qg=qg, cs=cs: e.tensor_tensor(out=qeT8[:, d * 4:(d + 1) * 4, :], in0=qg[:, :, cs],
                                                                                     in1=ebT8[:, d * 4:(d + 1) * 4, :], op=ALU.mult),
                                 reads=[kq_, "ebT8"], writes=["qeT8"])
                            S.op("pool", lambda e, d=d, kg=kg, cs=cs: e.tensor_tensor(out=keT8[:, d * 4:(d + 1) * 4, :], in0=kg[:, :, cs],
                                                                                      in1=enbT8[:, d * 4:(d + 1) * 4, :], op=ALU.mult),
                                 reads=[kk_, "enbT8"], writes=["keT8"])
                            S.op("dve", lambda e, d=d, kvg=kvg, ci=ci: e.tensor_tensor(out=ktl8[:, d * 256:(d + 1) * 256], in0=kvg[:, ci, 0:256],
                                                                                       in1=erem8[:, d * 256:(d + 1) * 256], op=ALU.mult),
                                 reads=[kkv_, "erem8"], writes=["ktl8"])
                        yield
                        AT_ps = v64(PB[0])
                        for u in range(8):
                            S.op("pe", lambda e, u=u: e.matmul(AT_ps[:, u, :], lhsT=keT8[:, u, :], rhs=qeT8[:, u, :], start=True, stop=True),
                                 reads=["keT8", "qeT8"], writes=["pb0"])
                        S.op("dve", lambda e: e.tensor_tensor(out=ATm8[:], in0=AT_ps, in1=mI2, op=ALU.mult), reads=["pb0", "c_masks"], writes=["ATm8"])
                        yield
                        for d in two_:
                            kvg, kkv_, ci = info[d][2], info[d][6], info[d][9]
                            o_ps = PB[d][0:64, 0:384].rearrange("p (h v) -> p h v", h=4)
                            for h in range(4):
                                u = d * 4 + h
                                S.op("pe", lambda e, u=u, h=h, kvg=kvg, ci=ci, o_ps=o_ps: e.matmul(o_ps[:, h, :], lhsT=ATm8[:, u, :],
                                                                                                 rhs=kvg[:, ci, 256 + h * 96:256 + (h + 1) * 96],
                                                                                                 start=True, stop=False),
                                     reads=["ATm8", kkv_], writes=[f"pb{d}"])
                                S.op("pe", lambda e, u=u, h=h, o_ps=o_ps: e.matmul(o_ps[:, h, :], lhsT=qeT8[:, u, :], rhs=gSbf8[:, u, :], start=False, stop=True),
                                     reads=["qeT8", "gSbf8"], writes=[f"pb{d}"])
                            S.op("act", lambda e, d=d: e.copy(out=go_sb[d][:], in_=PB[d][0:64, 0:384]), reads=[f"pb{d}"], writes=[("go_sb", d)])
                            S.dma("pool", lambda e, d=d: e.dma_start(out=OFB[d][ns[d] * 64:(ns[d] + 1) * 64, 0:384], in_=go_sb[d][:]),
                                  reads=[("go_sb", d)], writes=[("OFB", d)])
                        yield
                        for d in two_:
                            dcol = 63 if d == 0 else 0
                            S.op("pool", lambda e, d=d, dcol=dcol: e.tensor_tensor(out=gS328[:, d * 4:(d + 1) * 4, :], in0=gS328[:, d * 4:(d + 1) * 4, :],
                                                                                   in1=ebT8[:, d * 4:(d + 1) * 4, dcol:dcol + 1].broadcast_to([64, 4, 96]), op=ALU.mult),
                                 reads=["gS328", "ebT8"], writes=["gS328"])
                        for d in two_:
                            kvg, kkv_, ci = info[d][2], info[d][6], info[d][9]
                            kv_ps = PB[d][0:64, 0:384].rearrange("p (h v) -> p h v", h=4)
                            for h in range(4):
                                S.op("pe", lambda e, d=d, h=h, kvg=kvg, ci=ci, kv_ps=kv_ps: e.matmul(kv_ps[:, h, :], lhsT=ktl8[:, d * 256 + h * 64:d * 256 + (h + 1) * 64],
                                                                                                   rhs=kvg[:, ci, 256 + h * 96:256 + (h + 1) * 96], start=True, stop=True),
                                     reads=["ktl8", kkv_], writes=[f"pb{d}"])
                            S.op("dve", lambda e, d=d, kv_ps=kv_ps: e.tensor_tensor(out=gS328[:, d * 4:(d + 1) * 4, :], in0=gS328[:, d * 4:(d + 1) * 4, :], in1=kv_ps,
                                                                                    op=ALU.add), reads=["gS328", f"pb{d}"], writes=["gS328"])
                        S.op("act", lambda e: e.copy(out=gSbf8[:], in_=gS328[:]), reads=["gS328"], writes=["gSbf8"])
                        yield


                    def gdn_step(ns):
                        import os as _os
                        two_ = range(2)
                        info = []
                        for d in two_:
                            grp, ci = divmod(ns[d], 4)
                            gs = grp % 2
                            info.append((gdnc2[d][gs], gdngb[d][gs], ("gdnc2", d, gs), ("gdngb", d, gs), slice(ci * 64, (ci + 1) * 64), ci))
                        m8 = lambda a, b_: masks5[:, a:b_, :, :].rearrange("p t r c -> p (t r) c")
                        mI2, mS2, mSo2 = m8(0, 2), m8(2, 4), m8(3, 5)
                        mIl, mSl = (LE, GE), (GT, LT)
                        b8 = lambda ap, w, p=64: ap.unsqueeze(2).broadcast_to([p, 8, w])
                        idb8 = ident_f[:].unsqueeze(1).broadcast_to([64, 8, 64])
                        v64 = lambda t_: t_[0:64, :].rearrange("p (u c) -> p u c", u=8)
                        for d in two_:
                            gbg, kgb, ci = info[d][1], info[d][3], info[d][5]
                            S.op("pool", lambda e, d=d, gbg=gbg, ci=ci: e.tensor_copy(
                                out=gb8[:].rearrange("p (t z h) -> p t z h", t=2, z=2)[:, :, d, :],
                                in_=gbg[:, ci, :].rearrange("p (t z h) -> p t z h", t=2, z=2)[:, :, d, :]), reads=[kgb], writes=["gb8"])
                        beta8, g8 = gb8[:, 0:8], gb8[:, 8:16]
                        for d in two_:
                            c2g, kc2, cs = info[d][0], info[d][2], info[d][4]
                            tr_ps = PB[2 + d][0:64, 0:384].bitcast(BF16).rearrange("p (u x) -> p u x", u=8)
                            for u in range(8):
                                S.op("pe", lambda e, u=u, c2g=c2g, cs=cs, tr_ps=tr_ps: e.transpose(out=tr_ps[:, u, :], in_=c2g[:, 4 + u, cs],
                                                                                                 identity=ident_bf[0:96, 0:96]),
                                     reads=[kc2, "c_idbf"], writes=[f"pb{2 + d}"])
                            S.op("act", lambda e, d=d, tr_ps=tr_ps: e.copy(out=K8[:, d * 4:(d + 1) * 4, :], in_=tr_ps[:, 0:4, :]), reads=[f"pb{2 + d}"], writes=["K8"])
                            S.op("act", lambda e, d=d, tr_ps=tr_ps: e.copy(out=V8[:, d * 4:(d + 1) * 4, :], in_=tr_ps[:, 4:8, :]), reads=[f"pb{2 + d}"], writes=["V8"])
                        yield
                        for d in two_:
                            S.op("pe", lambda e, d=d: e.matmul(PB[3][0:64, 384 + d * 4:388 + d * 4], lhsT=masks5[:, mIl[d], 0, :], rhs=g8[:, d * 4:(d + 1) * 4],
                                                               start=True, stop=True), reads=["c_masks", "gb8"], writes=["pb3"])
                            S.op("pe", lambda e, d=d: e.matmul(PB[3][0:64, 392 + d * 4:396 + d * 4], lhsT=masks5[:, mSl[d], 0, :], rhs=g8[:, d * 4:(d + 1) * 4],
                                                               start=True, stop=True), reads=["c_masks", "gb8"], writes=["pb3"])
                        S.op("pe", lambda e: e.matmul(PB[3][0:96, 400:408], lhsT=ones_f[:, :], rhs=g8, start=True, stop=True),
                             reads=["c_onesf", "gb8"], writes=["pb3"])
                        S.op("act", lambda e: e.activation(out=eg16[:], in_=PB[3][0:64, 384:400], func=AF.Exp), reads=["pb3"], writes=["eg16"])
                        S.op("act", lambda e: e.activation(out=dec96[:], in_=PB[3][0:96, 400:408], func=AF.Exp), reads=["pb3"], writes=["dec96"])
                        yield
                        kk_ps, qk_ps = v64(PB[4]), v64(PB[5])
                        for d in two_:
                            c2g, kc2, cs = info[d][0], info[d][2], info[d][4]
                            for h in range(4):
                                u = d * 4 + h
                                S.op("pe", lambda e, u=u, h=h, c2g=c2g, cs=cs: e.matmul(kk_ps[:, u, :], lhsT=c2g[:, 4 + h, cs], rhs=c2g[:, 4 + h, cs],
                                                                                        start=True, stop=True), reads=[kc2], writes=["pb4"])
                                S.op("pe", lambda e, u=u, h=h, c2g=c2g, cs=cs: e.matmul(qk_ps[:, u, :], lhsT=c2g[:, 4 + h, cs], rhs=c2g[:, h, cs],
                                                                                        start=True, stop=True), reads=[kc2], writes=["pb5"])
                        yield
                        S.op("dve", lambda e: e.tensor_tensor(out=R8[:], in0=mI2, in1=b8(g8, 64), op=ALU.mult), reads=["c_masks", "gb8"], writes=["R8"])
                        S.op("dve", lambda e: e.tensor_tensor(out=kkN[:], in0=kk_ps, in1=mS2, op=ALU.mult), reads=["pb4", "c_masks"], writes=["kkN"])
                        S.op("dve", lambda e: e.tensor_tensor(out=kkM[:], in0=kk_ps, in1=mSo2, op=ALU.mult), reads=["pb4", "c_masks"], writes=["kkM"])
                        S.op("dve", lambda e: e.tensor_tensor(out=qkI[:], in0=qk_ps, in1=mI2, op=ALU.mult), reads=["pb5", "c_masks"], writes=["qkI"])
                        Rflat = R8[:].rearrange("p u c -> p (u c)")
                        yield
                        for d in two_:
                            S.op("pe", lambda e, d=d: e.matmul(PB[2][0:64, d * 256:(d + 1) * 256], lhsT=masks5[:, mSl[d], 0, :], rhs=Rflat[:, d * 256:(d + 1) * 256],
                                                               start=True, stop=True), reads=["c_masks", "R8"], writes=["pb2"])
                        for u in range(8):
                            S.op("pe", lambda e, u=u: e.matmul(PB[3][0:64, u * 64:(u + 1) * 64], lhsT=R8[:, u, :], rhs=masks5[:, mSl[u // 4], 0, :],
                                                               start=True, stop=True), reads=["c_masks", "R8"], writes=["pb3"])
                        S.op("act", lambda e: e.activation(out=decT[:], in_=v64(PB[2]), func=AF.Exp), reads=["pb2"], writes=["decT"])
                        S.op("act", lambda e: e.activation(out=decD[:], in_=v64(PB[3]), func=AF.Exp), reads=["pb3"], writes=["decD"])
                        yield
                        S.op("pe", lambda e: e.matmul(PB[6][0:96, :], lhsT=ones_f[:, :], rhs=Rflat, start=True, stop=True),
                             reads=["c_onesf", "R8"], writes=["pb6"])
                        S.op("act", lambda e: e.activation(out=EGB8[:], in_=PB[6][0:96, :].rearrange("p (u c) -> p u c", u=8), func=AF.Exp),
                             reads=["pb6"], writes=["EGB8"])
                        yield
                        S.op("pool", lambda e: e.tensor_tensor(out=DB8[:], in0=idb8, in1=b8(beta8, 64), op=ALU.mult), reads=["c_idf", "gb8"], writes=["DB8"])
                        S.op("pe", lambda e: e.matmul(PB[7][0:64, :], lhsT=ones_f[:, 0:64], rhs=DB8[:].rearrange("p u c -> p (u c)"), start=True, stop=True),
                             reads=["c_onesf", "DB8"], writes=["pb7"])
                        yield
                        S.op("dve", lambda e: e.tensor_tensor(out=kkN[:], in0=kkN[:], in1=decD[:], op=ALU.mult), reads=["kkN", "decD"], writes=["kkN"])
                        S.op("pool", lambda e: e.tensor_tensor(out=Nm[0][:], in0=kkN[:], in1=b8(beta8, 64), op=ALU.mult), reads=["kkN", "gb8"], writes=[("Nm", 0)])
                        S.op("pool", lambda e: e.tensor_tensor(out=kkM[:], in0=kkM[:], in1=decT[:], op=ALU.mult), reads=["kkM", "decT"], writes=["kkM"])
                        S.op("dve", lambda e: e.tensor_tensor(out=Mm[0][:], in0=kkM[:], in1=v64(PB[7]), op=ALU.mult), reads=["kkM", "pb7"], writes=[("Mm", 0)])
                        S.op("pool", lambda e: e.tensor_tensor(out=Pm8[:], in0=idb8, in1=Mm[0][:], op=ALU.subtract), reads=[("Mm", 0), "c_idf"], writes=["Pm8"])
                        S.op("pool", lambda e: e.tensor_tensor(out=qkm8[0:64, :, :], in0=qkI[:], in1=decT[:], op=ALU.mult), reads=["qkI", "decT"], writes=["qkm8"])
                        yield
                        S.op("pool", lambda e: e.tensor_tensor(out=vb8[:], in0=V8[:], in1=b8(beta8, 96), op=ALU.mult), reads=["V8", "gb8"], writes=["vb8"])
                        S.op("dve", lambda e: e.tensor_tensor(out=bg8[:], in0=beta8, in1=eg16[:, 0:8], op=ALU.mult), reads=["gb8", "eg16"], writes=["bg8"])
                        S.op("pool", lambda e: e.tensor_tensor(out=kbg8[:], in0=K8[:], in1=b8(bg8[:], 96), op=ALU.mult), reads=["K8", "bg8"], writes=["kbg8"])
                        S.op("pool", lambda e: e.tensor_tensor(out=kte8[:], in0=K8[:], in1=b8(eg16[:, 8:16], 96), op=ALU.mult), reads=["K8", "eg16"], writes=["kte8"])
                        for d in two_:
                            c2g, kc2, cs = info[d][0], info[d][2], info[d][4]
                            S.op("pool", lambda e, d=d, c2g=c2g, cs=cs: e.tensor_tensor(out=qdT8[:, d * 4:(d + 1) * 4, :], in0=c2g[:, 0:4, cs],
                                                                                       in1=EGB8[:, d * 4:(d + 1) * 4, :], op=ALU.mult),
                                 reads=[kc2, "EGB8"], writes=["qdT8"])
                        yield
                        sqM, sqN, pd = v64(PB[4]), v64(PB[5]), v64(PB[6])

                        def emit_sq(lev, cur, nxt):
                            lastlev = (lev == 4)
                            for u in range(8):
                                if not lastlev:
                                    S.op("pe", lambda e, u=u: e.matmul(sqM[:, u, :], lhsT=Nm[cur][:, u, :], rhs=Mm[cur][:, u, :], start=True, stop=True),
                                         reads=[("Nm", cur), ("Mm", cur)], writes=["pb4"])
                                S.op("pe", lambda e, u=u: e.matmul(sqN[:, u, :], lhsT=Mm[cur][:, u, :], rhs=Nm[cur][:, u, :], start=True, stop=True),
                                     reads=[("Nm", cur), ("Mm", cur)], writes=["pb5"])
                            S.op("act", lambda e: e.copy(out=Nm[nxt][:], in_=sqN), reads=["pb5"], writes=[("Nm", nxt)])
                            if not lastlev:
                                S.op("dve", lambda e: e.tensor_copy(out=Mm[nxt][:], in_=sqM), reads=["pb4"], writes=[("Mm", nxt)])

                        def emit_pd(lev, nb_):
                            for u in range(8):
                                S.op("pe", lambda e, u=u: e.matmul(pd[:, u, :], lhsT=Nm[nb_][:, u, :], rhs=Pm8[:, u, :], start=True, stop=True),
                                     reads=[("Nm", nb_), "Pm8"], writes=["pb6"])
                            if lev < 4:
                                S.op("dve", lambda e: e.tensor_tensor(out=Pm8[:], in0=Pm8[:], in1=pd, op=ALU.add), reads=["Pm8", "pb6"], writes=["Pm8"])
                            else:
                                S.op("dve", lambda e: e.tensor_tensor(out=TiT8[:], in0=Pm8[:], in1=pd, op=ALU.add), reads=["Pm8", "pb6"], writes=["TiT8"])

                        emit_sq(0, 0, 1)
                        yield
                        for lev in range(5):
                            if lev < 4:
                                emit_sq(lev + 1, (lev + 1) % 2, (lev + 2) % 2)
                            emit_pd(lev, (lev + 1) % 2)
                            yield

                        wT_ps = PB[7][0:96, :].rearrange("p (u c) -> p u c", u=8)
                        for d in two_:
                            u_ps = PB[2 + d][0:64, 0:384].rearrange("p (h v) -> p h v", h=4)
                            for h in range(4):
                                u = d * 4 + h
                                S.op("pe", lambda e, u=u, h=h, u_ps=u_ps: e.matmul(u_ps[:, h, :], lhsT=TiT8[:, u, :], rhs=vb8[:, u, :], start=True, stop=True),
                                     reads=["TiT8", "vb8"], writes=[f"pb{2 + d}"])
                            S.op("act", lambda e, d=d, u_ps=u_ps: e.copy(out=u8[:, d * 4:(d + 1) * 4, :], in_=u_ps), reads=[f"pb{2 + d}"], writes=["u8"])
                        for u in range(8):
                            S.op("pe", lambda e, u=u: e.matmul(wT_ps[:, u, :], lhsT=kbg8[:, u, :], rhs=TiT8[:, u, :], start=True, stop=True),
                                 reads=["TiT8", "kbg8"], writes=["pb7"])
                        S.op("act", lambda e: e.copy(out=wT8[:], in_=wT_ps), reads=["pb7"], writes=["wT8"])
                        yield
                        for d in two_:
                            ws_ps = PB[4 + d][0:64, 0:384].rearrange("p (h v) -> p h v", h=4)
                            for h in range(4):
                                u = d * 4 + h
                                S.op("pe", lambda e, u=u, h=h, ws_ps=ws_ps: e.matmul(ws_ps[:, h, :], lhsT=wT8[:, u, :], rhs=dSbf8[:, u, :], start=True, stop=True),
                                     reads=["wT8", "dSbf8"], writes=[f"pb{4 + d}"])
                            S.op("dve", lambda e, d=d, ws_ps=ws_ps: e.tensor_tensor(out=vnew8[0:64, d * 4:(d + 1) * 4, :], in0=u8[:, d * 4:(d + 1) * 4, :], in1=ws_ps,
                                                                                    op=ALU.subtract), reads=["u8", f"pb{4 + d}"], writes=["vnew8"])
                        yield
                        for d in two_:
                            o_ps = PB[2 + d][0:64, 0:384].rearrange("p (h v) -> p h v", h=4)
                            for h in range(4):
                                u = d * 4 + h
                                S.op("pe", lambda e, u=u, h=h, o_ps=o_ps: e.matmul(o_ps[:, h, :], lhsT=qdT8[:, u, :], rhs=dSbf8[:, u, :], start=True, stop=False),
                                     reads=["qdT8", "dSbf8"], writes=[f"pb{2 + d}"])
                                S.op("pe", lambda e, u=u, h=h, o_ps=o_ps: e.matmul(o_ps[:, h, :], lhsT=qkm8[:, u, :], rhs=vnew8[:, u, :], start=False, stop=True),
                                     reads=["qkm8", "vnew8"], writes=[f"pb{2 + d}"])
                            S.op("act", lambda e, d=d: e.copy(out=do_sb[d][:], in_=PB[2 + d][0:64, 0:384]), reads=[f"pb{2 + d}"], writes=[("do_sb", d)])
                            S.dma("pool", lambda e, d=d: e.dma_start(out=OFB[d][ns[d] * 64:(ns[d] + 1) * 64, 384:768], in_=do_sb[d][:]),
                                  reads=[("do_sb", d)], writes=[("OFB", d)])
                        S.op("pool", lambda e: e.tensor_tensor(out=dS328[:], in0=dS328[:], in1=dec96[:].unsqueeze(2).broadcast_to([96, 8, 96]), op=ALU.mult),
                             reads=["dS328", "dec96"], writes=["dS328"])
                        for d in two_:
                            kv_ps = PB[6 + d][0:96, 0:384].rearrange("p (h v) -> p h v", h=4)
                            for h in range(4):
                                u = d * 4 + h
                                S.op("pe", lambda e, u=u, h=h, kv_ps=kv_ps: e.matmul(kv_ps[:, h, :], lhsT=kte8[:, u, :], rhs=vnew8[0:64, u, :], start=True, stop=True),
                                     reads=["kte8", "vnew8"], writes=[f"pb{6 + d}"])
                            S.op("dve", lambda e, d=d, kv_ps=kv_ps: e.tensor_tensor(out=dS328[:, d * 4:(d + 1) * 4, :], in0=dS328[:, d * 4:(d + 1) * 4, :], in1=kv_ps,
                                                                                    op=ALU.add), reads=["dS328", f"pb{6 + d}"], writes=["dS328"])
                        S.op("act", lambda e: e.copy(out=dSbf8[:], in_=dS328[:]), reads=["dS328"], writes=["dSbf8"])


                    load_group(0, 0)
                    load_group(1, 15)
                    import os as _os
                    for j in range(int(_os.environ.get('K_NSTEPS', NCH))):
                        nf, nbk = j, NCH - 1 - j
                        if j % 4 == 0 and j + 4 < NCH:
                            load_group(0, j // 4 + 1)
                            load_group(1, 15 - (j // 4 + 1))
                        fillers = [gla_step((nf, nbk))] if "gla" in parts else []
                        main = gdn_step((nf, nbk)) if "gdn" in parts else iter(())
                        fi = 0
                        if not _osm.environ.get('K_ILV'):
                            for _ in range(int(_osm.environ.get('K_PRE', 0))):
                                next(main, "done")
                            for f_ in fillers:
                                for _ in f_:
                                    pass
                            fillers = []
                        while True:
                            main_alive = next(main, "done") != "done"
                            adv = False
                            while fi < len(fillers):
                                if next(fillers[fi], "done") != "done":
                                    adv = True
                                    break
                                fi += 1
                            if not main_alive and not adv:
                                break
                    S.barrier()

            mark(f'L{l} p3 done')
            if 4 in passes:
                with contextlib.ExitStack() as ps:
                    sbp = lambda n, s, dt: ps.enter_context(nc.sbuf_tensor(f"sbD{l}_" + n, s, dt))
                    two = range(2)
                    of_t = [sbp(f"of_t{i}", [128, 768], F32) for i in two]
                    ob_t = [sbp(f"ob_t{i}", [128, 768], F32) for i in two]
                    gt_t = [sbp(f"gt_t{i}", [128, 768], BF16) for i in two]
                    oc_all = [sbp(f"oc_all{i}", [128, 1024], BF16) for i in two]
                    x_t = [sbp(f"x_t{i}", [128, D], F32) for i in two]
                    xn = [sbp(f"xn{i}", [128, D], F32) for i in two]
                    osq = [sbp(f"osq{i}", [128, 768], F32) for i in two]
                    ss8 = [sbp(f"ss8{i}", [128, 8], F32) for i in two]
                    ocT = [sbp(f"ocT{i}", [128, 8, 128], BF16) for i in two]
                    junk4 = [sbp(f"junk4{i}", [128, D], BF16) for i in two]
                    ss4 = [sbp(f"ss4{i}", [128, 1], F32) for i in two]
                    def p4_tile(tt):
                        r0 = tt * 128
                        sl = tt % 2
                        pbase = 3 * (tt % 2)
                        S.dma("sp", lambda e, sl=sl, r0=r0: e.dma_start(out=of_t[sl][:], in_=OFB[0][r0:r0 + 128, :]), reads=[("OFB", 0)], writes=[("of_t", sl)])
                        S.dma("sp", lambda e, sl=sl, r0=r0: e.dma_start(out=ob_t[sl][:], in_=OFB[1][r0:r0 + 128, :]), reads=[("OFB", 1)], writes=[("ob_t", sl)])
                        S.dma("sp", lambda e, sl=sl, r0=r0: e.dma_start(out=gt_t[sl][:], in_=GATE[r0:r0 + 128, 0:768]), reads=["GATE"], writes=[("gt_t", sl)])
                        S.dma("sp", lambda e, sl=sl, r0=r0: e.dma_start(out=oc_all[sl][:, 768:1024], in_=OC3[r0:r0 + 128, :]), reads=["OC3"], writes=[("oc_all", sl)])
                        S.dma("sp", lambda e, sl=sl, r0=r0: e.dma_start(out=x_t[sl][:], in_=xsrc[r0:r0 + 128, :]), reads=["XSRC%d" % l], writes=[("x_t", sl)])
                        yield
                        S.op("dve", lambda e, sl=sl: e.tensor_tensor(out=of_t[sl][:], in0=of_t[sl][:], in1=ob_t[sl][:], op=ALU.add),
                             reads=[("of_t", sl), ("ob_t", sl)], writes=[("of_t", sl)])
                        S.op("pool", lambda e, sl=sl: e.tensor_tensor(out=osq[sl][:], in0=of_t[sl][:], in1=of_t[sl][:], op=ALU.mult),
                             reads=[("of_t", sl)], writes=[("osq", sl)])
                        S.op("dve", lambda e, sl=sl: e.tensor_reduce(out=ss8[sl][:], in_=osq[sl][:].rearrange("p (h v) -> p h v", h=8), axis=AX.X, op=ALU.add),
                             reads=[("osq", sl)], writes=[("ss8", sl)])
                        yield
                        rms_rstd(ss8[sl][:], ss8[sl][:], 96, ("ss8", sl), ("ss8", sl))
                        yield
                        S.op("dve", lambda e, sl=sl: e.tensor_tensor(out=of_t[sl][:].rearrange("p (h v) -> p h v", h=8),
                                                                     in0=of_t[sl][:].rearrange("p (h v) -> p h v", h=8),
                                                                     in1=ss8[sl][:].unsqueeze(2).broadcast_to([128, 8, 96]), op=ALU.mult),
                             reads=[("of_t", sl), ("ss8", sl)], writes=[("of_t", sl)])
                        S.op("pool", lambda e, sl=sl: e.tensor_tensor(out=of_t[sl][:], in0=of_t[sl][:], in1=ohnorm[:], op=ALU.mult),
                             reads=[("of_t", sl), "ohnorm"], writes=[("of_t", sl)])
                        S.op("dve", lambda e, sl=sl: e.tensor_tensor(out=oc_all[sl][:, 0:768], in0=of_t[sl][:], in1=gt_t[sl][:], op=ALU.mult),
                             reads=[("of_t", sl), ("gt_t", sl)], writes=[("oc_all", sl)])
                        yield
                        ptr = PB[pbase][:, 0:512].bitcast(BF16).rearrange("p (c t) -> p c t", c=8)
                        for c in range(8):
                            S.op("pe", lambda e, c=c, sl=sl: e.transpose(out=ptr[:, c, :], in_=oc_all[sl][:, c * 128:(c + 1) * 128], identity=ident_bf[:]),
                                 reads=[("oc_all", sl), "c_idbf"], writes=[f"pb{pbase}"])
                        S.op("act", lambda e, sl=sl: e.copy(out=ocT[sl][:], in_=ptr), reads=[f"pb{pbase}"], writes=[("ocT", sl)])
                        yield
                        for hf in range(2):
                            bk = pbase + 1 + hf
                            for c in range(8):
                                S.op("pe", lambda e, c=c, hf=hf, bk=bk, sl=sl: e.matmul(PB[bk][:, :], lhsT=ocT[sl][:, c, :], rhs=w_out_sb[:, c, hf * 512:(hf + 1) * 512],
                                                                                 start=(c == 0), stop=(c == 7)), reads=[("ocT", sl), "w_out_sb"], writes=[f"pb{bk}"])
                            S.op("dve", lambda e, hf=hf, bk=bk, sl=sl: e.tensor_tensor(out=xn[sl][:, hf * 512:(hf + 1) * 512], in0=x_t[sl][:, hf * 512:(hf + 1) * 512],
                                                                                       in1=PB[bk][:, :], op=ALU.add),
                                 reads=[("x_t", sl), f"pb{bk}"], writes=[("xn", sl)])
                        yield
                        if not last:
                            S.dma("pool", lambda e, sl=sl, r0=r0: e.dma_start(out=X1[r0:r0 + 128, :], in_=xn[sl][:]), reads=[("xn", sl)], writes=["XSRC%d" % (l + 1)])
                        else:
                            S.op("act", lambda e, sl=sl: e.activation(out=junk4[sl][:], in_=xn[sl][:], func=AF.Square, accum_out=ss4[sl][:]),
                                 reads=[("xn", sl)], writes=[("junk4", sl), ("ss4", sl)])
                            rms_rstd(ss4[sl][:], ss4[sl][:], D, ("ss4", sl), ("ss4", sl))
                            S.op("dve", lambda e, sl=sl: e.tensor_scalar(out=xn[sl][:], in0=xn[sl][:], scalar1=ss4[sl][:], scalar2=None, op0=ALU.mult),
                                 reads=[("xn", sl), ("ss4", sl)], writes=[("xn", sl)])
                            S.op("pool", lambda e, sl=sl: e.tensor_tensor(out=xn[sl][:], in0=xn[sl][:], in1=fnorm[:], op=ALU.mult),
                                 reads=[("xn", sl), "c_fnorm"], writes=[("xn", sl)])
                            S.dma("pool", lambda e, sl=sl, r0=r0: e.dma_start(out=out[r0:r0 + 128, :], in_=xn[sl][:]), reads=[("xn", sl)], writes=["out"])
                    active = []
                    nxt_tt = 0
                    while True:
                        while len(active) < int(_osm.environ.get('K_P4W', 2)) and nxt_tt < T // 128:
                            active.append(p4_tile(nxt_tt))
                            nxt_tt += 1
                        if not active:
                            break
                        for g_ in list(active):
                            if next(g_, "done") == "done":
                                active.remove(g_)
                    S.barrier()

        for l_ in range(nlayers):
            emit_layer(l_)
        S.barrier()
        S.finish()
        mark("end")
        print("MARKS", marks)
        print("ops", S.n_ops, "waits", S.n_waits, {e: len(S.ops[e]) for e in S.ENGS})
    return nc

def host_inputs(inputs, b):
    f32 = np.float32
    g = lambda k: np.asarray(inputs[k], dtype=f32)
    d = {}
    d["x"] = np.ascontiguousarray(g("x")[b])
    d["mem"] = np.ascontiguousarray(g("mem")[b])
    d["w_in"] = g("w_in")
    d["w_out"] = g("w_out")
    d["xa_w_kv"] = g("xa_w_kv")
    d["normw_pc"] = np.ascontiguousarray(g("norm_w").reshape(NL, 8, 128).transpose(0, 2, 1))
    d["memnormw_pc"] = np.ascontiguousarray(g("mem_norm_w").reshape(NL, 8, 128).transpose(0, 2, 1))
    d["w2"] = np.ascontiguousarray(g("gla_w2").transpose(0, 2, 1, 3))
    d["bias_bc"] = np.ascontiguousarray(np.broadcast_to(g("gla_b").reshape(NL, 1, 512), (NL, 128, 512)))
    ohn = np.concatenate([np.tile(g("gla_norm_w"), (1, 4)), np.tile(g("gdn_norm_w"), (1, 4))], axis=1)
    d["ohnorm_bc"] = np.ascontiguousarray(np.broadcast_to(ohn.reshape(NL, 1, 768), (NL, 128, 768)))
    d["xanorm_bc"] = np.ascontiguousarray(np.broadcast_to(np.tile(g("xa_norm_w"), (1, 4)).reshape(NL, 1, 256), (NL, 128, 256)))
    d["convw"] = np.ascontiguousarray(g("gdn_conv_w").reshape(NL, 12, 96, 5).transpose(0, 2, 1, 3))
    d["alog_bc"] = np.ascontiguousarray(np.broadcast_to(g("gdn_a_log").reshape(NL, 1, 8), (NL, 128, 8)))
    d["dtb_bc"] = np.ascontiguousarray(np.broadcast_to(g("gdn_dt_bias").reshape(NL, 1, 8), (NL, 128, 8)))
    d["fnorm_bc"] = np.ascontiguousarray(np.broadcast_to(g("final_norm_w").reshape(1, D), (128, D)))
    d["ident_bf"] = np.eye(128, dtype=f32).astype(ml_dtypes.bfloat16)
    d["ident_f"] = np.eye(64, dtype=f32)
    p = np.arange(64)[:, None]
    q = np.arange(64)[None, :]
    m4 = np.stack([(p <= q), (p >= q), (p > q), (p < q), (p > q)]).astype(f32)
    d["masks"] = np.ascontiguousarray(np.broadcast_to(m4.transpose(1, 0, 2)[:, :, None, :], (64, 5, 4, 64)))
    d["ones_f"] = np.ones((64, 96), f32)
    d["ones_bf"] = np.ones((96, 96), f32).astype(ml_dtypes.bfloat16)
    return d


_NC_CACHE = {}


def kernel(**inputs):
    if "nc" not in _NC_CACHE:
        _NC_CACHE["nc"] = build()
    nc = _NC_CACHE["nc"]
    in_maps = [host_inputs(inputs, b) for b in range(8)]
    res = run_bass_kernel_spmd(nc, in_maps, core_ids=list(range(8)))
    return np.stack([np.asarray(r["out"], dtype=np.float32) for r in res.results], axis=0)
```

```python
import contextlib
import numpy as np
import ml_dtypes
import concourse.bass as bass
import concourse.mybir as mybir
from concourse.bass_utils import run_bass_kernel_spmd

F32 = mybir.dt.float32
BF16 = mybir.dt.bfloat16
AF = mybir.ActivationFunctionType
ALU = mybir.AluOpType
AX = mybir.AxisListType

T = 4096
D = 1024
INW = 3376
NL = 2
MEM = 256
EPS = 1e-6
NCH = T // 64
import os as _osm
NDT = BF16 if _osm.environ.get('K_NDT', 'bf16') == 'bf16' else F32

EPOCH = 30000
N_DMA_SEMS = 32


class Sched:
    ENGS = ("pe", "act", "dve", "pool", "sp")

    def __init__(self, nc, stack):
        self.nc = nc
        self.stack = stack
        self.ops = {e: [] for e in self.ENGS}
        self.cnt = {e: 0 for e in self.ENGS}
        self.epoch = {e: 0 for e in self.ENGS}
        self.sems = {}
        for e in self.ENGS:
            self._new_eng_sem(e)
        self.dma_sems = []
        for i in range(2 * N_DMA_SEMS):
            s = stack.enter_context(nc.semaphore(f"dma{i}"))
            self.dma_sems.append([s, 0])
        self.dma_rr = {"sp": 0, "pool": 0}
        self.known = {e: {} for e in self.ENGS}
        self.last_w = {}
        self.readers = {}
        self.n_waits = 0
        self.n_ops = 0

    def _new_eng_sem(self, e):
        s = self.stack.enter_context(self.nc.semaphore(f"s_{e}_{self.epoch[e]}"))
        self.sems[(e, self.epoch[e])] = s

    def _deps(self, reads, writes):
        deps = []
        for b in reads:
            if b in self.last_w:
                deps.append(self.last_w[b])
        for b in writes:
            if b in self.last_w:
                deps.append(self.last_w[b])
            deps.extend(self.readers.get(b, {}).items())
        return deps

    def _emit_waits(self, eng, deps):
        need = {}
        for (sk, val) in deps:
            if sk[0] == eng and eng == "pe":
                continue
            if self.known[eng].get(sk, 0) >= val:
                continue
            if need.get(sk, 0) < val:
                need[sk] = val
        waits = []
        for sk, val in need.items():
            self.known[eng][sk] = val
            sem = self.sems[sk] if sk[0] in self.ENGS else self.dma_sems[sk[1]][0]
            waits.append((sem, val))
        return waits

    def _record(self, ev, reads, writes):
        for b in writes:
            self.last_w[b] = ev
            self.readers[b] = {}
        for b in reads:
            if b not in writes:
                d = self.readers.setdefault(b, {})
                if d.get(ev[0], 0) < ev[1]:
                    d[ev[0]] = ev[1]

    def op(self, eng, fn, reads=(), writes=()):
        waits = self._emit_waits(eng, self._deps(reads, writes))
        if self.cnt[eng] >= EPOCH:
            self.epoch[eng] += 1
            self.cnt[eng] = 0
            self._new_eng_sem(eng)
        self.cnt[eng] += 1
        sk = (eng, self.epoch[eng])
        sem = self.sems[sk]
        self.n_waits += len(waits)
        self.n_ops += 1

        def run(e, waits=waits, fn=fn, sem=sem):
            for (s, v) in waits:
                e.wait_ge(s, v)
            fn(e).then_inc(sem, 1)

        self.ops[eng].append(run)
        ev = (sk, self.cnt[eng])
        self._record(ev, reads, writes)
        return ev

    def dma(self, eng, fn, reads=(), writes=()):
        i = self.dma_rr[eng] + (0 if eng == "sp" else N_DMA_SEMS)
        self.dma_rr[eng] = (self.dma_rr[eng] + 1) % N_DMA_SEMS
        sem, cur = self.dma_sems[i]
        sk = ("dma", i)
        deps = self._deps(reads, writes)
        if cur > 0:
            deps.append((sk, cur))
        waits = self._emit_waits(eng, deps)
        val = cur + 16
        self.dma_sems[i][1] = val
        self.n_waits += len(waits)
        self.n_ops += 1

        def run(e, waits=waits, fn=fn, sem=sem):
            for (s, v) in waits:
                e.wait_ge(s, v)
            fn(e).then_inc(sem, 16)

        self.ops[eng].append(run)
        ev = (sk, val)
        self._record(ev, reads, writes)
        return ev

    def barrier(self):
        evs = [((e, self.epoch[e]), self.cnt[e]) for e in self.ENGS if self.cnt[e] > 0]
        evs += [(("dma", i), v) for i, (s, v) in enumerate(self.dma_sems) if v > 0]
        for eng in self.ENGS:
            waits = self._emit_waits(eng, evs)

            def run(e, waits=waits):
                for (s, v) in waits:
                    e.wait_ge(s, v)

            self.ops[eng].append(run)

    def wait_all(self, eng, bufs):
        deps = [self.last_w[b] for b in bufs if b in self.last_w]
        waits = self._emit_waits(eng, deps)

        def run(e, waits=waits):
            for (s, v) in waits:
                e.wait_ge(s, v)

        self.ops[eng].append(run)

    def finish(self):
        with self.nc.Block() as block:
            @block.tensor
            def _(e):
                for f in self.ops["pe"]:
                    f(e)

            @block.scalar
            def _(e):
                for f in self.ops["act"]:
                    f(e)

            @block.vector
            def _(e):
                for f in self.ops["dve"]:
                    f(e)

            @block.gpsimd
            def _(e):
                for f in self.ops["pool"]:
                    f(e)

            @block.sync
            def _(e):
                for f in self.ops["sp"]:
                    f(e)


def build(nlayers=NL, debug=False, passes=(0, 1, 2, 3, 4), parts=("gla", "gdn")):
    nc = bass.Bass("TRN2", target_bir_lowering=False)
    dram = lambda n, s, dt, k="ExternalInput": nc.dram_tensor(n, s, dt, kind=k).ap()
    SCR = "ExternalOutput" if debug else "Internal"
    x_in = dram("x", [T, D], F32)
    mem_in = dram("mem", [MEM, D], F32)
    w_in = dram("w_in", [NL, D, INW], F32)
    w_out = dram("w_out", [NL, D, D], F32)
    w_kv = dram("xa_w_kv", [NL, D, 512], F32)
    normw_pc = dram("normw_pc", [NL, 128, 8], F32)
    memnormw_pc = dram("memnormw_pc", [NL, 128, 8], F32)
    w2_in = dram("w2", [NL, 16, 2, 256], F32)
    bias_bc = dram("bias_bc", [NL, 128, 512], F32)
    ohnorm_bc = dram("ohnorm_bc", [NL, 128, 768], F32)
    xanorm_bc = dram("xanorm_bc", [NL, 128, 256], F32)
    convw_in = dram("convw", [NL, 96, 12, 5], F32)
    alog_bc = dram("alog_bc", [NL, 128, 8], F32)
    dtb_bc = dram("dtb_bc", [NL, 128, 8], F32)
    fnorm_bc = dram("fnorm_bc", [128, D], F32)
    ident_bf_in = dram("ident_bf", [128, 128], BF16)
    ident_f_in = dram("ident_f", [64, 64], F32)
    masks_in = dram("masks", [64, 5, 4, 64], F32)
    ones_f_in = dram("ones_f", [64, 96], F32)
    ones_bf_in = dram("ones_bf", [96, 96], BF16)
    out = dram("out", [T, D], F32, "ExternalOutput")
    KV = dram("s_kv", [T, 640], BF16, SCR)
    GATE = dram("s_gate", [T, 1024], BF16, SCR)
    GB = dram("s_gb", [T, 16], F32, SCR)
    LA = dram("s_la", [T, 512], F32, SCR)
    QKT = dram("s_qkt", [512, T], BF16, SCR)
    RAWT = dram("s_rawt", [12, 96, T], BF16, SCR)
    C2T = dram("s_c2t", [12, 96, T], BF16, SCR)
    OC3 = dram("s_oc3", [T, 256], BF16, SCR)
    OFB = [dram("s_of", [T, 768], F32, SCR), dram("s_ob", [T, 768], F32, SCR)]
    X1 = dram("s_x1", [T, D], F32, SCR)

    with contextlib.ExitStack() as st:
        S = Sched(nc, st)
        sb = lambda n, s, dt: st.enter_context(nc.sbuf_tensor("sb_" + n, s, dt))
        ident_bf = sb("ident_bf", [128, 128], BF16)
        ident_f = sb("ident_f", [64, 64], F32)
        masks = sb("masks", [64, 5, 4, 64], F32)
        masks5 = masks
        ones_f = sb("ones_f", [64, 96], F32)
        ones_bf = sb("ones_bf", [96, 96], BF16)
        fnorm = sb("fnorm", [128, D], F32)
        for (t_, d_, k_) in ((ident_bf, ident_bf_in, "c_idbf"), (ident_f, ident_f_in, "c_idf"),
                             (masks, masks_in, "c_masks"), (ones_f, ones_f_in, "c_onesf"),
                             (ones_bf, ones_bf_in, "c_onesbf"), (fnorm, fnorm_bc, "c_fnorm")):
            S.dma("sp", lambda e, t_=t_, d_=d_: e.dma_start(out=t_[:], in_=d_), writes=[k_])
        LE, GE, GT, LT = 0, 1, 2, 3

        w_out_sb = sb("w_out_sb", [128, 8, D], BF16)
        normw = sb("normw", [128, 8], F32)
        memnormw = sb("memnormw", [128, 8], F32)
        w2 = sb("w2sb", [16, 2, 256], F32)
        biasb = sb("biasb", [128, 512], F32)
        ohnorm = sb("ohnorm", [128, 768], F32)
        xanorm = sb("xanorm", [128, 256], F32)
        convw = sb("convw", [96, 12, 5], F32)
        nega = sb("nega", [128, 8], F32)
        dtb = sb("dtb", [128, 8], F32)
        mkT = sb("mkT", [64, 4, MEM], BF16)
        mv = sb("mv", [128, 2, 256], BF16)

        PB = [st.enter_context(nc.psum_tensor(f"pb{i}", [128, 512], F32)) for i in range(8)]

        def rms_rstd(eng_out, src_ap, n, key_src, key_out):
            S.op("act", lambda e: e.activation(out=eng_out, in_=src_ap, func=AF.Ln, scale=1.0 / n, bias=EPS),
                 reads=[key_src], writes=[key_out])
            S.op("act", lambda e: e.activation(out=eng_out, in_=eng_out, func=AF.Exp, scale=-0.5),
                 reads=[key_out], writes=[key_out])

        marks = []
        def mark(name):
            marks.append((name, len(S.ops['pe']), len(S.ops['act']), len(S.ops['dve']), len(S.ops['pool'])))
        def emit_layer(l):
            mark(f'L{l} start')
            xsrc = x_in if l == 0 else X1
            last = (l == nlayers - 1)
            if 0 in passes or 1 in passes:
                with contextlib.ExitStack() as ps:
                    sbp = lambda n, s, dt: ps.enter_context(nc.sbuf_tensor(f"sbA{l}_" + n, s, dt))
                    w_in_sb = sbp("w_in_sb", [128, 8, INW], BF16)
                    stage = [sbp(f"stage{i}", [128, 1688], F32) for i in range(2)]
                    xt = [sbp(f"xt{i}", [128, D], F32) for i in range(2)]
                    junk = sbp("junk", [128, D], BF16)
                    xs = [sbp(f"xs{i}", [128, D], BF16) for i in range(2)]
                    ss = sbp("ss", [128, 1], F32)
                    rstd = sbp("rstd", [128, 1], F32)
                    hTs = [sbp(f"hT{i}", [128, 8, 512], BF16) for i in range(2)]
                    hT = hTs[0]
                    kvo = [sbp(f"kvo{i}", [128, 640], BF16) for i in range(2)]
                    gate_o = [sbp(f"gateo{i}", [128, 1024], BF16) for i in range(2)]
                    gb_o = [sbp(f"gbo{i}", [128, 16], F32) for i in range(2)]
                    gtmp = sbp("gtmp", [128, 8], F32)
                    la_o = [sbp(f"lao{i}", [128, 512], F32) for i in range(2)]
                    fm_o = [sbp(f"fmo{i}", [128, 512], BF16) for i in range(2)]
                    lrT = sbp("lrT", [16, 2, 512], F32)
                    raw_o = sbp("raw_o", [96, 12, 512], BF16)
                    q3T = sbp("q3T", [64, 4, 512], BF16)
                    mx = [sbp(f"mx{i}", [128, 4], F32) for i in range(2)]
                    nb = [sbp(f"nb{i}", [128, 4], F32) for i in range(2)]
                    rs = [sbp(f"rs{i}", [128, 4], F32) for i in range(2)]
                    pexp = [sbp(f"pexp{i}", [128, 4, MEM], BF16) for i in range(2)]
                    pT = [sbp(f"pT{i}", [128, 8, 128], BF16) for i in range(2)]
                    o3 = [sbp(f"o3{i}", [128, 4, 64], F32) for i in range(2)]
                    o3sq = [sbp(f"o3sq{i}", [128, 4, 64], F32) for i in range(2)]
                    o3ss = [sbp(f"o3ss{i}", [128, 4], F32) for i in range(2)]
                    oc3_o = [sbp(f"oc3o{i}", [128, 256], BF16) for i in range(2)]
                    def token_tile_to_hT(src_dram, row0, slot, dst, dst_key, col0):
                        xk = ("xt", slot)
                        S.dma("sp", lambda e: e.dma_start(out=xt[slot][:], in_=src_dram[row0:row0 + 128, :]), writes=[xk])
                        S.op("act", lambda e: e.activation(out=junk[:], in_=xt[slot][:], func=AF.Square, accum_out=ss[:]),
                             reads=[xk], writes=["junk", "ss"])
                        rms_rstd(rstd[:], ss[:], D, "ss", "rstd")
                        S.op("dve", lambda e: e.tensor_scalar(out=xs[slot][:], in0=xt[slot][:], scalar1=rstd[:], scalar2=None,
                                                              op0=ALU.mult), reads=[xk, "rstd"], writes=[("xs", slot)])
                        ptr = PB[0][:, 0:512].bitcast(BF16).rearrange("p (c t) -> p c t", c=8)
                        for c in range(8):
                            S.op("pe", lambda e, c=c: e.transpose(out=ptr[:, c, :], in_=xs[slot][:, c * 128:(c + 1) * 128],
                                                                  identity=ident_bf[:]),
                                 reads=[("xs", slot), "c_idbf"], writes=["pb0"])
                        S.op("dve", lambda e: e.tensor_copy(out=dst[:, :, col0:col0 + 128], in_=ptr), reads=["pb0"],
                             writes=[dst_key])

                    if 0 in passes:
                        for (t_, d_, k_) in ((normw, normw_pc[l], "normw"), (memnormw, memnormw_pc[l], "memnormw"),
                                             (w2, w2_in[l], "w2"), (biasb, bias_bc[l], "biasb"),
                                             (ohnorm, ohnorm_bc[l], "ohnorm"), (xanorm, xanorm_bc[l], "xanorm"),
                                             (convw, convw_in[l], "convw"), (nega, alog_bc[l], "nega"),
                                             (dtb, dtb_bc[l], "dtb")):
                            S.dma("sp", lambda e, t_=t_, d_=d_: e.dma_start(out=t_[:], in_=d_), writes=[k_])
                        S.op("act", lambda e: e.activation(out=nega[:], in_=nega[:], func=AF.Exp), reads=["nega"], writes=["nega"])
                        S.op("dve", lambda e: e.tensor_scalar(out=nega[:], in0=nega[:], scalar1=-1.0, scalar2=None, op0=ALU.mult),
                             reads=["nega"], writes=["nega"])
                        for cc in range(16):
                            c, hf = divmod(cc, 2)
                            sl = cc % 2
                            S.dma("sp", lambda e, c=c, sl=sl, hf=hf: e.dma_start(out=stage[sl][:], in_=w_in[l, c * 128:(c + 1) * 128, hf * 1688:(hf + 1) * 1688]),
                                  writes=[("stage", sl)])
                            S.op("dve" if cc % 2 == 0 else "pool",
                                 lambda e, c=c, sl=sl, hf=hf: e.tensor_scalar(out=w_in_sb[:, c, hf * 1688:(hf + 1) * 1688], in0=stage[sl][:], scalar1=normw[:, c:c + 1],
                                                                       scalar2=None, op0=ALU.mult),
                                 reads=[("stage", sl), "normw"], writes=["w_in_sb"])
                        for c in range(8):
                            sl = c % 2
                            S.dma("sp", lambda e, c=c, sl=sl: e.dma_start(out=stage[sl][:, 0:D], in_=w_out[l, c * 128:(c + 1) * 128, :]),
                                  writes=[("stage", sl)])
                            S.op("act", lambda e, c=c, sl=sl: e.copy(out=w_out_sb[:, c, :], in_=stage[sl][:, 0:D]),
                                 reads=[("stage", sl)], writes=["w_out_sb"])
                        wkv = hT
                        for c in range(8):
                            sl = c % 2
                            S.dma("sp", lambda e, c=c, sl=sl: e.dma_start(out=stage[sl][:, 0:512], in_=w_kv[l, c * 128:(c + 1) * 128, :]),
                                  writes=[("stage", sl)])
                            S.op("dve", lambda e, c=c, sl=sl: e.tensor_scalar(out=wkv[:, c, :], in0=stage[sl][:, 0:512],
                                                                              scalar1=memnormw[:, c:c + 1], scalar2=None, op0=ALU.mult),
                                 reads=[("stage", sl), "memnormw"], writes=[("hT", 0)])
                        mT = raw_o[:, 0:4, :]
                        memT = sbp("memT", [128, 8, MEM], BF16)
                        for mt_ in range(2):
                            token_tile_to_hT(mem_in, mt_ * 128, mt_, memT, "memT", mt_ * 128)
                        for h in range(4):
                            pm = PB[1][0:64, 0:MEM]
                            for c in range(8):
                                S.op("pe", lambda e, c=c, h=h, pm=pm: e.matmul(pm, lhsT=wkv[:, c, h * 64:(h + 1) * 64], rhs=memT[:, c, :],
                                                                               start=(c == 0), stop=(c == 7)),
                                     reads=[("hT", 0), "memT"], writes=["pb1"])
                            S.op("act", lambda e, h=h, pm=pm: e.copy(out=mkT[:, h, :], in_=pm), reads=["pb1"], writes=["mkT"])
                        for mt_ in range(2):
                            pm = PB[1][:, 0:256]
                            for c in range(8):
                                S.op("pe", lambda e, c=c, mt_=mt_, pm=pm: e.matmul(pm, lhsT=memT[:, c, mt_ * 128:(mt_ + 1) * 128],
                                                                                   rhs=wkv[:, c, 256:512], start=(c == 0), stop=(c == 7)),
                                     reads=[("hT", 0), "memT"], writes=["pb1"])
                            S.op("act", lambda e, mt_=mt_, pm=pm: e.copy(out=mv[:, mt_, :], in_=pm), reads=["pb1"], writes=["mv"])

                    mark(f'L{l} p0 done')
                    if 1 in passes:
                        def do_mt(mt, hT, hk):
                            for sub in range(4):
                                tt = mt * 4 + sub
                                r0 = tt * 128
                                sl = tt % 2
                                if tt + 1 < T // 128:
                                    prep_tile(tt + 1)
                                hsl = lambda c: hT[:, c, sub * 128:(sub + 1) * 128]
                                groups = ((1, 256, 512), (2, 768, 512), (3, 2464, 400), (4, 3120, 256))
                                for (bk, c0, wd) in groups:
                                    for c in range(8):
                                        S.op("pe", lambda e, c=c, bk=bk, c0=c0, wd=wd, sub=sub: e.matmul(
                                            PB[bk][:, 0:wd], lhsT=hT[:, c, sub * 128:(sub + 1) * 128], rhs=w_in_sb[:, c, c0:c0 + wd],
                                            start=(c == 0), stop=(c == 7)), reads=[hk, "w_in_sb"], writes=[f"pb{bk}"])
                                S.op("act", lambda e, sl=sl: e.copy(out=kvo[sl][:, 0:512], in_=PB[1][:, 0:512]), reads=["pb1"],
                                     writes=[("kvo", sl)])
                                S.op("dve", lambda e, sl=sl: e.tensor_copy(out=kvo[sl][:, 512:640], in_=PB[2][:, 0:128]), reads=["pb2"],
                                     writes=[("kvo", sl)])
                                S.dma("pool", lambda e, sl=sl, r0=r0: e.dma_start(out=KV[r0:r0 + 128, :], in_=kvo[sl][:]),
                                      reads=[("kvo", sl)], writes=["KV"])
                                S.op("act", lambda e, sl=sl: e.activation(out=gate_o[sl][:, 0:384], in_=PB[2][:, 128:512], func=AF.Silu),
                                     reads=["pb2"], writes=[("gateo", sl)])
                                S.op("act", lambda e, sl=sl: e.activation(out=gate_o[sl][:, 384:768], in_=PB[3][:, 0:384], func=AF.Silu),
                                     reads=["pb3"], writes=[("gateo", sl)])
                                S.op("act", lambda e, sl=sl: e.activation(out=gate_o[sl][:, 768:1024], in_=PB[4][:, 0:256], func=AF.Silu),
                                     reads=["pb4"], writes=[("gateo", sl)])
                                S.dma("pool", lambda e, sl=sl, r0=r0: e.dma_start(out=GATE[r0:r0 + 128, :], in_=gate_o[sl][:]),
                                      reads=[("gateo", sl)], writes=["GATE"])
                                S.op("act", lambda e, sl=sl: e.activation(out=gb_o[sl][:, 0:8], in_=PB[3][:, 384:392], func=AF.Sigmoid),
                                     reads=["pb3"], writes=[("gbo", sl)])
                                S.op("dve", lambda e: e.tensor_tensor(out=gtmp[:], in0=PB[3][:, 392:400], in1=dtb[:], op=ALU.add),
                                     reads=["pb3", "dtb"], writes=["gtmp"])
                                S.op("act", lambda e: e.activation(out=gtmp[:], in_=gtmp[:], func=AF.Exp), reads=["gtmp"], writes=["gtmp"])
                                S.op("act", lambda e: e.activation(out=gtmp[:], in_=gtmp[:], func=AF.Ln, bias=1.0), reads=["gtmp"],
                                     writes=["gtmp"])
                                S.op("dve", lambda e, sl=sl: e.tensor_tensor(out=gb_o[sl][:, 8:16], in0=gtmp[:], in1=nega[:], op=ALU.mult),
                                     reads=["gtmp", "nega"], writes=[("gbo", sl)])
                                S.dma("pool", lambda e, sl=sl, r0=r0: e.dma_start(out=GB[r0:r0 + 128, :], in_=gb_o[sl][:]),
                                      reads=[("gbo", sl)], writes=["GB"])
                            t0 = mt * 512
                            fmi = 0
                            for m in range(4):
                                bk = 5 + (m % 2)
                                for c in range(8):
                                    S.op("pe", lambda e, c=c, m=m, bk=bk: e.matmul(PB[bk][:, :], lhsT=w_in_sb[:, c, m * 128:(m + 1) * 128],
                                                                                   rhs=hT[:, c, :], start=(c == 0), stop=(c == 7)),
                                         reads=[hk, "w_in_sb"], writes=[f"pb{bk}"])
                                fs = fmi % 2
                                fmi += 1
                                S.op("act", lambda e, bk=bk, fs=fs, m=m: e.activation(out=fm_o[fs][:], in_=PB[bk][:, :], func=AF.Copy,
                                                                                      scale=(0.125 if m < 2 else 1.0)),
                                     reads=[f"pb{bk}"], writes=[("fmo", fs)])
                                S.dma("pool", lambda e, fs=fs, m=m, t0=t0: e.dma_start(out=QKT[m * 128:(m + 1) * 128, t0:t0 + 512],
                                                                                       in_=fm_o[fs][:]),
                                      reads=[("fmo", fs)], writes=["QKT"])
                            for z in range(2):
                                for c in range(8):
                                    S.op("pe", lambda e, c=c, z=z: e.matmul(PB[7][0:16, :], lhsT=w_in_sb[:, c, 1280 + z * 16:1296 + z * 16],
                                                                            rhs=hT[:, c, :], start=(c == 0), stop=(c == 7)),
                                         reads=[hk, "w_in_sb"], writes=["pb7"])
                                S.op("act", lambda e, z=z: e.copy(out=lrT[:, z, :], in_=PB[7][0:16, :]), reads=["pb7"], writes=["lrT"])
                            def gate_logits(sub):
                                tt = mt * 4 + sub
                                r0 = tt * 128
                                sl = tt % 2
                                for z in range(2):
                                    S.op("pe", lambda e, z=z, sub=sub: e.matmul(PB[7][:, z * 256:(z + 1) * 256],
                                                                                lhsT=lrT[:, z, sub * 128:(sub + 1) * 128], rhs=w2[:, z, :],
                                                                                start=True, stop=True),
                                         reads=["lrT", "w2"], writes=["pb7"])
                                S.op("dve", lambda e, sl=sl: e.tensor_tensor(out=la_o[sl][:], in0=PB[7][:, :], in1=biasb[:], op=ALU.add),
                                     reads=["pb7", "biasb"], writes=[("lao", sl)])
                                S.op("act", lambda e, sl=sl: e.activation(out=la_o[sl][:], in_=la_o[sl][:], func=AF.Exp, scale=-1.0),
                                     reads=[("lao", sl)], writes=[("lao", sl)])
                                S.op("act", lambda e, sl=sl: e.activation(out=la_o[sl][:], in_=la_o[sl][:], func=AF.Ln, bias=1.0),
                                     reads=[("lao", sl)], writes=[("lao", sl)])
                                S.op("dve", lambda e, sl=sl: e.tensor_scalar(out=la_o[sl][:], in0=la_o[sl][:], scalar1=-1.0 / 16.0,
                                                                             scalar2=None, op0=ALU.mult),
                                     reads=[("lao", sl)], writes=[("lao", sl)])
                                S.dma("pool", lambda e, sl=sl, r0=r0: e.dma_start(out=LA[r0:r0 + 128, :], in_=la_o[sl][:]),
                                      reads=[("lao", sl)], writes=["LA"])
                            for jj in range(12):
                                bk = 5 + (jj % 2)
                                for c in range(8):
                                    S.op("pe", lambda e, c=c, jj=jj, bk=bk: e.matmul(PB[bk][0:96, :],
                                                                                     lhsT=w_in_sb[:, c, 1312 + jj * 96:1312 + (jj + 1) * 96],
                                                                                     rhs=hT[:, c, :], start=(c == 0), stop=(c == 7)),
                                         reads=[hk, "w_in_sb"], writes=[f"pb{bk}"])
                                S.op("act" if jj % 2 == 0 else "dve",
                                     (lambda e, jj=jj, bk=bk: e.copy(out=raw_o[:, jj, :], in_=PB[bk][0:96, :])) if jj % 2 == 0 else
                                     (lambda e, jj=jj, bk=bk: e.tensor_copy(out=raw_o[:, jj, :], in_=PB[bk][0:96, :])),
                                     reads=[f"pb{bk}"], writes=["raw_o"])
                                if jj % 3 == 2:
                                    gate_logits(jj // 3)
                            S.dma("pool", lambda e, t0=t0: e.dma_start(out=RAWT[:, :, t0:t0 + 512].rearrange("j p c -> p j c"), in_=raw_o[:]),
                                  reads=["raw_o"], writes=["RAWT"])
                            for h in range(4):
                                bk = 5 + (h % 2)
                                for c in range(8):
                                    S.op("pe", lambda e, c=c, h=h, bk=bk: e.matmul(PB[bk][0:64, :],
                                                                                   lhsT=w_in_sb[:, c, 2864 + h * 64:2864 + (h + 1) * 64],
                                                                                   rhs=hT[:, c, :], start=(c == 0), stop=(c == 7)),
                                         reads=[hk, "w_in_sb"], writes=[f"pb{bk}"])
                                S.op("act", lambda e, h=h, bk=bk: e.copy(out=q3T[:, h, :], in_=PB[bk][0:64, :]), reads=[f"pb{bk}"],
                                     writes=["q3T"])
                            def xa_sub(sub):
                                tt = mt * 4 + sub
                                r0 = tt * 128
                                sl = tt % 2
                                xb = 5 if sl == 0 else 1
                                for h in range(4):
                                    bk = xb + (h // 2)
                                    S.op("pe", lambda e, h=h, bk=bk, sub=sub: e.matmul(PB[bk][:, (h % 2) * 256:(h % 2 + 1) * 256],
                                                                                       lhsT=q3T[:, h, sub * 128:(sub + 1) * 128], rhs=mkT[:, h, :],
                                                                                       start=True, stop=True),
                                         reads=["q3T", "mkT"], writes=[f"pb{bk}"])
                                yield
                                for hp in range(2):
                                    S.op("dve", lambda e, hp=hp: e.tensor_reduce(out=mx[sl][:, hp * 2:hp * 2 + 2],
                                                                                 in_=PB[xb + hp][:, :].rearrange("p (h m) -> p h m", h=2),
                                                                                 axis=AX.X, op=ALU.max),
                                         reads=[f"pb{xb + hp}"], writes=[("mx", sl)])
                                S.op("dve", lambda e: e.tensor_scalar(out=nb[sl][:], in0=mx[sl][:], scalar1=-0.125, scalar2=None, op0=ALU.mult),
                                     reads=[("mx", sl)], writes=[("nb", sl)])
                                for h in range(4):
                                    bk = xb + (h // 2)
                                    S.op("act", lambda e, h=h, bk=bk: e.activation(out=pexp[sl][:, h, :], in_=PB[bk][:, (h % 2) * 256:(h % 2 + 1) * 256],
                                                                                   func=AF.Exp, scale=0.125, bias=nb[sl][:, h:h + 1],
                                                                                   accum_out=rs[sl][:, h:h + 1]),
                                         reads=[f"pb{bk}", ("nb", sl)], writes=[("pexp", sl), ("rs", sl)])
                                yield
                                ptp = PB[xb + 2][:, :].bitcast(BF16).rearrange("p (c t) -> p c t", c=8)
                                for h in range(4):
                                    for mb in range(2):
                                        S.op("pe", lambda e, h=h, mb=mb: e.transpose(out=ptp[:, h * 2 + mb, :],
                                                                                     in_=pexp[sl][:, h, mb * 128:(mb + 1) * 128], identity=ident_bf[:]),
                                             reads=[("pexp", sl), "c_idbf"], writes=[f"pb{xb + 2}"])
                                S.op("dve", lambda e: e.tensor_copy(out=pT[sl][:], in_=ptp), reads=[f"pb{xb + 2}"], writes=[("pT", sl)])
                                yield
                                po = PB[xb][:, 0:256].rearrange("p (h d) -> p h d", h=4)
                                for h in range(4):
                                    for mb in range(2):
                                        S.op("pe", lambda e, h=h, mb=mb: e.matmul(po[:, h, :], lhsT=pT[sl][:, h * 2 + mb, :],
                                                                                  rhs=mv[:, mb, h * 64:(h + 1) * 64], start=(mb == 0), stop=(mb == 1)),
                                             reads=[("pT", sl), "mv"], writes=[f"pb{xb}"])
                                yield
                                S.op("dve", lambda e: e.reciprocal(out=rs[sl][:], in_=rs[sl][:]), reads=[("rs", sl)], writes=[("rs", sl)])
                                S.op("dve", lambda e: e.tensor_tensor(out=o3[sl][:], in0=po, in1=rs[sl][:].unsqueeze(2).broadcast_to([128, 4, 64]),
                                                                      op=ALU.mult), reads=[f"pb{xb}", ("rs", sl)], writes=[("o3", sl)])
                                S.op("pool", lambda e: e.tensor_tensor(out=o3sq[sl][:], in0=o3[sl][:], in1=o3[sl][:], op=ALU.mult), reads=[("o3", sl)],
                                     writes=[("o3sq", sl)])
                                S.op("dve", lambda e: e.tensor_reduce(out=o3ss[sl][:], in_=o3sq[sl][:], axis=AX.X, op=ALU.add), reads=[("o3sq", sl)],
                                     writes=[("o3ss", sl)])
                                yield
                                rms_rstd(o3ss[sl][:], o3ss[sl][:], 64, ("o3ss", sl), ("o3ss", sl))
                                S.op("dve", lambda e: e.tensor_tensor(out=o3[sl][:], in0=o3[sl][:], in1=o3ss[sl][:].unsqueeze(2).broadcast_to([128, 4, 64]),
                                                                      op=ALU.mult), reads=[("o3", sl), ("o3ss", sl)], writes=[("o3", sl)])
                                S.op("pool", lambda e: e.tensor_tensor(out=o3[sl][:], in0=o3[sl][:], in1=xanorm[:].rearrange("p (h d) -> p h d", h=4),
                                                                       op=ALU.mult), reads=[("o3", sl), "xanorm"], writes=[("o3", sl)])
                                S.dma("sp", lambda e, sl=sl, r0=r0: e.dma_start(out=oc3_o[sl][:], in_=GATE[r0:r0 + 128, 768:1024]),
                                      reads=["GATE"], writes=[("oc3o", sl)])
                                S.op("dve", lambda e, sl=sl: e.tensor_tensor(out=oc3_o[sl][:], in0=o3[sl][:].rearrange("p h d -> p (h d)"),
                                                                             in1=oc3_o[sl][:], op=ALU.mult),
                                     reads=[("o3", sl), ("oc3o", sl)], writes=[("oc3o", sl)])
                                S.dma("pool", lambda e, sl=sl, r0=r0: e.dma_start(out=OC3[r0:r0 + 128, :], in_=oc3_o[sl][:]),
                                      reads=[("oc3o", sl)], writes=["OC3"])
                            xa_act = []
                            xa_n = 0
                            while True:
                                while len(xa_act) < 2 and xa_n < 4:
                                    xa_act.append(xa_sub(xa_n))
                                    xa_n += 1
                                if not xa_act:
                                    break
                                for g_ in list(xa_act):
                                    if next(g_, "done") == "done":
                                        xa_act.remove(g_)

                        def prep_tile(tt_):
                            mt__, sub__ = divmod(tt_, 4)
                            token_tile_to_hT(xsrc, tt_ * 128, tt_ % 2, hTs[mt__ % 2], ("hT", mt__ % 2), sub__ * 128)
                        prep_tile(0)
                        for mt_ in range(T // 512):
                            do_mt(mt_, hTs[mt_ % 2], ("hT", mt_ % 2))
                    S.barrier()

            mark(f'L{l} p1 done')
            if 2 in passes:
                with contextlib.ExitStack() as ps:
                    sbp = lambda n, s, dt: ps.enter_context(nc.sbuf_tensor(f"sbB{l}_" + n, s, dt))
                    rawh = [sbp(f"rawh{i}", [96, 12, 516], BF16) for i in range(2)]
                    dg = sbp("dg", [96, 12, 5, 96], BF16)
                    for jj in range(12):
                        S.op("dve" if jj % 2 == 0 else "pool", lambda e, jj=jj: e.tensor_tensor(
                            out=dg[:, jj, :, :], in0=ident_bf[0:96, 0:96].unsqueeze(1).broadcast_to([96, 5, 96]),
                            in1=convw[:, jj, :].unsqueeze(2).broadcast_to([96, 5, 96]), op=ALU.mult),
                            reads=["c_idbf", "convw"], writes=["dg"])
                    sil = [sbp(f"sil{i}", [96, 512], F32) for i in range(8)]
                    sqb = [sbp(f"sqb{i}", [96, 512], BF16) for i in range(8)]
                    rsd = [sbp(f"rsd{i}", [96, 512], F32) for i in range(2)]
                    c2_o = [sbp(f"c2o{i}", [96, 12, 512], BF16) for i in range(2)]
                    for mt in range(T // 512):
                        t0 = mt * 512
                        rs_ = mt % 2
                        lo = max(t0 - 2, 0)
                        hi = min(t0 + 514, T)
                        rk = ("rawh", rs_)
                        if mt == 0:
                            S.op("pool", lambda e, rs_=rs_: e.memset(rawh[rs_][:, :, 0:2], 0.0), writes=[rk])
                        if mt == T // 512 - 1:
                            S.op("pool", lambda e, rs_=rs_: e.memset(rawh[rs_][:, :, 514:516], 0.0), writes=[rk])
                        S.dma("sp", lambda e, rs_=rs_, lo=lo, hi=hi, t0=t0: e.dma_start(
                            out=rawh[rs_][:, :, lo - (t0 - 2):hi - (t0 - 2)], in_=RAWT[:, :, lo:hi].rearrange("j p c -> p j c")),
                            reads=["RAWT"], writes=[rk])
                        for jj in range(12):
                            a_ = jj % 4
                            ak = f"pb{1 + a_}"
                            accp = PB[1 + a_][0:96, :]
                            for k in range(5):
                                S.op("pe", lambda e, jj=jj, a_=a_, rs_=rs_, k=k: e.matmul(PB[1 + a_][0:96, :], lhsT=dg[:, jj, k, :],
                                                                                         rhs=rawh[rs_][:, jj, k:k + 512], start=(k == 0), stop=(k == 4)),
                                     reads=[rk, "dg"], writes=[ak])
                            if jj >= 8:
                                S.op("act", lambda e, jj=jj, a_=a_, rs_=rs_: e.activation(out=c2_o[rs_][:, jj, :], in_=PB[1 + a_][0:96, :], func=AF.Silu),
                                     reads=[ak], writes=[("c2o", rs_)])
                            else:
                                s_ = jj
                                S.op("act", lambda e, a_=a_, s_=s_: e.activation(out=sil[s_][:], in_=PB[1 + a_][0:96, :], func=AF.Silu),
                                     reads=[ak], writes=[("sil", s_)])
                                S.op("pool", lambda e, s_=s_: e.tensor_tensor(out=sqb[s_][:], in0=sil[s_][:], in1=sil[s_][:], op=ALU.mult),
                                     reads=[("sil", s_)], writes=[("sqb", s_)])
                        for jj in range(8):
                            s_ = jj
                            r_ = jj % 2
                            bk = (5, 6, 7, 0)[jj % 4]
                            S.op("pe", lambda e, s_=s_, bk=bk: e.matmul(PB[bk][0:96, :], lhsT=ones_bf[:], rhs=sqb[s_][:], start=True, stop=True),
                                 reads=[("sqb", s_), "c_onesbf"], writes=[f"pb{bk}"])
                            S.op("act", lambda e, r_=r_, bk=bk: e.activation(out=rsd[r_][:], in_=PB[bk][0:96, :], func=AF.Ln, bias=EPS),
                                 reads=[f"pb{bk}"], writes=[("rsd", r_)])
                            S.op("act", lambda e, r_=r_: e.activation(out=rsd[r_][:], in_=rsd[r_][:], func=AF.Exp, scale=-0.5),
                                 reads=[("rsd", r_)], writes=[("rsd", r_)])
                            qs = (96.0 ** -0.5) if jj < 4 else 1.0
                            S.op("dve", lambda e, s_=s_, r_=r_, jj=jj, qs=qs, rs_=rs_: e.scalar_tensor_tensor(
                                out=c2_o[rs_][:, jj, :], in0=sil[s_][:], scalar=qs, in1=rsd[r_][:], op0=ALU.mult, op1=ALU.mult),
                                reads=[("sil", s_), ("rsd", r_)], writes=[("c2o", rs_)])
                        S.dma("pool", lambda e, rs_=rs_, t0=t0: e.dma_start(out=C2T[:, :, t0:t0 + 512].rearrange("j p c -> p j c"),
                                                                          in_=c2_o[rs_][:]), reads=[("c2o", rs_)], writes=["C2T"])
                    S.barrier()

            mark(f'L{l} p2 done')
            if 3 in passes:
                with contextlib.ExitStack() as ps:
                    sbp = lambda n, s, dt: ps.enter_context(nc.sbuf_tensor(f"sbC{l}_" + n, s, dt))
                    two = range(2)
                    glaq = [[sbp(f"glaq{d}{i}", [64, 4, 256], BF16) for i in two] for d in two]
                    glak = [[sbp(f"glak{d}{i}", [64, 4, 256], BF16) for i in two] for d in two]
                    glakv = [[sbp(f"glakv{d}{i}", [64, 4, 640], BF16) for i in two] for d in two]
                    glala = [[sbp(f"glala{d}{i}", [64, 4, 256], F32) for i in two] for d in two]
                    gdnc2 = [[sbp(f"gdnc2{d}{i}", [96, 12, 256], BF16) for i in two] for d in two]
                    gdngb = [[sbp(f"gdngb{d}{i}", [64, 4, 16], F32) for i in two] for d in two]
                    ebT = [sbp(f"ebT{d}", [64, 4, 64], F32) for d in two]
                    enbT = [sbp(f"enbT{d}", [64, 4, 64], F32) for d in two]
                    erem = [sbp(f"erem{d}", [64, 256], F32) for d in two]
                    qeT = [sbp(f"qeT{d}", [64, 4, 64], BF16) for d in two]
                    keT = [sbp(f"keT{d}", [64, 4, 64], BF16) for d in two]
                    ktl = [sbp(f"ktl{d}", [64, 256], BF16) for d in two]
                    ATm = [sbp(f"ATm{d}", [64, 4, 64], BF16) for d in two]
                    gS32 = [sbp(f"gS32{d}", [64, 4, 96], F32) for d in two]
                    gSbf = [sbp(f"gSbf{d}", [64, 4, 96], BF16) for d in two]
                    go_sb = [sbp(f"go_sb{d}", [64, 384], F32) for d in two]
                    ebT8 = sbp("ebT8", [64, 8, 64], F32)
                    enbT8 = sbp("enbT8", [64, 8, 64], F32)
                    erem8 = sbp("erem8", [64, 512], F32)
                    qeT8 = sbp("qeT8", [64, 8, 64], BF16)
                    keT8 = sbp("keT8", [64, 8, 64], BF16)
                    ktl8 = sbp("ktl8", [64, 512], BF16)
                    ATm8 = sbp("ATm8", [64, 8, 64], BF16)
                    gS328 = sbp("gS328", [64, 8, 96], F32)
                    gSbf8 = sbp("gSbf8", [64, 8, 96], BF16)
                    S.op("pool", lambda e: e.memset(gS328[:], 0.0), writes=["gS328"])
                    S.op("pool", lambda e: e.memset(gSbf8[:], 0.0), writes=["gSbf8"])

                    gb8 = sbp("gb8", [64, 16], F32)
                    K8 = sbp("K8", [64, 8, 96], BF16)
                    V8 = sbp("V8", [64, 8, 96], BF16)
                    eg16 = sbp("eg16", [64, 16], F32)
                    dec96 = sbp("dec96m", [96, 8], F32)
                    R8 = sbp("R8", [64, 8, 64], F32)
                    kkN = sbp("kkN", [64, 8, 64], F32)
                    kkM = sbp("kkM", [64, 8, 64], F32)
                    qkI = sbp("qkI", [64, 8, 64], F32)
                    decT = sbp("decT", [64, 8, 64], F32)
                    decD = sbp("decD", [64, 8, 64], F32)
                    EGB8 = sbp("EGB8", [96, 8, 64], F32)
                    DB8 = sbp("DB8", [64, 8, 64], BF16)
                    Nm = [sbp(f"Nm{i}", [64, 8, 64], NDT) for i in two]
                    Mm = [sbp(f"Mm{i}", [64, 8, 64], NDT) for i in two]
                    qkm8 = sbp("qkm8", [96, 8, 64], BF16)
                    Pm8 = sbp("Pm8", [64, 8, 64], NDT)
                    TiT8 = sbp("TiT8", [64, 8, 64], BF16)
                    vb8 = sbp("vb8", [64, 8, 96], BF16)
                    bg8 = sbp("bg8", [64, 8], F32)
                    kbg8 = sbp("kbg8", [64, 8, 96], BF16)
                    kte8 = sbp("kte8", [64, 8, 96], BF16)
                    qdT8 = sbp("qdT8", [96, 8, 64], BF16)
                    u8 = sbp("u8", [64, 8, 96], F32)
                    wT8 = sbp("wT8", [96, 8, 64], BF16)
                    vnew8 = sbp("vnew8", [96, 8, 96], BF16)
                    dS328 = sbp("dS328", [96, 8, 96], F32)
                    dSbf8 = sbp("dSbf8", [96, 8, 96], BF16)
                    do_sb = [sbp(f"do_sb{d}", [64, 384], F32) for d in two]
                    S.op("pool", lambda e: e.memset(dS328[:], 0.0), writes=["dS328"])
                    S.op("pool", lambda e: e.memset(qkm8[:], 0.0), writes=["qkm8"])
                    S.op("pool", lambda e: e.memset(vnew8[:], 0.0), writes=["vnew8"])
                    S.op("pool", lambda e: e.memset(dSbf8[:], 0.0), writes=["dSbf8"])

                    for d in two:
                        S.op("pool", lambda e, d=d: e.memset(gS32[d][:], 0.0), writes=[("gS32", d)])
                        S.op("pool", lambda e, d=d: e.memset(gSbf[d][:], 0.0), writes=[("gSbf", d)])

                    def load_group(d, grp):
                        gs = grp % 2
                        g0 = grp * 256
                        S.dma("sp", lambda e: e.dma_start(out=glaq[d][gs][:], in_=QKT[0:256, g0:g0 + 256].rearrange("(h p) c -> p h c", p=64)),
                              reads=["QKT"], writes=[("glaq", d, gs)])
                        S.dma("sp", lambda e: e.dma_start(out=glak[d][gs][:], in_=QKT[256:512, g0:g0 + 256].rearrange("(h p) c -> p h c", p=64)),
                              reads=["QKT"], writes=[("glak", d, gs)])
                        S.dma("sp", lambda e: e.dma_start(out=glakv[d][gs][:], in_=KV[g0:g0 + 256, :].rearrange("(n p) c -> p n c", p=64)),
                              reads=["KV"], writes=[("glakv", d, gs)])
                        S.dma("sp", lambda e: e.dma_start(out=glala[d][gs][:], in_=LA[g0:g0 + 256, d * 256:(d + 1) * 256].rearrange("(n p) c -> p n c", p=64)),
                              reads=["LA"], writes=[("glala", d, gs)])
                        S.dma("sp", lambda e: e.dma_start(out=gdnc2[d][gs][:], in_=C2T[:, :, g0:g0 + 256].rearrange("j p c -> p j c")),
                              reads=["C2T"], writes=[("gdnc2", d, gs)])
                        S.dma("sp", lambda e: e.dma_start(out=gdngb[d][gs][:], in_=GB[g0:g0 + 256, :].rearrange("(n p) c -> p n c", p=64)),
                              reads=["GB"], writes=[("gdngb", d, gs)])

                    def gla_chunk(d, n):
                        grp, ci = divmod(n, 4)
                        gs = grp % 2
                        qg, kg, kvg, lag = glaq[d][gs], glak[d][gs], glakv[d][gs], glala[d][gs]
                        kq_, kk_, kkv_, kla_ = ("glaq", d, gs), ("glak", d, gs), ("glakv", d, gs), ("glala", d, gs)
                        cs = slice(ci * 64, (ci + 1) * 64)
                        mI = LE if d == 0 else GE
                        mS = GT if d == 0 else LT
                        bT_ps = PB[0][0:64, 0:256].rearrange("p (h c) -> p h c", h=4)
                        rem_ps = PB[0][0:64, 256:512]
                        import os as _os
                        STG = int(_os.environ.get('K_STAGE', 99))
                        for h in range(4):
                            S.op("pe", lambda e, h=h: e.matmul(bT_ps[:, h, :], lhsT=lag[:, ci, h * 64:(h + 1) * 64], rhs=masks[:, mI, 0, :],
                                                               start=True, stop=True), reads=[kla_, "c_masks"], writes=["pb0"])
                        S.op("pe", lambda e: e.matmul(rem_ps, lhsT=masks[:, mS, 0, :], rhs=lag[:, ci, :], start=True, stop=True),
                             reads=[kla_, "c_masks"], writes=["pb0"])
                        yield
                        S.op("act", lambda e: e.activation(out=ebT[d][:], in_=bT_ps, func=AF.Exp), reads=["pb0"], writes=[("ebT", d)])
                        S.op("act", lambda e: e.activation(out=enbT[d][:], in_=bT_ps, func=AF.Exp, scale=-1.0), reads=["pb0"], writes=[("enbT", d)])
                        S.op("act", lambda e: e.activation(out=erem[d][:], in_=rem_ps, func=AF.Exp), reads=["pb0"], writes=[("erem", d)])
                        yield
                        S.op("dve", lambda e: e.tensor_tensor(out=qeT[d][:], in0=qg[:, :, cs], in1=ebT[d][:], op=ALU.mult),
                             reads=[kq_, ("ebT", d)], writes=[("qeT", d)])
                        S.op("pool", lambda e: e.tensor_tensor(out=keT[d][:], in0=kg[:, :, cs], in1=enbT[d][:], op=ALU.mult),
                             reads=[kk_, ("enbT", d)], writes=[("keT", d)])
                        S.op("dve", lambda e: e.tensor_tensor(out=ktl[d][:], in0=kvg[:, ci, 0:256], in1=erem[d][:], op=ALU.mult),
                             reads=[kkv_, ("erem", d)], writes=[("ktl", d)])
                        yield
                        AT_ps = PB[1][0:64, 0:256].rearrange("p (h c) -> p h c", h=4)
                        for h in range(4):
                            S.op("pe", lambda e, h=h: e.matmul(AT_ps[:, h, :], lhsT=keT[d][:, h, :], rhs=qeT[d][:, h, :], start=True, stop=True),
                                 reads=[("keT", d), ("qeT", d)], writes=["pb1"])
                        yield
                        S.op("dve", lambda e: e.tensor_tensor(out=ATm[d][:], in0=AT_ps, in1=masks[:, mI, :, :], op=ALU.mult),
                             reads=["pb1", "c_masks"], writes=[("ATm", d)])
                        yield
                        o_ps = PB[0][0:64, 0:384].rearrange("p (h v) -> p h v", h=4)
                        for h in range(4):
                            S.op("pe", lambda e, h=h: e.matmul(o_ps[:, h, :], lhsT=ATm[d][:, h, :], rhs=kvg[:, ci, 256 + h * 96:256 + (h + 1) * 96],
                                                               start=True, stop=False), reads=[("ATm", d), kkv_], writes=["pb0"])
                            S.op("pe", lambda e, h=h: e.matmul(o_ps[:, h, :], lhsT=qeT[d][:, h, :], rhs=gSbf[d][:, h, :], start=False, stop=True),
                                 reads=[("qeT", d), ("gSbf", d)], writes=["pb0"])
                        S.op("act", lambda e: e.copy(out=go_sb[d][:], in_=PB[0][0:64, 0:384]), reads=["pb0"], writes=[("go_sb", d)])
                        S.dma("pool", lambda e: e.dma_start(out=OFB[d][n * 64:(n + 1) * 64, 0:384], in_=go_sb[d][:]),
                              reads=[("go_sb", d)], writes=[("OFB", d)])
                        yield
                        kv_ps = PB[1][0:64, 0:384].rearrange("p (h v) -> p h v", h=4)
                        for h in range(4):
                            S.op("pe", lambda e, h=h: e.matmul(kv_ps[:, h, :], lhsT=ktl[d][:, h * 64:(h + 1) * 64],
                                                               rhs=kvg[:, ci, 256 + h * 96:256 + (h + 1) * 96], start=True, stop=True),
                                 reads=[("ktl", d), kkv_], writes=["pb1"])
                        dcol = 63 if d == 0 else 0
                        S.op("pool", lambda e: e.tensor_tensor(out=gS32[d][:], in0=gS32[d][:],
                                                               in1=ebT[d][:, :, dcol:dcol + 1].broadcast_to([64, 4, 96]), op=ALU.mult),
                             reads=[("gS32", d), ("ebT", d)], writes=[("gS32", d)])
                        S.op("dve", lambda e: e.tensor_tensor(out=gS32[d][:], in0=gS32[d][:], in1=kv_ps, op=ALU.add),
                             reads=[("gS32", d), "pb1"], writes=[("gS32", d)])
                        S.op("act", lambda e: e.copy(out=gSbf[d][:], in_=gS32[d][:]), reads=[("gS32", d)], writes=[("gSbf", d)])

                    def gla_step(ns):
                        two_ = range(2)
                        info = []
                        for d in two_:
                            grp, ci = divmod(ns[d], 4)
                            gs = grp % 2
                            info.append((glaq[d][gs], glak[d][gs], glakv[d][gs], glala[d][gs], ("glaq", d, gs), ("glak", d, gs), ("glakv", d, gs),
                                         ("glala", d, gs), slice(ci * 64, (ci + 1) * 64), ci))
                        mIl, mSl = (LE, GE), (GT, LT)
                        mI2 = masks5[:, 0:2, :, :].rearrange("p t r c -> p (t r) c")
                        v64 = lambda t_: t_[0:64, :].rearrange("p (u c) -> p u c", u=8)
                        bT_ps = v64(PB[0])
                        for d in two_:
                            lag, kla_, ci = info[d][3], info[d][7], info[d][9]
                            for h in range(4):
                                u = d * 4 + h
                                S.op("pe", lambda e, u=u, h=h, lag=lag, ci=ci, d=d: e.matmul(bT_ps[:, u, :], lhsT=lag[:, ci, h * 64:(h + 1) * 64],
                                                                                              rhs=masks5[:, mIl[d], 0, :], start=True, stop=True),
                                     reads=[kla_, "c_masks"], writes=["pb0"])
                            S.op("pe", lambda e, lag=lag, ci=ci, d=d: e.matmul(PB[1][0:64, d * 256:(d + 1) * 256], lhsT=masks5[:, mSl[d], 0, :], rhs=lag[:, ci, :],
                                                                              start=True, stop=True), reads=[kla_, "c_masks"], writes=["pb1"])
                        S.op("act", lambda e: e.activation(out=ebT8[:], in_=bT_ps, func=AF.Exp), reads=["pb0"], writes=["ebT8"])
                        S.op("act", lambda e: e.activation(out=enbT8[:], in_=bT_ps, func=AF.Exp, scale=-1.0), reads=["pb0"], writes=["enbT8"])
                        S.op("act", lambda e: e.activation(out=erem8[:], in_=PB[1][0:64, :], func=AF.Exp), reads=["pb1"], writes=["erem8"])
                        yield
                        for d in two_:
                            qg, kg, kvg, kq_, kk_, kkv_, cs, ci = info[d][0], info[d][1], info[d][2], info[d][4], info[d][5], info[d][6], info[d][8], info[d][9]
                            S.op("dve", lambda e, d=d, qg=qg, cs=cs: e.tensor_tensor(out=qeT8[:, d * 4:(d + 1) * 4, :], in0=qg[:, :, cs],
                                                                                     in1=ebT8[:, d * 4:(d + 1) * 4, :], op=ALU.mult),
                                 reads=[kq_, "ebT8"], writes=["qeT8"])
                            S.op("pool", lambda e, d=d, kg=kg, cs=cs: e.tensor_tensor(out=keT8[:, d * 4:(d + 1) * 4, :], in0=kg[:, :, cs],
                                                                                      in1=enbT8[:, d * 4:(d + 1) * 4, :], op=ALU.mult),
                                 reads=[kk_, "enbT8"], writes=["keT8"])
                            S.op("dve", lambda e, d=d, kvg=kvg, ci=ci: e.tensor_tensor(out=ktl8[:, d * 256:(d + 1) * 256], in0=kvg[:, ci, 0:256],
                                                                                       in1=erem8[:, d * 256:(d + 1) * 256], op=ALU.mult),
                                 reads=[kkv_, "erem8"], writes=["ktl8"])
                        yield
                        AT_ps = v64(PB[0])
                        for u in range(8):
                            S.op("pe", lambda e, u=u: e.matmul(AT_ps[:, u, :], lhsT=keT8[:, u, :], rhs=qeT8[:, u, :], start=True, stop=True),
                                 reads=["keT8", "qeT8"], writes=["pb0"])
                        S.op("dve", lambda e: e.tensor_tensor(out=ATm8[:], in0=AT_ps, in1=mI2, op=ALU.mult), reads=["pb0", "c_masks"], writes=["ATm8"])
                        yield
                        for d in two_:
                            kvg, kkv_, ci = info[d][2], info[d][6], info[d][9]
                            o_ps = PB[d][0:64, 0:384].rearrange("p (h v) -> p h v", h=4)
                            for h in range(4):
                                u = d * 4 + h
                                S.op("pe", lambda e, u=u, h=h, kvg=kvg, ci=ci, o_ps=o_ps: e.matmul(o_ps[:, h, :], lhsT=ATm8[:, u, :],
                                                                                                 rhs=kvg[:, ci, 256 + h * 96:256 + (h + 1) * 96],
                                                                                                 start=True, stop=False),
                                     reads=["ATm8", kkv_], writes=[f"pb{d}"])
                                S.op("pe", lambda e, u=u, h=h, o_ps=o_ps: e.matmul(o_ps[:, h, :], lhsT=qeT8[:, u, :], rhs=gSbf8[:, u, :], start=False, stop=True),
                                     reads=["qeT8", "gSbf8"], writes=[f"pb{d}"])
                            S.op("act", lambda e, d=d: e.copy(out=go_sb[d][:], in_=PB[d][0:64, 0:384]), reads=[f"pb{d}"], writes=[("go_sb", d)])
                            S.dma("pool", lambda e, d=d: e.dma_start(out=OFB[d][ns[d] * 64:(ns[d] + 1) * 64, 0:384], in_=go_sb[d][:]),
                                  reads=[("go_sb", d)], writes=[("OFB", d)])
                        yield
                        for d in two_:
                            dcol = 63 if d == 0 else 0
                            S.op("pool", lambda e, d=d, dcol=dcol: e.tensor_tensor(out=gS328[:, d * 4:(d + 1) * 4, :], in0=gS328[:, d * 4:(d + 1) * 4, :],
                                                                                   in1=ebT8[:, d * 4:(d + 1) * 4, dcol:dcol + 1].broadcast_to([64, 4, 96]), op=ALU.mult),
                                 reads=["gS328", "ebT8"], writes=["gS328"])
                        for d in two_:
                            kvg, kkv_, ci = info[d][2], info[d][6], info[d][9]
                            kv_ps = PB[d][0:64, 0:384].rearrange("p (h v) -> p h v", h=4)
                            for h in range(4):
                                S.op("pe", lambda e, d=d, h=h, kvg=kvg, ci=ci, kv_ps=kv_ps: e.matmul(kv_ps[:, h, :], lhsT=ktl8[:, d * 256 + h * 64:d * 256 + (h + 1) * 64],
                                                                                                   rhs=kvg[:, ci, 256 + h * 96:256 + (h + 1) * 96], start=True, stop=True),
                                     reads=["ktl8", kkv_], writes=[f"pb{d}"])
                            S.op("dve", lambda e, d=d, kv_ps=kv_ps: e.tensor_tensor(out=gS328[:, d * 4:(d + 1) * 4, :], in0=gS328[:, d * 4:(d + 1) * 4, :], in1=kv_ps,
                                                                                    op=ALU.add), reads=["gS328", f"pb{d}"], writes=["gS328"])
                        S.op("act", lambda e: e.copy(out=gSbf8[:], in_=gS328[:]), reads=["gS328"], writes=["gSbf8"])
                        yield


                    def gdn_step(ns):
                        import os as _os
                        two_ = range(2)
                        info = []
                        for d in two_:
                            grp, ci = divmod(ns[d], 4)
                            gs = grp % 2
                            info.append((gdnc2[d][gs], gdngb[d][gs], ("gdnc2", d, gs), ("gdngb", d, gs), slice(ci * 64, (ci + 1) * 64), ci))
                        m8 = lambda a, b_: masks5[:, a:b_, :, :].rearrange("p t r c -> p (t r) c")
                        mI2, mS2, mSo2 = m8(0, 2), m8(2, 4), m8(3, 5)
                        mIl, mSl = (LE, GE), (GT, LT)
                        b8 = lambda ap, w, p=64: ap.unsqueeze(2).broadcast_to([p, 8, w])
                        idb8 = ident_f[:].unsqueeze(1).broadcast_to([64, 8, 64])
                        v64 = lambda t_: t_[0:64, :].rearrange("p (u c) -> p u c", u=8)
                        for d in two_:
                            gbg, kgb, ci = info[d][1], info[d][3], info[d][5]
                            S.op("pool", lambda e, d=d, gbg=gbg, ci=ci: e.tensor_copy(
                                out=gb8[:].rearrange("p (t z h) -> p t z h", t=2, z=2)[:, :, d, :],
                                in_=gbg[:, ci, :].rearrange("p (t z h) -> p t z h", t=2, z=2)[:, :, d, :]), reads=[kgb], writes=["gb8"])
                        beta8, g8 = gb8[:, 0:8], gb8[:, 8:16]
                        for d in two_:
                            c2g, kc2, cs = info[d][0], info[d][2], info[d][4]
                            tr_ps = PB[2 + d][0:64, 0:384].bitcast(BF16).rearrange("p (u x) -> p u x", u=8)
                            for u in range(8):
                                S.op("pe", lambda e, u=u, c2g=c2g, cs=cs, tr_ps=tr_ps: e.transpose(out=tr_ps[:, u, :], in_=c2g[:, 4 + u, cs],
                                                                                                 identity=ident_bf[0:96, 0:96]),
                                     reads=[kc2, "c_idbf"], writes=[f"pb{2 + d}"])
                            S.op("act", lambda e, d=d, tr_ps=tr_ps: e.copy(out=K8[:, d * 4:(d + 1) * 4, :], in_=tr_ps[:, 0:4, :]), reads=[f"pb{2 + d}"], writes=["K8"])
                            S.op("act", lambda e, d=d, tr_ps=tr_ps: e.copy(out=V8[:, d * 4:(d + 1) * 4, :], in_=tr_ps[:, 4:8, :]), reads=[f"pb{2 + d}"], writes=["V8"])
                        yield
                        for d in two_:
                            S.op("pe", lambda e, d=d: e.matmul(PB[3][0:64, 384 + d * 4:388 + d * 4], lhsT=masks5[:, mIl[d], 0, :], rhs=g8[:, d * 4:(d + 1) * 4],
                                                               start=True, stop=True), reads=["c_masks", "gb8"], writes=["pb3"])
                            S.op("pe", lambda e, d=d: e.matmul(PB[3][0:64, 392 + d * 4:396 + d * 4], lhsT=masks5[:, mSl[d], 0, :], rhs=g8[:, d * 4:(d + 1) * 4],
                                                               start=True, stop=True), reads=["c_masks", "gb8"], writes=["pb3"])
                        S.op("pe", lambda e: e.matmul(PB[3][0:96, 400:408], lhsT=ones_f[:, :], rhs=g8, start=True, stop=True),
                             reads=["c_onesf", "gb8"], writes=["pb3"])
                        S.op("act", lambda e: e.activation(out=eg16[:], in_=PB[3][0:64, 384:400], func=AF.Exp), reads=["pb3"], writes=["eg16"])
                        S.op("act", lambda e: e.activation(out=dec96[:], in_=PB[3][0:96, 400:408], func=AF.Exp), reads=["pb3"], writes=["dec96"])
                        yield
                        kk_ps, qk_ps = v64(PB[4]), v64(PB[5])
                        for d in two_:
                            c2g, kc2, cs = info[d][0], info[d][2], info[d][4]
                            for h in range(4):
                                u = d * 4 + h
                                S.op("pe", lambda e, u=u, h=h, c2g=c2g, cs=cs: e.matmul(kk_ps[:, u, :], lhsT=c2g[:, 4 + h, cs], rhs=c2g[:, 4 + h, cs],
                                                                                        start=True, stop=True), reads=[kc2], writes=["pb4"])
                                S.op("pe", lambda e, u=u, h=h, c2g=c2g, cs=cs: e.matmul(qk_ps[:, u, :], lhsT=c2g[:, 4 + h, cs], rhs=c2g[:, h, cs],
                                                                                        start=True, stop=True), reads=[kc2], writes=["pb5"])
                        yield
                        S.op("dve", lambda e: e.tensor_tensor(out=R8[:], in0=mI2, in1=b8(g8, 64), op=ALU.mult), reads=["c_masks", "gb8"], writes=["R8"])
                        S.op("dve", lambda e: e.tensor_tensor(out=kkN[:], in0=kk_ps, in1=mS2, op=ALU.mult), reads=["pb4", "c_masks"], writes=["kkN"])
                        S.op("dve", lambda e: e.tensor_tensor(out=kkM[:], in0=kk_ps, in1=mSo2, op=ALU.mult), reads=["pb4", "c_masks"], writes=["kkM"])
                        S.op("dve", lambda e: e.tensor_tensor(out=qkI[:], in0=qk_ps, in1=mI2, op=ALU.mult), reads=["pb5", "c_masks"], writes=["qkI"])
                        Rflat = R8[:].rearrange("p u c -> p (u c)")
                        yield
                        for d in two_:
                            S.op("pe", lambda e, d=d: e.matmul(PB[2][0:64, d * 256:(d + 1) * 256], lhsT=masks5[:, mSl[d], 0, :], rhs=Rflat[:, d * 256:(d + 1) * 256],
                                                               start=True, stop=True), reads=["c_masks", "R8"], writes=["pb2"])
                        for u in range(8):
                            S.op("pe", lambda e, u=u: e.matmul(PB[3][0:64, u * 64:(u + 1) * 64], lhsT=R8[:, u, :], rhs=masks5[:, mSl[u // 4], 0, :],
                                                               start=True, stop=True), reads=["c_masks", "R8"], writes=["pb3"])
                        S.op("act", lambda e: e.activation(out=decT[:], in_=v64(PB[2]), func=AF.Exp), reads=["pb2"], writes=["decT"])
                        S.op("act", lambda e: e.activation(out=decD[:], in_=v64(PB[3]), func=AF.Exp), reads=["pb3"], writes=["decD"])
                        yield
                        S.op("pe", lambda e: e.matmul(PB[6][0:96, :], lhsT=ones_f[:, :], rhs=Rflat, start=True, stop=True),
                             reads=["c_onesf", "R8"], writes=["pb6"])
                        S.op("act", lambda e: e.activation(out=EGB8[:], in_=PB[6][0:96, :].rearrange("p (u c) -> p u c", u=8), func=AF.Exp),
                             reads=["pb6"], writes=["EGB8"])
                        yield
                        S.op("pool", lambda e: e.tensor_tensor(out=DB8[:], in0=idb8, in1=b8(beta8, 64), op=ALU.mult), reads=["c_idf", "gb8"], writes=["DB8"])
                        S.op("pe", lambda e: e.matmul(PB[7][0:64, :], lhsT=ones_bf[0:64, 0:64], rhs=DB8[:].rearrange("p u c -> p (u c)"), start=True, stop=True),
                             reads=["c_onesbf", "DB8"], writes=["pb7"])
                        yield
                        S.op("dve", lambda e: e.tensor_tensor(out=kkN[:], in0=kkN[:], in1=decD[:], op=ALU.mult), reads=["kkN", "decD"], writes=["kkN"])
                        S.op("pool", lambda e: e.tensor_tensor(out=Nm[0][:], in0=kkN[:], in1=b8(beta8, 64), op=ALU.mult), reads=["kkN", "gb8"], writes=[("Nm", 0)])
                        S.op("dve", lambda e: e.tensor_tensor(out=kkM[:], in0=kkM[:], in1=decT[:], op=ALU.mult), reads=["kkM", "decT"], writes=["kkM"])
                        S.op("dve", lambda e: e.tensor_tensor(out=Mm[0][:], in0=kkM[:], in1=v64(PB[7]), op=ALU.mult), reads=["kkM", "pb7"], writes=[("Mm", 0)])
                        S.op("pool", lambda e: e.tensor_tensor(out=Pm8[:], in0=idb8, in1=Mm[0][:], op=ALU.subtract), reads=[("Mm", 0), "c_idf"], writes=["Pm8"])
                        S.op("pool", lambda e: e.tensor_tensor(out=qkm8[0:64, :, :], in0=qkI[:], in1=decT[:], op=ALU.mult), reads=["qkI", "decT"], writes=["qkm8"])
                        yield
                        S.op("pool", lambda e: e.tensor_tensor(out=vb8[:], in0=V8[:], in1=b8(beta8, 96), op=ALU.mult), reads=["V8", "gb8"], writes=["vb8"])
                        S.op("dve", lambda e: e.tensor_tensor(out=bg8[:], in0=beta8, in1=eg16[:, 0:8], op=ALU.mult), reads=["gb8", "eg16"], writes=["bg8"])
                        S.op("pool", lambda e: e.tensor_tensor(out=kbg8[:], in0=K8[:], in1=b8(bg8[:], 96), op=ALU.mult), reads=["K8", "bg8"], writes=["kbg8"])
                        S.op("pool", lambda e: e.tensor_tensor(out=kte8[:], in0=K8[:], in1=b8(eg16[:, 8:16], 96), op=ALU.mult), reads=["K8", "eg16"], writes=["kte8"])
                        for d in two_:
                            c2g, kc2, cs = info[d][0], info[d][2], info[d][4]
                            S.op("pool", lambda e, d=d, c2g=c2g, cs=cs: e.tensor_tensor(out=qdT8[:, d * 4:(d + 1) * 4, :], in0=c2g[:, 0:4, cs],
                                                                                       in1=EGB8[:, d * 4:(d + 1) * 4, :], op=ALU.mult),
                                 reads=[kc2, "EGB8"], writes=["qdT8"])
                        yield
                        sqM, sqN, pd = v64(PB[4]), v64(PB[5]), v64(PB[6])

                        def emit_sq(lev, cur, nxt):
                            lastlev = (lev == 4)
                            for u in range(8):
                                if not lastlev:
                                    S.op("pe", lambda e, u=u: e.matmul(sqM[:, u, :], lhsT=Nm[cur][:, u, :], rhs=Mm[cur][:, u, :], start=True, stop=True),
                                         reads=[("Nm", cur), ("Mm", cur)], writes=["pb4"])
                                S.op("pe", lambda e, u=u: e.matmul(sqN[:, u, :], lhsT=Mm[cur][:, u, :], rhs=Nm[cur][:, u, :], start=True, stop=True),
                                     reads=[("Nm", cur), ("Mm", cur)], writes=["pb5"])
                            S.op("act", lambda e: e.copy(out=Nm[nxt][:], in_=sqN), reads=["pb5"], writes=[("Nm", nxt)])
                            if not lastlev:
                                S.op("dve", lambda e: e.tensor_copy(out=Mm[nxt][:], in_=sqM), reads=["pb4"], writes=[("Mm", nxt)])

                        def emit_pd(lev, nb_):
                            for u in range(8):
                                S.op("pe", lambda e, u=u: e.matmul(pd[:, u, :], lhsT=Nm[nb_][:, u, :], rhs=Pm8[:, u, :], start=True, stop=True),
                                     reads=[("Nm", nb_), "Pm8"], writes=["pb6"])
                            if lev < 4:
                                S.op("dve", lambda e: e.tensor_tensor(out=Pm8[:], in0=Pm8[:], in1=pd, op=ALU.add), reads=["Pm8", "pb6"], writes=["Pm8"])
                            else:
                                S.op("dve", lambda e: e.tensor_tensor(out=TiT8[:], in0=Pm8[:], in1=pd, op=ALU.add), reads=["Pm8", "pb6"], writes=["TiT8"])

                        emit_sq(0, 0, 1)
                        yield
                        for lev in range(5):
                            if lev < 4:
                                emit_sq(lev + 1, (lev + 1) % 2, (lev + 2) % 2)
                            emit_pd(lev, (lev + 1) % 2)
                            yield

                        wT_ps = PB[7][0:96, :].rearrange("p (u c) -> p u c", u=8)
                        for d in two_:
                            u_ps = PB[2 + d][0:64, 0:384].rearrange("p (h v) -> p h v", h=4)
                            for h in range(4):
                                u = d * 4 + h
                                S.op("pe", lambda e, u=u, h=h, u_ps=u_ps: e.matmul(u_ps[:, h, :], lhsT=TiT8[:, u, :], rhs=vb8[:, u, :], start=True, stop=True),
                                     reads=["TiT8", "vb8"], writes=[f"pb{2 + d}"])
                            S.op("act", lambda e, d=d, u_ps=u_ps: e.copy(out=u8[:, d * 4:(d + 1) * 4, :], in_=u_ps), reads=[f"pb{2 + d}"], writes=["u8"])
                        for u in range(8):
                            S.op("pe", lambda e, u=u: e.matmul(wT_ps[:, u, :], lhsT=kbg8[:, u, :], rhs=TiT8[:, u, :], start=True, stop=True),
                                 reads=["TiT8", "kbg8"], writes=["pb7"])
                        S.op("act", lambda e: e.copy(out=wT8[:], in_=wT_ps), reads=["pb7"], writes=["wT8"])
                        yield
                        for d in two_:
                            ws_ps = PB[4 + d][0:64, 0:384].rearrange("p (h v) -> p h v", h=4)
                            for h in range(4):
                                u = d * 4 + h
                                S.op("pe", lambda e, u=u, h=h, ws_ps=ws_ps: e.matmul(ws_ps[:, h, :], lhsT=wT8[:, u, :], rhs=dSbf8[:, u, :], start=True, stop=True),
                                     reads=["wT8", "dSbf8"], writes=[f"pb{4 + d}"])
                            S.op("dve", lambda e, d=d, ws_ps=ws_ps: e.tensor_tensor(out=vnew8[0:64, d * 4:(d + 1) * 4, :], in0=u8[:, d * 4:(d + 1) * 4, :], in1=ws_ps,
                                                                                    op=ALU.subtract), reads=["u8", f"pb{4 + d}"], writes=["vnew8"])
                        yield
                        for d in two_:
                            o_ps = PB[2 + d][0:64, 0:384].rearrange("p (h v) -> p h v", h=4)
                            for h in range(4):
                                u = d * 4 + h
                                S.op("pe", lambda e, u=u, h=h, o_ps=o_ps: e.matmul(o_ps[:, h, :], lhsT=qdT8[:, u, :], rhs=dSbf8[:, u, :], start=True, stop=False),
                                     reads=["qdT8", "dSbf8"], writes=[f"pb{2 + d}"])
                                S.op("pe", lambda e, u=u, h=h, o_ps=o_ps: e.matmul(o_ps[:, h, :], lhsT=qkm8[:, u, :], rhs=vnew8[:, u, :], start=False, stop=True),
                                     reads=["qkm8", "vnew8"], writes=[f"pb{2 + d}"])
                            S.op("act", lambda e, d=d: e.copy(out=do_sb[d][:], in_=PB[2 + d][0:64, 0:384]), reads=[f"pb{2 + d}"], writes=[("do_sb", d)])
                            S.dma("pool", lambda e, d=d: e.dma_start(out=OFB[d][ns[d] * 64:(ns[d] + 1) * 64, 384:768], in_=do_sb[d][:]),
                                  reads=[("do_sb", d)], writes=[("OFB", d)])
                        S.op("pool", lambda e: e.tensor_tensor(out=dS328[:], in0=dS328[:], in1=dec96[:].unsqueeze(2).broadcast_to([96, 8, 96]), op=ALU.mult),
                             reads=["dS328", "dec96"], writes=["dS328"])
                        for d in two_:
                            kv_ps = PB[6 + d][0:96, 0:384].rearrange("p (h v) -> p h v", h=4)
                            for h in range(4):
                                u = d * 4 + h
                                S.op("pe", lambda e, u=u, h=h, kv_ps=kv_ps: e.matmul(kv_ps[:, h, :], lhsT=kte8[:, u, :], rhs=vnew8[0:64, u, :], start=True, stop=True),
                                     reads=["kte8", "vnew8"], writes=[f"pb{6 + d}"])
                            S.op("dve", lambda e, d=d, kv_ps=kv_ps: e.tensor_tensor(out=dS328[:, d * 4:(d + 1) * 4, :], in0=dS328[:, d * 4:(d + 1) * 4, :], in1=kv_ps,
                                                                                    op=ALU.add), reads=["dS328", f"pb{6 + d}"], writes=["dS328"])
                        S.op("act", lambda e: e.copy(out=dSbf8[:], in_=dS328[:]), reads=["dS328"], writes=["dSbf8"])


                    load_group(0, 0)
                    load_group(1, 15)
                    import os as _os
                    for j in range(int(_os.environ.get('K_NSTEPS', NCH))):
                        nf, nbk = j, NCH - 1 - j
                        if j % 4 == 0 and j + 4 < NCH:
                            load_group(0, j // 4 + 1)
                            load_group(1, 15 - (j // 4 + 1))
                        fillers = [gla_step((nf, nbk))] if "gla" in parts else []
                        main = gdn_step((nf, nbk)) if "gdn" in parts else iter(())
                        fi = 0
                        if not _osm.environ.get('K_ILV'):
                            for _ in range(int(_osm.environ.get('K_PRE', 0))):
                                next(main, "done")
                            for f_ in fillers:
                                for _ in f_:
                                    pass
                            fillers = []
                        while True:
                            main_alive = next(main, "done") != "done"
                            adv = False
                            while fi < len(fillers):
                                if next(fillers[fi], "done") != "done":
                                    adv = True
                                    break
                                fi += 1
                            if not main_alive and not adv:
                                break
                    S.barrier()

            mark(f'L{l} p3 done')
            if 4 in passes:
                with contextlib.ExitStack() as ps:
                    sbp = lambda n, s, dt: ps.enter_context(nc.sbuf_tensor(f"sbD{l}_" + n, s, dt))
                    two = range(2)
                    of_t = [sbp(f"of_t{i}", [128, 768], F32) for i in two]
                    ob_t = [sbp(f"ob_t{i}", [128, 768], F32) for i in two]
                    gt_t = [sbp(f"gt_t{i}", [128, 768], BF16) for i in two]
                    oc_all = [sbp(f"oc_all{i}", [128, 1024], BF16) for i in two]
                    x_t = [sbp(f"x_t{i}", [128, D], F32) for i in two]
                    xn = [sbp(f"xn{i}", [128, D], F32) for i in two]
                    osq = [sbp(f"osq{i}", [128, 768], F32) for i in two]
                    ss8 = [sbp(f"ss8{i}", [128, 8], F32) for i in two]
                    ocT = [sbp(f"ocT{i}", [128, 8, 128], BF16) for i in two]
                    junk4 = [sbp(f"junk4{i}", [128, D], BF16) for i in two]
                    ss4 = [sbp(f"ss4{i}", [128, 1], F32) for i in two]
                    def p4_tile(tt):
                        r0 = tt * 128
                        sl = tt % 2
                        pbase = 3 * (tt % 2)
                        S.dma("sp", lambda e, sl=sl, r0=r0: e.dma_start(out=of_t[sl][:], in_=OFB[0][r0:r0 + 128, :]), reads=[("OFB", 0)], writes=[("of_t", sl)])
                        S.dma("sp", lambda e, sl=sl, r0=r0: e.dma_start(out=ob_t[sl][:], in_=OFB[1][r0:r0 + 128, :]), reads=[("OFB", 1)], writes=[("ob_t", sl)])
                        S.dma("sp", lambda e, sl=sl, r0=r0: e.dma_start(out=gt_t[sl][:], in_=GATE[r0:r0 + 128, 0:768]), reads=["GATE"], writes=[("gt_t", sl)])
                        S.dma("sp", lambda e, sl=sl, r0=r0: e.dma_start(out=oc_all[sl][:, 768:1024], in_=OC3[r0:r0 + 128, :]), reads=["OC3"], writes=[("oc_all", sl)])
                        S.dma("sp", lambda e, sl=sl, r0=r0: e.dma_start(out=x_t[sl][:], in_=xsrc[r0:r0 + 128, :]), reads=["XSRC%d" % l], writes=[("x_t", sl)])
                        yield
                        S.op("dve", lambda e, sl=sl: e.tensor_tensor(out=of_t[sl][:], in0=of_t[sl][:], in1=ob_t[sl][:], op=ALU.add),
                             reads=[("of_t", sl), ("ob_t", sl)], writes=[("of_t", sl)])
                        S.op("pool", lambda e, sl=sl: e.tensor_tensor(out=osq[sl][:], in0=of_t[sl][:], in1=of_t[sl][:], op=ALU.mult),
                             reads=[("of_t", sl)], writes=[("osq", sl)])
                        S.op("dve", lambda e, sl=sl: e.tensor_reduce(out=ss8[sl][:], in_=osq[sl][:].rearrange("p (h v) -> p h v", h=8), axis=AX.X, op=ALU.add),
                             reads=[("osq", sl)], writes=[("ss8", sl)])
                        yield
                        rms_rstd(ss8[sl][:], ss8[sl][:], 96, ("ss8", sl), ("ss8", sl))
                        yield
                        S.op("dve", lambda e, sl=sl: e.tensor_tensor(out=of_t[sl][:].rearrange("p (h v) -> p h v", h=8),
                                                                     in0=of_t[sl][:].rearrange("p (h v) -> p h v", h=8),
                                                                     in1=ss8[sl][:].unsqueeze(2).broadcast_to([128, 8, 96]), op=ALU.mult),
                             reads=[("of_t", sl), ("ss8", sl)], writes=[("of_t", sl)])
                        S.op("pool", lambda e, sl=sl: e.tensor_tensor(out=of_t[sl][:], in0=of_t[sl][:], in1=ohnorm[:], op=ALU.mult),
                             reads=[("of_t", sl), "ohnorm"], writes=[("of_t", sl)])
                        S.op("dve", lambda e, sl=sl: e.tensor_tensor(out=oc_all[sl][:, 0:768], in0=of_t[sl][:], in1=gt_t[sl][:], op=ALU.mult),
                             reads=[("of_t", sl), ("gt_t", sl)], writes=[("oc_all", sl)])
                        yield
                        ptr = PB[pbase][:, 0:512].bitcast(BF16).rearrange("p (c t) -> p c t", c=8)
                        for c in range(8):
                            S.op("pe", lambda e, c=c, sl=sl: e.transpose(out=ptr[:, c, :], in_=oc_all[sl][:, c * 128:(c + 1) * 128], identity=ident_bf[:]),
                                 reads=[("oc_all", sl), "c_idbf"], writes=[f"pb{pbase}"])
                        S.op("act", lambda e, sl=sl: e.copy(out=ocT[sl][:], in_=ptr), reads=[f"pb{pbase}"], writes=[("ocT", sl)])
                        yield
                        for hf in range(2):
                            bk = pbase + 1 + hf
                            for c in range(8):
                                S.op("pe", lambda e, c=c, hf=hf, bk=bk, sl=sl: e.matmul(PB[bk][:, :], lhsT=ocT[sl][:, c, :], rhs=w_out_sb[:, c, hf * 512:(hf + 1) * 512],
                                                                                 start=(c == 0), stop=(c == 7)), reads=[("ocT", sl), "w_out_sb"], writes=[f"pb{bk}"])
                            S.op("dve", lambda e, hf=hf, bk=bk, sl=sl: e.tensor_tensor(out=xn[sl][:, hf * 512:(hf + 1) * 512], in0=x_t[sl][:, hf * 512:(hf + 1) * 512],
                                                                                       in1=PB[bk][:, :], op=ALU.add),
                                 reads=[("x_t", sl), f"pb{bk}"], writes=[("xn", sl)])
                        yield
                        if not last:
                            S.dma("pool", lambda e, sl=sl, r0=r0: e.dma_start(out=X1[r0:r0 + 128, :], in_=xn[sl][:]), reads=[("xn", sl)], writes=["XSRC%d" % (l + 1)])
                        else:
                            S.op("act", lambda e, sl=sl: e.activation(out=junk4[sl][:], in_=xn[sl][:], func=AF.Square, accum_out=ss4[sl][:]),
                                 reads=[("xn", sl)], writes=[("junk4", sl), ("ss4", sl)])
                            rms_rstd(ss4[sl][:], ss4[sl][:], D, ("ss4", sl), ("ss4", sl))
                            S.op("dve", lambda e, sl=sl: e.tensor_scalar(out=xn[sl][:], in0=xn[sl][:], scalar1=ss4[sl][:], scalar2=None, op0=ALU.mult),
                                 reads=[("xn", sl), ("ss4", sl)], writes=[("xn", sl)])
                            S.op("pool", lambda e, sl=sl: e.tensor_tensor(out=xn[sl][:], in0=xn[sl][:], in1=fnorm[:], op=ALU.mult),
                                 reads=[("xn", sl), "c_fnorm"], writes=[("xn", sl)])
                            S.dma("pool", lambda e, sl=sl, r0=r0: e.dma_start(out=out[r0:r0 + 128, :], in_=xn[sl][:]), reads=[("xn", sl)], writes=["out"])
                    active = []
                    nxt_tt = 0
                    while True:
                        while len(active) < int(_osm.environ.get('K_P4W', 2)) and nxt_tt < T // 128:
                            active.append(p4_tile(nxt_tt))
                            nxt_tt += 1
                        if not active:
                            break
                        for g_ in list(active):
                            if next(g_, "done") == "done":
                                active.remove(g_)
                    S.barrier()

        for l_ in range(nlayers):
            emit_layer(l_)
        S.barrier()
        S.finish()
        mark("end")
        print("MARKS", marks)
        print("ops", S.n_ops, "waits", S.n_waits, {e: len(S.ops[e]) for e in S.ENGS})
    return nc

def host_inputs(inputs, b):
    f32 = np.float32
    g = lambda k: np.asarray(inputs[k], dtype=f32)
    d = {}
    d["x"] = np.ascontiguousarray(g("x")[b])
    d["mem"] = np.ascontiguousarray(g("mem")[b])
    d["w_in"] = g("w_in")
    d["w_out"] = g("w_out")
    d["xa_w_kv"] = g("xa_w_kv")
    d["normw_pc"] = np.ascontiguousarray(g("norm_w").reshape(NL, 8, 128).transpose(0, 2, 1))
    d["memnormw_pc"] = np.ascontiguousarray(g("mem_norm_w").reshape(NL, 8, 128).transpose(0, 2, 1))
    d["w2"] = np.ascontiguousarray(g("gla_w2").transpose(0, 2, 1, 3))
    d["bias_bc"] = np.ascontiguousarray(np.broadcast_to(g("gla_b").reshape(NL, 1, 512), (NL, 128, 512)))
    ohn = np.concatenate([np.tile(g("gla_norm_w"), (1, 4)), np.tile(g("gdn_norm_w"), (1, 4))], axis=1)
    d["ohnorm_bc"] = np.ascontiguousarray(np.broadcast_to(ohn.reshape(NL, 1, 768), (NL, 128, 768)))
    d["xanorm_bc"] = np.ascontiguousarray(np.broadcast_to(np.tile(g("xa_norm_w"), (1, 4)).reshape(NL, 1, 256), (NL, 128, 256)))
    d["convw"] = np.ascontiguousarray(g("gdn_conv_w").reshape(NL, 12, 96, 5).transpose(0, 2, 1, 3))
    d["alog_bc"] = np.ascontiguousarray(np.broadcast_to(g("gdn_a_log").reshape(NL, 1, 8), (NL, 128, 8)))
    d["dtb_bc"] = np.ascontiguousarray(np.broadcast_to(g("gdn_dt_bias").reshape(NL, 1, 8), (NL, 128, 8)))
    d["fnorm_bc"] = np.ascontiguousarray(np.broadcast_to(g("final_norm_w").reshape(1, D), (128, D)))
    d["ident_bf"] = np.eye(128, dtype=f32).astype(ml_dtypes.bfloat16)
    d["ident_f"] = np.eye(64, dtype=f32)
    p = np.arange(64)[:, None]
    q = np.arange(64)[None, :]
    m4 = np.stack([(p <= q), (p >= q), (p > q), (p < q), (p > q)]).astype(f32)
    d["masks"] = np.ascontiguousarray(np.broadcast_to(m4.transpose(1, 0, 2)[:, :, None, :], (64, 5, 4, 64)))
    d["ones_f"] = np.ones((64, 96), f32)
    d["ones_bf"] = np.ones((96, 96), f32).astype(ml_dtypes.bfloat16)
    return d


_NC_CACHE = {}


def kernel(**inputs):
    if "nc" not in _NC_CACHE:
        _NC_CACHE["nc"] = build()
    nc = _NC_CACHE["nc"]
    in_maps = [host_inputs(inputs, b) for b in range(8)]
    res = run_bass_kernel_spmd(nc, in_maps, core_ids=list(range(8)))
    return np.stack([np.asarray(r["out"], dtype=np.float32) for r in res.results], axis=0)
```

```python
import contextlib
import numpy as np
import ml_dtypes
import concourse.bass as bass
import concourse.mybir as mybir
from concourse.bass_utils import run_bass_kernel_spmd

F32 = mybir.dt.float32
BF16 = mybir.dt.bfloat16
AF = mybir.ActivationFunctionType
ALU = mybir.AluOpType
AX = mybir.AxisListType

T = 4096
D = 1024
INW = 3376
NL = 2
MEM = 256
EPS = 1e-6
NCH = T // 64
import os as _osm
NDT = BF16 if _osm.environ.get('K_NDT', 'bf16') == 'bf16' else F32

EPOCH = 30000
N_DMA_SEMS = 32


class Sched:
    ENGS = ("pe", "act", "dve", "pool", "sp")

    def __init__(self, nc, stack):
        self.nc = nc
        self.stack = stack
        self.ops = {e: [] for e in self.ENGS}
        self.cnt = {e: 0 for e in self.ENGS}
        self.epoch = {e: 0 for e in self.ENGS}
        self.sems = {}
        for e in self.ENGS:
            self._new_eng_sem(e)
        self.dma_sems = []
        for i in range(2 * N_DMA_SEMS):
            s = stack.enter_context(nc.semaphore(f"dma{i}"))
            self.dma_sems.append([s, 0])
        self.dma_rr = {"sp": 0, "pool": 0}
        self.known = {e: {} for e in self.ENGS}
        self.last_w = {}
        self.readers = {}
        self.n_waits = 0
        self.n_ops = 0

    def _new_eng_sem(self, e):
        s = self.stack.enter_context(self.nc.semaphore(f"s_{e}_{self.epoch[e]}"))
        self.sems[(e, self.epoch[e])] = s

    def _deps(self, reads, writes):
        deps = []
        for b in reads:
            if b in self.last_w:
                deps.append(self.last_w[b])
        for b in writes:
            if b in self.last_w:
                deps.append(self.last_w[b])
            deps.extend(self.readers.get(b, {}).items())
        return deps

    def _emit_waits(self, eng, deps):
        need = {}
        for (sk, val) in deps:
            if sk[0] == eng and eng == "pe":
                continue
            if self.known[eng].get(sk, 0) >= val:
                continue
            if need.get(sk, 0) < val:
                need[sk] = val
        waits = []
        for sk, val in need.items():
            self.known[eng][sk] = val
            sem = self.sems[sk] if sk[0] in self.ENGS else self.dma_sems[sk[1]][0]
            waits.append((sem, val))
        return waits

    def _record(self, ev, reads, writes):
        for b in writes:
            self.last_w[b] = ev
            self.readers[b] = {}
        for b in reads:
            if b not in writes:
                d = self.readers.setdefault(b, {})
                if d.get(ev[0], 0) < ev[1]:
                    d[ev[0]] = ev[1]

    def op(self, eng, fn, reads=(), writes=()):
        waits = self._emit_waits(eng, self._deps(reads, writes))
        if self.cnt[eng] >= EPOCH:
            self.epoch[eng] += 1
            self.cnt[eng] = 0
            self._new_eng_sem(eng)
        self.cnt[eng] += 1
        sk = (eng, self.epoch[eng])
        sem = self.sems[sk]
        self.n_waits += len(waits)
        self.n_ops += 1

        def run(e, waits=waits, fn=fn, sem=sem):
            for (s, v) in waits:
                e.wait_ge(s, v)
            fn(e).then_inc(sem, 1)

        self.ops[eng].append(run)
        ev = (sk, self.cnt[eng])
        self._record(ev, reads, writes)
        return ev

    def dma(self, eng, fn, reads=(), writes=()):
        i = self.dma_rr[eng] + (0 if eng == "sp" else N_DMA_SEMS)
        self.dma_rr[eng] = (self.dma_rr[eng] + 1) % N_DMA_SEMS
        sem, cur = self.dma_sems[i]
        sk = ("dma", i)
        deps = self._deps(reads, writes)
        if cur > 0:
            deps.append((sk, cur))
        waits = self._emit_waits(eng, deps)
        val = cur + 16
        self.dma_sems[i][1] = val
        self.n_waits += len(waits)
        self.n_ops += 1

        def run(e, waits=waits, fn=fn, sem=sem):
            for (s, v) in waits:
                e.wait_ge(s, v)
            fn(e).then_inc(sem, 16)

        self.ops[eng].append(run)
        ev = (sk, val)
        self._record(ev, reads, writes)
        return ev

    def barrier(self):
        evs = [((e, self.epoch[e]), self.cnt[e]) for e in self.ENGS if self.cnt[e] > 0]
        evs += [(("dma", i), v) for i, (s, v) in enumerate(self.dma_sems) if v > 0]
        for eng in self.ENGS:
            waits = self._emit_waits(eng, evs)

            def run(e, waits=waits):
                for (s, v) in waits:
                    e.wait_ge(s, v)

            self.ops[eng].append(run)

    def wait_all(self, eng, bufs):
        deps = [self.last_w[b] for b in bufs if b in self.last_w]
        waits = self._emit_waits(eng, deps)

        def run(e, waits=waits):
            for (s, v) in waits:
                e.wait_ge(s, v)

        self.ops[eng].append(run)

    def finish(self):
        with self.nc.Block() as block:
            @block.tensor
            def _(e):
                for f in self.ops["pe"]:
                    f(e)

            @block.scalar
            def _(e):
                for f in self.ops["act"]:
                    f(e)

            @block.vector
            def _(e):
                for f in self.ops["dve"]:
                    f(e)

            @block.gpsimd
            def _(e):
                for f in self.ops["pool"]:
                    f(e)

            @block.sync
            def _(e):
                for f in self.ops["sp"]:
                    f(e)


def build(nlayers=NL, debug=False, passes=(0, 1, 2, 3, 4), parts=("gla", "gdn")):
    nc = bass.Bass("TRN2", target_bir_lowering=False)
    dram = lambda n, s, dt, k="ExternalInput": nc.dram_tensor(n, s, dt, kind=k).ap()
    SCR = "ExternalOutput" if debug else "Internal"
    x_in = dram("x", [T, D], F32)
    mem_in = dram("mem", [MEM, D], F32)
    w_in = dram("w_in", [NL, D, INW], F32)
    w_out = dram("w_out", [NL, D, D], F32)
    w_kv = dram("xa_w_kv", [NL, D, 512], F32)
    normw_pc = dram("normw_pc", [NL, 128, 8], F32)
    memnormw_pc = dram("memnormw_pc", [NL, 128, 8], F32)
    w2_in = dram("w2", [NL, 16, 2, 256], F32)
    bias_bc = dram("bias_bc", [NL, 128, 512], F32)
    ohnorm_bc = dram("ohnorm_bc", [NL, 128, 768], F32)
    xanorm_bc = dram("xanorm_bc", [NL, 128, 256], F32)
    convw_in = dram("convw", [NL, 96, 12, 5], F32)
    alog_bc = dram("alog_bc", [NL, 128, 8], F32)
    dtb_bc = dram("dtb_bc", [NL, 128, 8], F32)
    fnorm_bc = dram("fnorm_bc", [128, D], F32)
    ident_bf_in = dram("ident_bf", [128, 128], BF16)
    ident_f_in = dram("ident_f", [64, 64], F32)
    masks_in = dram("masks", [64, 5, 4, 64], F32)
    ones_f_in = dram("ones_f", [64, 96], F32)
    ones_bf_in = dram("ones_bf", [96, 96], BF16)
    out = dram("out", [T, D], F32, "ExternalOutput")
    KV = dram("s_kv", [T, 640], BF16, SCR)
    GATE = dram("s_gate", [T, 1024], BF16, SCR)
    GB = dram("s_gb", [T, 16], F32, SCR)
    LA = dram("s_la", [T, 512], F32, SCR)
    QKT = dram("s_qkt", [512, T], BF16, SCR)
    RAWT = dram("s_rawt", [12, 96, T], BF16, SCR)
    C2T = dram("s_c2t", [12, 96, T], BF16, SCR)
    OC3 = dram("s_oc3", [T, 256], BF16, SCR)
    OFB = [dram("s_of", [T, 768], F32, SCR), dram("s_ob", [T, 768], F32, SCR)]
    X1 = dram("s_x1", [T, D], F32, SCR)

    with contextlib.ExitStack() as st:
        S = Sched(nc, st)
        sb = lambda n, s, dt: st.enter_context(nc.sbuf_tensor("sb_" + n, s, dt))
        ident_bf = sb("ident_bf", [128, 128], BF16)
        ident_f = sb("ident_f", [64, 64], F32)
        masks = sb("masks", [64, 5, 4, 64], F32)
        masks5 = masks
        ones_f = sb("ones_f", [64, 96], F32)
        ones_bf = sb("ones_bf", [96, 96], BF16)
        fnorm = sb("fnorm", [128, D], F32)
        for (t_, d_, k_) in ((ident_bf, ident_bf_in, "c_idbf"), (ident_f, ident_f_in, "c_idf"),
                             (masks, masks_in, "c_masks"), (ones_f, ones_f_in, "c_onesf"),
                             (ones_bf, ones_bf_in, "c_onesbf"), (fnorm, fnorm_bc, "c_fnorm")):
            S.dma("sp", lambda e, t_=t_, d_=d_: e.dma_start(out=t_[:], in_=d_), writes=[k_])
        LE, GE, GT, LT = 0, 1, 2, 3

        w_out_sb = sb("w_out_sb", [128, 8, D], BF16)
        normw = sb("normw", [128, 8], F32)
        memnormw = sb("memnormw", [128, 8], F32)
        w2 = sb("w2sb", [16, 2, 256], F32)
        biasb = sb("biasb", [128, 512], F32)
        ohnorm = sb("ohnorm", [128, 768], F32)
        xanorm = sb("xanorm", [128, 256], F32)
        convw = sb("convw", [96, 12, 5], F32)
        nega = sb("nega", [128, 8], F32)
        dtb = sb("dtb", [128, 8], F32)
        mkT = sb("mkT", [64, 4, MEM], BF16)
        mv = sb("mv", [128, 2, 256], BF16)

        PB = [st.enter_context(nc.psum_tensor(f"pb{i}", [128, 512], F32)) for i in range(8)]

        def rms_rstd(eng_out, src_ap, n, key_src, key_out):
            S.op("act", lambda e: e.activation(out=eng_out, in_=src_ap, func=AF.Ln, scale=1.0 / n, bias=EPS),
                 reads=[key_src], writes=[key_out])
            S.op("act", lambda e: e.activation(out=eng_out, in_=eng_out, func=AF.Exp, scale=-0.5),
                 reads=[key_out], writes=[key_out])

        marks = []
        def mark(name):
            marks.append((name, len(S.ops['pe']), len(S.ops['act']), len(S.ops['dve']), len(S.ops['pool'])))
        def emit_layer(l):
            mark(f'L{l} start')
            xsrc = x_in if l == 0 else X1
            last = (l == nlayers - 1)
            if 0 in passes or 1 in passes:
                with contextlib.ExitStack() as ps:
                    sbp = lambda n, s, dt: ps.enter_context(nc.sbuf_tensor(f"sbA{l}_" + n, s, dt))
                    w_in_sb = sbp("w_in_sb", [128, 8, INW], BF16)
                    stage = [sbp(f"stage{i}", [128, 1688], F32) for i in range(2)]
                    xt = [sbp(f"xt{i}", [128, D], F32) for i in range(2)]
                    junk = sbp("junk", [128, D], BF16)
                    xs = [sbp(f"xs{i}", [128, D], BF16) for i in range(2)]
                    ss = sbp("ss", [128, 1], F32)
                    rstd = sbp("rstd", [128, 1], F32)
                    hTs = [sbp(f"hT{i}", [128, 8, 512], BF16) for i in range(2)]
                    hT = hTs[0]
                    kvo = [sbp(f"kvo{i}", [128, 640], BF16) for i in range(2)]
                    gate_o = [sbp(f"gateo{i}", [128, 1024], BF16) for i in range(2)]
                    gb_o = [sbp(f"gbo{i}", [128, 16], F32) for i in range(2)]
                    gtmp = sbp("gtmp", [128, 8], F32)
                    la_o = [sbp(f"lao{i}", [128, 512], F32) for i in range(2)]
                    fm_o = [sbp(f"fmo{i}", [128, 512], BF16) for i in range(2)]
                    lrT = sbp("lrT", [16, 2, 512], F32)
                    raw_o = sbp("raw_o", [96, 12, 512], BF16)
                    q3T = sbp("q3T", [64, 4, 512], BF16)
                    mx = [sbp(f"mx{i}", [128, 4], F32) for i in range(2)]
                    nb = [sbp(f"nb{i}", [128, 4], F32) for i in range(2)]
                    rs = [sbp(f"rs{i}", [128, 4], F32) for i in range(2)]
                    pexp = [sbp(f"pexp{i}", [128, 4, MEM], BF16) for i in range(2)]
                    pT = [sbp(f"pT{i}", [128, 8, 128], BF16) for i in range(2)]
                    o3 = [sbp(f"o3{i}", [128, 4, 64], F32) for i in range(2)]
                    o3sq = [sbp(f"o3sq{i}", [128, 4, 64], F32) for i in range(2)]
                    o3ss = [sbp(f"o3ss{i}", [128, 4], F32) for i in range(2)]
                    oc3_o = [sbp(f"oc3o{i}", [128, 256], BF16) for i in range(2)]
                    def token_tile_to_hT(src_dram, row0, slot, dst, dst_key, col0):
                        xk = ("xt", slot)
                        S.dma("sp", lambda e: e.dma_start(out=xt[slot][:], in_=src_dram[row0:row0 + 128, :]), writes=[xk])
                        S.op("act", lambda e: e.activation(out=junk[:], in_=xt[slot][:], func=AF.Square, accum_out=ss[:]),
                             reads=[xk], writes=["junk", "ss"])
                        rms_rstd(rstd[:], ss[:], D, "ss", "rstd")
                        S.op("dve", lambda e: e.tensor_scalar(out=xs[slot][:], in0=xt[slot][:], scalar1=rstd[:], scalar2=None,
                                                              op0=ALU.mult), reads=[xk, "rstd"], writes=[("xs", slot)])
                        ptr = PB[0][:, 0:512].bitcast(BF16).rearrange("p (c t) -> p c t", c=8)
                        for c in range(8):
                            S.op("pe", lambda e, c=c: e.transpose(out=ptr[:, c, :], in_=xs[slot][:, c * 128:(c + 1) * 128],
                                                                  identity=ident_bf[:]),
                                 reads=[("xs", slot), "c_idbf"], writes=["pb0"])
                        S.op("dve", lambda e: e.tensor_copy(out=dst[:, :, col0:col0 + 128], in_=ptr), reads=["pb0"],
                             writes=[dst_key])

                    if 0 in passes:
                        for (t_, d_, k_) in ((normw, normw_pc[l], "normw"), (memnormw, memnormw_pc[l], "memnormw"),
                                             (w2, w2_in[l], "w2"), (biasb, bias_bc[l], "biasb"),
                                             (ohnorm, ohnorm_bc[l], "ohnorm"), (xanorm, xanorm_bc[l], "xanorm"),
                                             (convw, convw_in[l], "convw"), (nega, alog_bc[l], "nega"),
                                             (dtb, dtb_bc[l], "dtb")):
                            S.dma("sp", lambda e, t_=t_, d_=d_: e.dma_start(out=t_[:], in_=d_), writes=[k_])
                        S.op("act", lambda e: e.activation(out=nega[:], in_=nega[:], func=AF.Exp), reads=["nega"], writes=["nega"])
                        S.op("dve", lambda e: e.tensor_scalar(out=nega[:], in0=nega[:], scalar1=-1.0, scalar2=None, op0=ALU.mult),
                             reads=["nega"], writes=["nega"])
                        for cc in range(16):
                            c, hf = divmod(cc, 2)
                            sl = cc % 2
                            S.dma("sp", lambda e, c=c, sl=sl, hf=hf: e.dma_start(out=stage[sl][:], in_=w_in[l, c * 128:(c + 1) * 128, hf * 1688:(hf + 1) * 1688]),
                                  writes=[("stage", sl)])
                            S.op("dve" if cc % 2 == 0 else "pool",
                                 lambda e, c=c, sl=sl, hf=hf: e.tensor_scalar(out=w_in_sb[:, c, hf * 1688:(hf + 1) * 1688], in0=stage[sl][:], scalar1=normw[:, c:c + 1],
                                                                       scalar2=None, op0=ALU.mult),
                                 reads=[("stage", sl), "normw"], writes=["w_in_sb"])
                        for c in range(8):
                            sl = c % 2
                            S.dma("sp", lambda e, c=c, sl=sl: e.dma_start(out=stage[sl][:, 0:D], in_=w_out[l, c * 128:(c + 1) * 128, :]),
                                  writes=[("stage", sl)])
                            S.op("act", lambda e, c=c, sl=sl: e.copy(out=w_out_sb[:, c, :], in_=stage[sl][:, 0:D]),
                                 reads=[("stage", sl)], writes=["w_out_sb"])
                        wkv = hT
                        for c in range(8):
                            sl = c % 2
                            S.dma("sp", lambda e, c=c, sl=sl: e.dma_start(out=stage[sl][:, 0:512], in_=w_kv[l, c * 128:(c + 1) * 128, :]),
                                  writes=[("stage", sl)])
                            S.op("dve", lambda e, c=c, sl=sl: e.tensor_scalar(out=wkv[:, c, :], in0=stage[sl][:, 0:512],
                                                                              scalar1=memnormw[:, c:c + 1], scalar2=None, op0=ALU.mult),
                                 reads=[("stage", sl), "memnormw"], writes=[("hT", 0)])
                        mT = raw_o[:, 0:4, :]
                        memT = sbp("memT", [128, 8, MEM], BF16)
                        for mt_ in range(2):
                            token_tile_to_hT(mem_in, mt_ * 128, mt_, memT, "memT", mt_ * 128)
                        for h in range(4):
                            pm = PB[1][0:64, 0:MEM]
                            for c in range(8):
                                S.op("pe", lambda e, c=c, h=h, pm=pm: e.matmul(pm, lhsT=wkv[:, c, h * 64:(h + 1) * 64], rhs=memT[:, c, :],
                                                                               start=(c == 0), stop=(c == 7)),
                                     reads=[("hT", 0), "memT"], writes=["pb1"])
                            S.op("act", lambda e, h=h, pm=pm: e.copy(out=mkT[:, h, :], in_=pm), reads=["pb1"], writes=["mkT"])
                        for mt_ in range(2):
                            pm = PB[1][:, 0:256]
                            for c in range(8):
                                S.op("pe", lambda e, c=c, mt_=mt_, pm=pm: e.matmul(pm, lhsT=memT[:, c, mt_ * 128:(mt_ + 1) * 128],
                                                                                   rhs=wkv[:, c, 256:512], start=(c == 0), stop=(c == 7)),
                                     reads=[("hT", 0), "memT"], writes=["pb1"])
                            S.op("act", lambda e, mt_=mt_, pm=pm: e.copy(out=mv[:, mt_, :], in_=pm), reads=["pb1"], writes=["mv"])

                    mark(f'L{l} p0 done')
                    if 1 in passes:
                        def do_mt(mt, hT, hk):
                            for sub in range(4):
                                tt = mt * 4 + sub
                                r0 = tt * 128
                                sl = tt % 2
                                if tt + 1 < T // 128:
                                    prep_tile(tt + 1)
                                hsl = lambda c: hT[:, c, sub * 128:(sub + 1) * 128]
                                groups = ((1, 256, 512), (2, 768, 512), (3, 2464, 400), (4, 3120, 256))
                                for (bk, c0, wd) in groups:
                                    for c in range(8):
                                        S.op("pe", lambda e, c=c, bk=bk, c0=c0, wd=wd, sub=sub: e.matmul(
                                            PB[bk][:, 0:wd], lhsT=hT[:, c, sub * 128:(sub + 1) * 128], rhs=w_in_sb[:, c, c0:c0 + wd],
                                            start=(c == 0), stop=(c == 7)), reads=[hk, "w_in_sb"], writes=[f"pb{bk}"])
                                S.op("act", lambda e, sl=sl: e.copy(out=kvo[sl][:, 0:512], in_=PB[1][:, 0:512]), reads=["pb1"],
                                     writes=[("kvo", sl)])
                                S.op("dve", lambda e, sl=sl: e.tensor_copy(out=kvo[sl][:, 512:640], in_=PB[2][:, 0:128]), reads=["pb2"],
                                     writes=[("kvo", sl)])
                                S.dma("pool", lambda e, sl=sl, r0=r0: e.dma_start(out=KV[r0:r0 + 128, :], in_=kvo[sl][:]),
                                      reads=[("kvo", sl)], writes=["KV"])
                                S.op("act", lambda e, sl=sl: e.activation(out=gate_o[sl][:, 0:384], in_=PB[2][:, 128:512], func=AF.Silu),
                                     reads=["pb2"], writes=[("gateo", sl)])
                                S.op("act", lambda e, sl=sl: e.activation(out=gate_o[sl][:, 384:768], in_=PB[3][:, 0:384], func=AF.Silu),
                                     reads=["pb3"], writes=[("gateo", sl)])
                                S.op("act", lambda e, sl=sl: e.activation(out=gate_o[sl][:, 768:1024], in_=PB[4][:, 0:256], func=AF.Silu),
                                     reads=["pb4"], writes=[("gateo", sl)])
                                S.dma("pool", lambda e, sl=sl, r0=r0: e.dma_start(out=GATE[r0:r0 + 128, :], in_=gate_o[sl][:]),
                                      reads=[("gateo", sl)], writes=["GATE"])
                                S.op("act", lambda e, sl=sl: e.activation(out=gb_o[sl][:, 0:8], in_=PB[3][:, 384:392], func=AF.Exp, scale=-1.0),
                                     reads=["pb3"], writes=[("gbo", sl)])
                                S.op("dve", lambda e, sl=sl: e.tensor_scalar(out=gb_o[sl][:, 0:8], in0=gb_o[sl][:, 0:8], scalar1=1.0, scalar2=None, op0=ALU.add),
                                     reads=[("gbo", sl)], writes=[("gbo", sl)])
                                S.op("dve", lambda e, sl=sl: e.reciprocal(out=gb_o[sl][:, 0:8], in_=gb_o[sl][:, 0:8]), reads=[("gbo", sl)], writes=[("gbo", sl)])
                                S.op("dve", lambda e: e.tensor_tensor(out=gtmp[:], in0=PB[3][:, 392:400], in1=dtb[:], op=ALU.add),
                                     reads=["pb3", "dtb"], writes=["gtmp"])
                                S.op("act", lambda e: e.activation(out=gtmp[:], in_=gtmp[:], func=AF.Exp), reads=["gtmp"], writes=["gtmp"])
                                S.op("act", lambda e: e.activation(out=gtmp[:], in_=gtmp[:], func=AF.Ln, bias=1.0), reads=["gtmp"],
                                     writes=["gtmp"])
                                S.op("dve", lambda e, sl=sl: e.tensor_tensor(out=gb_o[sl][:, 8:16], in0=gtmp[:], in1=nega[:], op=ALU.mult),
                                     reads=["gtmp", "nega"], writes=[("gbo", sl)])
                                S.dma("pool", lambda e, sl=sl, r0=r0: e.dma_start(out=GB[r0:r0 + 128, :], in_=gb_o[sl][:]),
                                      reads=[("gbo", sl)], writes=["GB"])
                            t0 = mt * 512
                            fmi = 0
                            for m in range(4):
                                bk = 5 + (m % 2)
                                for c in range(8):
                                    S.op("pe", lambda e, c=c, m=m, bk=bk: e.matmul(PB[bk][:, :], lhsT=w_in_sb[:, c, m * 128:(m + 1) * 128],
                                                                                   rhs=hT[:, c, :], start=(c == 0), stop=(c == 7)),
                                         reads=[hk, "w_in_sb"], writes=[f"pb{bk}"])
                                fs = fmi % 2
                                fmi += 1
                                S.op("act", lambda e, bk=bk, fs=fs, m=m: e.activation(out=fm_o[fs][:], in_=PB[bk][:, :], func=AF.Copy,
                                                                                      scale=(0.125 if m < 2 else 1.0)),
                                     reads=[f"pb{bk}"], writes=[("fmo", fs)])
                                S.dma("pool", lambda e, fs=fs, m=m, t0=t0: e.dma_start(out=QKT[m * 128:(m + 1) * 128, t0:t0 + 512],
                                                                                       in_=fm_o[fs][:]),
                                      reads=[("fmo", fs)], writes=["QKT"])
                            for z in range(2):
                                for c in range(8):
                                    S.op("pe", lambda e, c=c, z=z: e.matmul(PB[7][0:16, :], lhsT=w_in_sb[:, c, 1280 + z * 16:1296 + z * 16],
                                                                            rhs=hT[:, c, :], start=(c == 0), stop=(c == 7)),
                                         reads=[hk, "w_in_sb"], writes=["pb7"])
                                S.op("act", lambda e, z=z: e.copy(out=lrT[:, z, :], in_=PB[7][0:16, :]), reads=["pb7"], writes=["lrT"])
                            def gate_logits(sub):
                                tt = mt * 4 + sub
                                r0 = tt * 128
                                sl = tt % 2
                                for z in range(2):
                                    S.op("pe", lambda e, z=z, sub=sub: e.matmul(PB[7][:, z * 256:(z + 1) * 256],
                                                                                lhsT=lrT[:, z, sub * 128:(sub + 1) * 128], rhs=w2[:, z, :],
                                                                                start=True, stop=True),
                                         reads=["lrT", "w2"], writes=["pb7"])
                                S.op("dve", lambda e, sl=sl: e.tensor_tensor(out=la_o[sl][:], in0=PB[7][:, :], in1=biasb[:], op=ALU.add),
                                     reads=["pb7", "biasb"], writes=[("lao", sl)])
                                S.op("act", lambda e, sl=sl: e.activation(out=la_o[sl][:], in_=la_o[sl][:], func=AF.Exp, scale=-1.0),
                                     reads=[("lao", sl)], writes=[("lao", sl)])
                                S.op("act", lambda e, sl=sl: e.activation(out=la_o[sl][:], in_=la_o[sl][:], func=AF.Ln, bias=1.0),
                                     reads=[("lao", sl)], writes=[("lao", sl)])
                                S.op("dve", lambda e, sl=sl: e.tensor_scalar(out=la_o[sl][:], in0=la_o[sl][:], scalar1=-1.0 / 16.0,
                                                                             scalar2=None, op0=ALU.mult),
                                     reads=[("lao", sl)], writes=[("lao", sl)])
                                S.dma("pool", lambda e, sl=sl, r0=r0: e.dma_start(out=LA[r0:r0 + 128, :], in_=la_o[sl][:]),
                                      reads=[("lao", sl)], writes=["LA"])
                            for jj in range(12):
                                bk = 5 + (jj % 2)
                                for c in range(8):
                                    S.op("pe", lambda e, c=c, jj=jj, bk=bk: e.matmul(PB[bk][0:96, :],
                                                                                     lhsT=w_in_sb[:, c, 1312 + jj * 96:1312 + (jj + 1) * 96],
                                                                                     rhs=hT[:, c, :], start=(c == 0), stop=(c == 7)),
                                         reads=[hk, "w_in_sb"], writes=[f"pb{bk}"])
                                S.op("act" if jj % 2 == 0 else "dve",
                                     (lambda e, jj=jj, bk=bk: e.copy(out=raw_o[:, jj, :], in_=PB[bk][0:96, :])) if jj % 2 == 0 else
                                     (lambda e, jj=jj, bk=bk: e.tensor_copy(out=raw_o[:, jj, :], in_=PB[bk][0:96, :])),
                                     reads=[f"pb{bk}"], writes=["raw_o"])
                                if jj % 3 == 2:
                                    gate_logits(jj // 3)
                            S.dma("pool", lambda e, t0=t0: e.dma_start(out=RAWT[:, :, t0:t0 + 512].rearrange("j p c -> p j c"), in_=raw_o[:]),
                                  reads=["raw_o"], writes=["RAWT"])
                            for h in range(4):
                                bk = 5 + (h % 2)
                                for c in range(8):
                                    S.op("pe", lambda e, c=c, h=h, bk=bk: e.matmul(PB[bk][0:64, :],
                                                                                   lhsT=w_in_sb[:, c, 2864 + h * 64:2864 + (h + 1) * 64],
                                                                                   rhs=hT[:, c, :], start=(c == 0), stop=(c == 7)),
                                         reads=[hk, "w_in_sb"], writes=[f"pb{bk}"])
                                S.op("act", lambda e, h=h, bk=bk: e.copy(out=q3T[:, h, :], in_=PB[bk][0:64, :]), reads=[f"pb{bk}"],
                                     writes=["q3T"])
                            def xa_sub(sub):
                                tt = mt * 4 + sub
                                r0 = tt * 128
                                sl = tt % 2
                                xb = 5 if sl == 0 else 1
                                for h in range(4):
                                    bk = xb + (h // 2)
                                    S.op("pe", lambda e, h=h, bk=bk, sub=sub: e.matmul(PB[bk][:, (h % 2) * 256:(h % 2 + 1) * 256],
                                                                                       lhsT=q3T[:, h, sub * 128:(sub + 1) * 128], rhs=mkT[:, h, :],
                                                                                       start=True, stop=True),
                                         reads=["q3T", "mkT"], writes=[f"pb{bk}"])
                                yield
                                for hp in range(2):
                                    S.op("dve", lambda e, hp=hp: e.tensor_reduce(out=mx[sl][:, hp * 2:hp * 2 + 2],
                                                                                 in_=PB[xb + hp][:, :].rearrange("p (h m) -> p h m", h=2),
                                                                                 axis=AX.X, op=ALU.max),
                                         reads=[f"pb{xb + hp}"], writes=[("mx", sl)])
                                S.op("dve", lambda e: e.tensor_scalar(out=nb[sl][:], in0=mx[sl][:], scalar1=-0.125, scalar2=None, op0=ALU.mult),
                                     reads=[("mx", sl)], writes=[("nb", sl)])
                                for h in range(4):
                                    bk = xb + (h // 2)
                                    S.op("act", lambda e, h=h, bk=bk: e.activation(out=pexp[sl][:, h, :], in_=PB[bk][:, (h % 2) * 256:(h % 2 + 1) * 256],
                                                                                   func=AF.Exp, scale=0.125, bias=nb[sl][:, h:h + 1],
                                                                                   accum_out=rs[sl][:, h:h + 1]),
                                         reads=[f"pb{bk}", ("nb", sl)], writes=[("pexp", sl), ("rs", sl)])
                                yield
                                ptp = PB[xb + 2][:, :].bitcast(BF16).rearrange("p (c t) -> p c t", c=8)
                                for h in range(4):
                                    for mb in range(2):
                                        S.op("pe", lambda e, h=h, mb=mb: e.transpose(out=ptp[:, h * 2 + mb, :],
                                                                                     in_=pexp[sl][:, h, mb * 128:(mb + 1) * 128], identity=ident_bf[:]),
                                             reads=[("pexp", sl), "c_idbf"], writes=[f"pb{xb + 2}"])
                                S.op("dve", lambda e: e.tensor_copy(out=pT[sl][:], in_=ptp), reads=[f"pb{xb + 2}"], writes=[("pT", sl)])
                                yield
                                po = PB[xb][:, 0:256].rearrange("p (h d) -> p h d", h=4)
                                for h in range(4):
                                    for mb in range(2):
                                        S.op("pe", lambda e, h=h, mb=mb: e.matmul(po[:, h, :], lhsT=pT[sl][:, h * 2 + mb, :],
                                                                                  rhs=mv[:, mb, h * 64:(h + 1) * 64], start=(mb == 0), stop=(mb == 1)),
                                             reads=[("pT", sl), "mv"], writes=[f"pb{xb}"])
                                yield
                                S.op("dve", lambda e: e.reciprocal(out=rs[sl][:], in_=rs[sl][:]), reads=[("rs", sl)], writes=[("rs", sl)])
                                S.op("dve", lambda e: e.tensor_tensor(out=o3[sl][:], in0=po, in1=rs[sl][:].unsqueeze(2).broadcast_to([128, 4, 64]),
                                                                      op=ALU.mult), reads=[f"pb{xb}", ("rs", sl)], writes=[("o3", sl)])
                                S.op("pool", lambda e: e.tensor_tensor(out=o3sq[sl][:], in0=o3[sl][:], in1=o3[sl][:], op=ALU.mult), reads=[("o3", sl)],
                                     writes=[("o3sq", sl)])
                                S.op("dve", lambda e: e.tensor_reduce(out=o3ss[sl][:], in_=o3sq[sl][:], axis=AX.X, op=ALU.add), reads=[("o3sq", sl)],
                                     writes=[("o3ss", sl)])
                                yield
                                rms_rstd(o3ss[sl][:], o3ss[sl][:], 64, ("o3ss", sl), ("o3ss", sl))
                                S.op("dve", lambda e: e.tensor_tensor(out=o3[sl][:], in0=o3[sl][:], in1=o3ss[sl][:].unsqueeze(2).broadcast_to([128, 4, 64]),
                                                                      op=ALU.mult), reads=[("o3", sl), ("o3ss", sl)], writes=[("o3", sl)])
                                S.op("pool", lambda e: e.tensor_tensor(out=o3[sl][:], in0=o3[sl][:], in1=xanorm[:].rearrange("p (h d) -> p h d", h=4),
                                                                       op=ALU.mult), reads=[("o3", sl), "xanorm"], writes=[("o3", sl)])
                                S.dma("sp", lambda e, sl=sl, r0=r0: e.dma_start(out=oc3_o[sl][:], in_=GATE[r0:r0 + 128, 768:1024]),
                                      reads=["GATE"], writes=[("oc3o", sl)])
                                S.op("dve", lambda e, sl=sl: e.tensor_tensor(out=oc3_o[sl][:], in0=o3[sl][:].rearrange("p h d -> p (h d)"),
                                                                             in1=oc3_o[sl][:], op=ALU.mult),
                                     reads=[("o3", sl), ("oc3o", sl)], writes=[("oc3o", sl)])
                                S.dma("pool", lambda e, sl=sl, r0=r0: e.dma_start(out=OC3[r0:r0 + 128, :], in_=oc3_o[sl][:]),
                                      reads=[("oc3o", sl)], writes=["OC3"])
                            xa_act = []
                            xa_n = 0
                            while True:
                                while len(xa_act) < 2 and xa_n < 4:
                                    xa_act.append(xa_sub(xa_n))
                                    xa_n += 1
                                if not xa_act:
                                    break
                                for g_ in list(xa_act):
                                    if next(g_, "done") == "done":
                                        xa_act.remove(g_)

                        def prep_tile(tt_):
                            mt__, sub__ = divmod(tt_, 4)
                            token_tile_to_hT(xsrc, tt_ * 128, tt_ % 2, hTs[mt__ % 2], ("hT", mt__ % 2), sub__ * 128)
                        prep_tile(0)
                        for mt_ in range(T // 512):
                            do_mt(mt_, hTs[mt_ % 2], ("hT", mt_ % 2))
                    S.barrier()

            mark(f'L{l} p1 done')
            if 2 in passes:
                with contextlib.ExitStack() as ps:
                    sbp = lambda n, s, dt: ps.enter_context(nc.sbuf_tensor(f"sbB{l}_" + n, s, dt))
                    rawh = [sbp(f"rawh{i}", [96, 12, 516], BF16) for i in range(2)]
                    dg = sbp("dg", [96, 12, 5, 96], BF16)
                    for jj in range(12):
                        S.op("dve" if jj % 2 == 0 else "pool", lambda e, jj=jj: e.tensor_tensor(
                            out=dg[:, jj, :, :], in0=ident_bf[0:96, 0:96].unsqueeze(1).broadcast_to([96, 5, 96]),
                            in1=convw[:, jj, :].unsqueeze(2).broadcast_to([96, 5, 96]), op=ALU.mult),
                            reads=["c_idbf", "convw"], writes=["dg"])
                    sil = [sbp(f"sil{i}", [96, 512], F32) for i in range(8)]
                    sqb = [sbp(f"sqb{i}", [96, 512], BF16) for i in range(8)]
                    rsd = [sbp(f"rsd{i}", [96, 512], F32) for i in range(2)]
                    c2_o = [sbp(f"c2o{i}", [96, 12, 512], BF16) for i in range(2)]
                    for mt in range(T // 512):
                        t0 = mt * 512
                        rs_ = mt % 2
                        lo = max(t0 - 2, 0)
                        hi = min(t0 + 514, T)
                        rk = ("rawh", rs_)
                        if mt == 0:
                            S.op("pool", lambda e, rs_=rs_: e.memset(rawh[rs_][:, :, 0:2], 0.0), writes=[rk])
                        if mt == T // 512 - 1:
                            S.op("pool", lambda e, rs_=rs_: e.memset(rawh[rs_][:, :, 514:516], 0.0), writes=[rk])
                        S.dma("sp", lambda e, rs_=rs_, lo=lo, hi=hi, t0=t0: e.dma_start(
                            out=rawh[rs_][:, :, lo - (t0 - 2):hi - (t0 - 2)], in_=RAWT[:, :, lo:hi].rearrange("j p c -> p j c")),
                            reads=["RAWT"], writes=[rk])
                        for jj in range(12):
                            a_ = jj % 4
                            ak = f"pb{1 + a_}"
                            accp = PB[1 + a_][0:96, :]
                            for k in range(5):
                                S.op("pe", lambda e, jj=jj, a_=a_, rs_=rs_, k=k: e.matmul(PB[1 + a_][0:96, :], lhsT=dg[:, jj, k, :],
                                                                                         rhs=rawh[rs_][:, jj, k:k + 512], start=(k == 0), stop=(k == 4)),
                                     reads=[rk, "dg"], writes=[ak])
                            if jj >= 8:
                                S.op("act", lambda e, jj=jj, a_=a_, rs_=rs_: e.activation(out=c2_o[rs_][:, jj, :], in_=PB[1 + a_][0:96, :], func=AF.Silu),
                                     reads=[ak], writes=[("c2o", rs_)])
                            else:
                                s_ = jj
                                S.op("act", lambda e, a_=a_, s_=s_: e.activation(out=sil[s_][:], in_=PB[1 + a_][0:96, :], func=AF.Silu),
                                     reads=[ak], writes=[("sil", s_)])
                                S.op("pool", lambda e, s_=s_: e.tensor_tensor(out=sqb[s_][:], in0=sil[s_][:], in1=sil[s_][:], op=ALU.mult),
                                     reads=[("sil", s_)], writes=[("sqb", s_)])
                        for jj in range(8):
                            s_ = jj
                            r_ = jj % 2
                            bk = (5, 6, 7, 0)[jj % 4]
                            S.op("pe", lambda e, s_=s_, bk=bk: e.matmul(PB[bk][0:96, :], lhsT=ones_bf[:], rhs=sqb[s_][:], start=True, stop=True),
                                 reads=[("sqb", s_), "c_onesbf"], writes=[f"pb{bk}"])
                            S.op("act", lambda e, r_=r_, bk=bk: e.activation(out=rsd[r_][:], in_=PB[bk][0:96, :], func=AF.Ln, bias=EPS),
                                 reads=[f"pb{bk}"], writes=[("rsd", r_)])
                            S.op("act", lambda e, r_=r_: e.activation(out=rsd[r_][:], in_=rsd[r_][:], func=AF.Exp, scale=-0.5),
                                 reads=[("rsd", r_)], writes=[("rsd", r_)])
                            qs = (96.0 ** -0.5) if jj < 4 else 1.0
                            S.op("dve", lambda e, s_=s_, r_=r_, jj=jj, qs=qs, rs_=rs_: e.scalar_tensor_tensor(
                                out=c2_o[rs_][:, jj, :], in0=sil[s_][:], scalar=qs, in1=rsd[r_][:], op0=ALU.mult, op1=ALU.mult),
                                reads=[("sil", s_), ("rsd", r_)], writes=[("c2o", rs_)])
                        S.dma("pool", lambda e, rs_=rs_, t0=t0: e.dma_start(out=C2T[:, :, t0:t0 + 512].rearrange("j p c -> p j c"),
                                                                          in_=c2_o[rs_][:]), reads=[("c2o", rs_)], writes=["C2T"])
                    S.barrier()

            mark(f'L{l} p2 done')
            if 3 in passes:
                with contextlib.ExitStack() as ps:
                    sbp = lambda n, s, dt: ps.enter_context(nc.sbuf_tensor(f"sbC{l}_" + n, s, dt))
                    two = range(2)
                    glaq = [[sbp(f"glaq{d}{i}", [64, 4, 256], BF16) for i in two] for d in two]
                    glak = [[sbp(f"glak{d}{i}", [64, 4, 256], BF16) for i in two] for d in two]
                    glakv = [[sbp(f"glakv{d}{i}", [64, 4, 640], BF16) for i in two] for d in two]
                    glala = [[sbp(f"glala{d}{i}", [64, 4, 256], F32) for i in two] for d in two]
                    gdnc2 = [[sbp(f"gdnc2{d}{i}", [96, 12, 256], BF16) for i in two] for d in two]
                    gdngb = [[sbp(f"gdngb{d}{i}", [64, 4, 16], F32) for i in two] for d in two]
                    ebT = [sbp(f"ebT{d}", [64, 4, 64], F32) for d in two]
                    enbT = [sbp(f"enbT{d}", [64, 4, 64], F32) for d in two]
                    erem = [sbp(f"erem{d}", [64, 256], F32) for d in two]
                    qeT = [sbp(f"qeT{d}", [64, 4, 64], BF16) for d in two]
                    keT = [sbp(f"keT{d}", [64, 4, 64], BF16) for d in two]
                    ktl = [sbp(f"ktl{d}", [64, 256], BF16) for d in two]
                    ATm = [sbp(f"ATm{d}", [64, 4, 64], BF16) for d in two]
                    gS32 = [sbp(f"gS32{d}", [64, 4, 96], F32) for d in two]
                    gSbf = [sbp(f"gSbf{d}", [64, 4, 96], BF16) for d in two]
                    go_sb = [sbp(f"go_sb{d}", [64, 384], F32) for d in two]
                    ebT8 = sbp("ebT8", [64, 8, 64], F32)
                    enbT8 = sbp("enbT8", [64, 8, 64], F32)
                    erem8 = sbp("erem8", [64, 512], F32)
                    qeT8 = sbp("qeT8", [64, 8, 64], BF16)
                    keT8 = sbp("keT8", [64, 8, 64], BF16)
                    ktl8 = sbp("ktl8", [64, 512], BF16)
                    ATm8 = sbp("ATm8", [64, 8, 64], BF16)
                    gS328 = sbp("gS328", [64, 8, 96], F32)
                    gSbf8 = sbp("gSbf8", [64, 8, 96], BF16)
                    S.op("pool", lambda e: e.memset(gS328[:], 0.0), writes=["gS328"])
                    S.op("pool", lambda e: e.memset(gSbf8[:], 0.0), writes=["gSbf8"])

                    gb8 = sbp("gb8", [64, 16], F32)
                    K8 = sbp("K8", [64, 8, 96], BF16)
                    V8 = sbp("V8", [64, 8, 96], BF16)
                    eg16 = sbp("eg16", [64, 16], F32)
                    dec96 = sbp("dec96m", [96, 8], F32)
                    R8 = sbp("R8", [64, 8, 64], F32)
                    kkN = sbp("kkN", [64, 8, 64], F32)
                    kkM = sbp("kkM", [64, 8, 64], F32)
                    qkI = sbp("qkI", [64, 8, 64], F32)
                    decT = sbp("decT", [64, 8, 64], F32)
                    decD = sbp("decD", [64, 8, 64], F32)
                    EGB8 = sbp("EGB8", [96, 8, 64], F32)
                    DB8 = sbp("DB8", [64, 8, 64], BF16)
                    Nm = [sbp(f"Nm{i}", [64, 8, 64], NDT) for i in two]
                    Mm = [sbp(f"Mm{i}", [64, 8, 64], NDT) for i in two]
                    qkm8 = sbp("qkm8", [96, 8, 64], BF16)
                    Pm8 = sbp("Pm8", [64, 8, 64], NDT)
                    TiT8 = sbp("TiT8", [64, 8, 64], BF16)
                    vb8 = sbp("vb8", [64, 8, 96], BF16)
                    bg8 = sbp("bg8", [64, 8], F32)
                    kbg8 = sbp("kbg8", [64, 8, 96], BF16)
                    kte8 = sbp("kte8", [64, 8, 96], BF16)
                    qdT8 = sbp("qdT8", [96, 8, 64], BF16)
                    u8 = sbp("u8", [64, 8, 96], F32)
                    wT8 = sbp("wT8", [96, 8, 64], BF16)
                    vnew8 = sbp("vnew8", [96, 8, 96], BF16)
                    dS328 = sbp("dS328", [96, 8, 96], F32)
                    dSbf8 = sbp("dSbf8", [96, 8, 96], BF16)
                    do_sb = [sbp(f"do_sb{d}", [64, 384], F32) for d in two]
                    S.op("pool", lambda e: e.memset(dS328[:], 0.0), writes=["dS328"])
                    S.op("pool", lambda e: e.memset(qkm8[:], 0.0), writes=["qkm8"])
                    S.op("pool", lambda e: e.memset(vnew8[:], 0.0), writes=["vnew8"])
                    S.op("pool", lambda e: e.memset(dSbf8[:], 0.0), writes=["dSbf8"])

                    for d in two:
                        S.op("pool", lambda e, d=d: e.memset(gS32[d][:], 0.0), writes=[("gS32", d)])
                        S.op("pool", lambda e, d=d: e.memset(gSbf[d][:], 0.0), writes=[("gSbf", d)])

                    def load_group(d, grp):
                        gs = grp % 2
                        g0 = grp * 256
                        S.dma("sp", lambda e: e.dma_start(out=glaq[d][gs][:], in_=QKT[0:256, g0:g0 + 256].rearrange("(h p) c -> p h c", p=64)),
                              reads=["QKT"], writes=[("glaq", d, gs)])
                        S.dma("sp", lambda e: e.dma_start(out=glak[d][gs][:], in_=QKT[256:512, g0:g0 + 256].rearrange("(h p) c -> p h c", p=64)),
                              reads=["QKT"], writes=[("glak", d, gs)])
                        S.dma("sp", lambda e: e.dma_start(out=glakv[d][gs][:], in_=KV[g0:g0 + 256, :].rearrange("(n p) c -> p n c", p=64)),
                              reads=["KV"], writes=[("glakv", d, gs)])
                        S.dma("sp", lambda e: e.dma_start(out=glala[d][gs][:], in_=LA[g0:g0 + 256, d * 256:(d + 1) * 256].rearrange("(n p) c -> p n c", p=64)),
                              reads=["LA"], writes=[("glala", d, gs)])
                        S.dma("sp", lambda e: e.dma_start(out=gdnc2[d][gs][:], in_=C2T[:, :, g0:g0 + 256].rearrange("j p c -> p j c")),
                              reads=["C2T"], writes=[("gdnc2", d, gs)])
                        S.dma("sp", lambda e: e.dma_start(out=gdngb[d][gs][:], in_=GB[g0:g0 + 256, :].rearrange("(n p) c -> p n c", p=64)),
                              reads=["GB"], writes=[("gdngb", d, gs)])

                    def gla_chunk(d, n):
                        grp, ci = divmod(n, 4)
                        gs = grp % 2
                        qg, kg, kvg, lag = glaq[d][gs], glak[d][gs], glakv[d][gs], glala[d][gs]
                        kq_, kk_, kkv_, kla_ = ("glaq", d, gs), ("glak", d, gs), ("glakv", d, gs), ("glala", d, gs)
                        cs = slice(ci * 64, (ci + 1) * 64)
                        mI = LE if d == 0 else GE
                        mS = GT if d == 0 else LT
                        bT_ps = PB[0][0:64, 0:256].rearrange("p (h c) -> p h c", h=4)
                        rem_ps = PB[0][0:64, 256:512]
                        import os as _os
                        STG = int(_os.environ.get('K_STAGE', 99))
                        for h in range(4):
                            S.op("pe", lambda e, h=h: e.matmul(bT_ps[:, h, :], lhsT=lag[:, ci, h * 64:(h + 1) * 64], rhs=masks[:, mI, 0, :],
                                                               start=True, stop=True), reads=[kla_, "c_masks"], writes=["pb0"])
                        S.op("pe", lambda e: e.matmul(rem_ps, lhsT=masks[:, mS, 0, :], rhs=lag[:, ci, :], start=True, stop=True),
                             reads=[kla_, "c_masks"], writes=["pb0"])
                        yield
                        S.op("act", lambda e: e.activation(out=ebT[d][:], in_=bT_ps, func=AF.Exp), reads=["pb0"], writes=[("ebT", d)])
                        S.op("act", lambda e: e.activation(out=enbT[d][:], in_=bT_ps, func=AF.Exp, scale=-1.0), reads=["pb0"], writes=[("enbT", d)])
                        S.op("act", lambda e: e.activation(out=erem[d][:], in_=rem_ps, func=AF.Exp), reads=["pb0"], writes=[("erem", d)])
                        yield
                        S.op("dve", lambda e: e.tensor_tensor(out=qeT[d][:], in0=qg[:, :, cs], in1=ebT[d][:], op=ALU.mult),
                             reads=[kq_, ("ebT", d)], writes=[("qeT", d)])
                        S.op("pool", lambda e: e.tensor_tensor(out=keT[d][:], in0=kg[:, :, cs], in1=enbT[d][:], op=ALU.mult),
                             reads=[kk_, ("enbT", d)], writes=[("keT", d)])
                        S.op("dve", lambda e: e.tensor_tensor(out=ktl[d][:], in0=kvg[:, ci, 0:256], in1=erem[d][:], op=ALU.mult),
                             reads=[kkv_, ("erem", d)], writes=[("ktl", d)])
                        yield
                        AT_ps = PB[1][0:64, 0:256].rearrange("p (h c) -> p h c", h=4)
                        for h in range(4):
                            S.op("pe", lambda e, h=h: e.matmul(AT_ps[:, h, :], lhsT=keT[d][:, h, :], rhs=qeT[d][:, h, :], start=True, stop=True),
                                 reads=[("keT", d), ("qeT", d)], writes=["pb1"])
                        yield
                        S.op("dve", lambda e: e.tensor_tensor(out=ATm[d][:], in0=AT_ps, in1=masks[:, mI, :, :], op=ALU.mult),
                             reads=["pb1", "c_masks"], writes=[("ATm", d)])
                        yield
                        o_ps = PB[0][0:64, 0:384].rearrange("p (h v) -> p h v", h=4)
                        for h in range(4):
                            S.op("pe", lambda e, h=h: e.matmul(o_ps[:, h, :], lhsT=ATm[d][:, h, :], rhs=kvg[:, ci, 256 + h * 96:256 + (h + 1) * 96],
                                                               start=True, stop=False), reads=[("ATm", d), kkv_], writes=["pb0"])
                            S.op("pe", lambda e, h=h: e.matmul(o_ps[:, h, :], lhsT=qeT[d][:, h, :], rhs=gSbf[d][:, h, :], start=False, stop=True),
                                 reads=[("qeT", d), ("gSbf", d)], writes=["pb0"])
                        S.op("act", lambda e: e.copy(out=go_sb[d][:], in_=PB[0][0:64, 0:384]), reads=["pb0"], writes=[("go_sb", d)])
                        S.dma("pool", lambda e: e.dma_start(out=OFB[d][n * 64:(n + 1) * 64, 0:384], in_=go_sb[d][:]),
                              reads=[("go_sb", d)], writes=[("OFB", d)])
                        yield
                        kv_ps = PB[1][0:64, 0:384].rearrange("p (h v) -> p h v", h=4)
                        for h in range(4):
                            S.op("pe", lambda e, h=h: e.matmul(kv_ps[:, h, :], lhsT=ktl[d][:, h * 64:(h + 1) * 64],
                                                               rhs=kvg[:, ci, 256 + h * 96:256 + (h + 1) * 96], start=True, stop=True),
                                 reads=[("ktl", d), kkv_], writes=["pb1"])
                        dcol = 63 if d == 0 else 0
                        S.op("pool", lambda e: e.tensor_tensor(out=gS32[d][:], in0=gS32[d][:],
                                                               in1=ebT[d][:, :, dcol:dcol + 1].broadcast_to([64, 4, 96]), op=ALU.mult),
                             reads=[("gS32", d), ("ebT", d)], writes=[("gS32", d)])
                        S.op("dve", lambda e: e.tensor_tensor(out=gS32[d][:], in0=gS32[d][:], in1=kv_ps, op=ALU.add),
                             reads=[("gS32", d), "pb1"], writes=[("gS32", d)])
                        S.op("act", lambda e: e.copy(out=gSbf[d][:], in_=gS32[d][:]), reads=[("gS32", d)], writes=[("gSbf", d)])

                    def gla_step(ns):
                        two_ = range(2)
                        info = []
                        for d in two_:
                            grp, ci = divmod(ns[d], 4)
                            gs = grp % 2
                            info.append((glaq[d][gs], glak[d][gs], glakv[d][gs], glala[d][gs], ("glaq", d, gs), ("glak", d, gs), ("glakv", d, gs),
                                         ("glala", d, gs), slice(ci * 64, (ci + 1) * 64), ci))
                        mIl, mSl = (LE, GE), (GT, LT)
                        mI2 = masks5[:, 0:2, :, :].rearrange("p t r c -> p (t r) c")
                        v64 = lambda t_: t_[0:64, :].rearrange("p (u c) -> p u c", u=8)
                        bT_ps = v64(PB[0])
                        for d in two_:
                            lag, kla_, ci = info[d][3], info[d][7], info[d][9]
                            for h in range(4):
                                u = d * 4 + h
                                S.op("pe", lambda e, u=u, h=h, lag=lag, ci=ci, d=d: e.matmul(bT_ps[:, u, :], lhsT=lag[:, ci, h * 64:(h + 1) * 64],
                                                                                              rhs=masks5[:, mIl[d], 0, :], start=True, stop=True),
                                     reads=[kla_, "c_masks"], writes=["pb0"])
                            S.op("pe", lambda e, lag=lag, ci=ci, d=d: e.matmul(PB[1][0:64, d * 256:(d + 1) * 256], lhsT=masks5[:, mSl[d], 0, :], rhs=lag[:, ci, :],
                                                                              start=True, stop=True), reads=[kla_, "c_masks"], writes=["pb1"])
                        S.op("act", lambda e: e.activation(out=ebT8[:], in_=bT_ps, func=AF.Exp), reads=["pb0"], writes=["ebT8"])
                        S.op("act", lambda e: e.activation(out=enbT8[:], in_=bT_ps, func=AF.Exp, scale=-1.0), reads=["pb0"], writes=["enbT8"])
                        S.op("act", lambda e: e.activation(out=erem8[:], in_=PB[1][0:64, :], func=AF.Exp), reads=["pb1"], writes=["erem8"])
                        yield
                        for d in two_:
                            qg, kg, kvg, kq_, kk_, kkv_, cs, ci = info[d][0], info[d][1], info[d][2], info[d][4], info[d][5], info[d][6], info[d][8], info[d][9]
                            S.op("dve", lambda e, d=d, qg=qg, cs=cs: e.tensor_tensor(out=qeT8[:, d * 4:(d + 1) * 4, :], in0=qg[:, :, cs],
                                                                                     in1=ebT8[:, d * 4:(d + 1) * 4, :], op=ALU.mult),
                                 reads=[kq_, "ebT8"], writes=["qeT8"])
                            S.op("pool", lambda e, d=d, kg=kg, cs=cs: e.tensor_tensor(out=keT8[:, d * 4:(d + 1) * 4, :], in0=kg[:, :, cs],
                                                                                      in1=enbT8[:, d * 4:(d + 1) * 4, :], op=ALU.mult),
                                 reads=[kk_, "enbT8"], writes=["keT8"])
                            S.op("dve", lambda e, d=d, kvg=kvg, ci=ci: e.tensor_tensor(out=ktl8[:, d * 256:(d + 1) * 256], in0=kvg[:, ci, 0:256],
                                                                                       in1=erem8[:, d * 256:(d + 1) * 256], op=ALU.mult),
                                 reads=[kkv_, "erem8"], writes=["ktl8"])
                        yield
                        AT_ps = v64(PB[0])
                        for u in range(8):
                            S.op("pe", lambda e, u=u: e.matmul(AT_ps[:, u, :], lhsT=keT8[:, u, :], rhs=qeT8[:, u, :], start=True, stop=True),
                                 reads=["keT8", "qeT8"], writes=["pb0"])
                        S.op("dve", lambda e: e.tensor_tensor(out=ATm8[:], in0=AT_ps, in1=mI2, op=ALU.mult), reads=["pb0", "c_masks"], writes=["ATm8"])
                        yield
                        for d in two_:
                            kvg, kkv_, ci = info[d][2], info[d][6], info[d][9]
                            o_ps = PB[d][0:64, 0:384].rearrange("p (h v) -> p h v", h=4)
                            for h in range(4):
                                u = d * 4 + h
                                S.op("pe", lambda e, u=u, h=h, kvg=kvg, ci=ci, o_ps=o_ps: e.matmul(o_ps[:, h, :], lhsT=ATm8[:, u, :],
                                                                                                 rhs=kvg[:, ci, 256 + h * 96:256 + (h + 1) * 96],
                                                                                                 start=True, stop=False),
                                     reads=["ATm8", kkv_], writes=[f"pb{d}"])
                                S.op("pe", lambda e, u=u, h=h, o_ps=o_ps: e.matmul(o_ps[:, h, :], lhsT=qeT8[:, u, :], rhs=gSbf8[:, u, :], start=False, stop=True),
                                     reads=["qeT8", "gSbf8"], writes=[f"pb{d}"])
                            S.op("act", lambda e, d=d: e.copy(out=go_sb[d][:], in_=PB[d][0:64, 0:384]), reads=[f"pb{d}"], writes=[("go_sb", d)])
                            S.dma("pool", lambda e, d=d: e.dma_start(out=OFB[d][ns[d] * 64:(ns[d] + 1) * 64, 0:384], in_=go_sb[d][:]),
                                  reads=[("go_sb", d)], writes=[("OFB", d)])
                        yield
                        for d in two_:
                            kvg, kkv_, ci = info[d][2], info[d][6], info[d][9]
                            dcol = 63 if d == 0 else 0
                            kv_ps = PB[d][0:64, 0:384].rearrange("p (h v) -> p h v", h=4)
                            for h in range(4):
                                S.op("pe", lambda e, d=d, h=h, kvg=kvg, ci=ci, kv_ps=kv_ps: e.matmul(kv_ps[:, h, :], lhsT=ktl8[:, d * 256 + h * 64:d * 256 + (h + 1) * 64],
                                                                                                   rhs=kvg[:, ci, 256 + h * 96:256 + (h + 1) * 96], start=True, stop=True),
                                     reads=["ktl8", kkv_], writes=[f"pb{d}"])
                            for h in range(4):
                                u = d * 4 + h
                                S.op("dve", lambda e, u=u, h=h, dcol=dcol, kv_ps=kv_ps: e.scalar_tensor_tensor(
                                    out=gS328[:, u, :], in0=gS328[:, u, :], scalar=ebT8[:, u, dcol:dcol + 1], in1=kv_ps[:, h, :], op0=ALU.mult, op1=ALU.add),
                                    reads=["gS328", "ebT8", f"pb{d}"], writes=["gS328"])
                        S.op("act", lambda e: e.copy(out=gSbf8[:], in_=gS328[:]), reads=["gS328"], writes=["gSbf8"])
                        yield


                    def gdn_step(ns):
                        import os as _os
                        two_ = range(2)
                        info = []
                        for d in two_:
                            grp, ci = divmod(ns[d], 4)
                            gs = grp % 2
                            info.append((gdnc2[d][gs], gdngb[d][gs], ("gdnc2", d, gs), ("gdngb", d, gs), slice(ci * 64, (ci + 1) * 64), ci))
                        m8 = lambda a, b_: masks5[:, a:b_, :, :].rearrange("p t r c -> p (t r) c")
                        mI2, mS2, mSo2 = m8(0, 2), m8(2, 4), m8(3, 5)
                        mIl, mSl = (LE, GE), (GT, LT)
                        b8 = lambda ap, w, p=64: ap.unsqueeze(2).broadcast_to([p, 8, w])
                        idb8 = ident_f[:].unsqueeze(1).broadcast_to([64, 8, 64])
                        v64 = lambda t_: t_[0:64, :].rearrange("p (u c) -> p u c", u=8)
                        for d in two_:
                            gbg, kgb, ci = info[d][1], info[d][3], info[d][5]
                            S.op("pool", lambda e, d=d, gbg=gbg, ci=ci: e.tensor_copy(
                                out=gb8[:].rearrange("p (t z h) -> p t z h", t=2, z=2)[:, :, d, :],
                                in_=gbg[:, ci, :].rearrange("p (t z h) -> p t z h", t=2, z=2)[:, :, d, :]), reads=[kgb], writes=["gb8"])
                        beta8, g8 = gb8[:, 0:8], gb8[:, 8:16]
                        for d in two_:
                            c2g, kc2, cs = info[d][0], info[d][2], info[d][4]
                            tr_ps = PB[2 + d][0:64, 0:384].bitcast(BF16).rearrange("p (u x) -> p u x", u=8)
                            for u in range(8):
                                S.op("pe", lambda e, u=u, c2g=c2g, cs=cs, tr_ps=tr_ps: e.transpose(out=tr_ps[:, u, :], in_=c2g[:, 4 + u, cs],
                                                                                                 identity=ident_bf[0:96, 0:96]),
                                     reads=[kc2, "c_idbf"], writes=[f"pb{2 + d}"])
                            S.op("act", lambda e, d=d, tr_ps=tr_ps: e.copy(out=K8[:, d * 4:(d + 1) * 4, :], in_=tr_ps[:, 0:4, :]), reads=[f"pb{2 + d}"], writes=["K8"])
                            S.op("act", lambda e, d=d, tr_ps=tr_ps: e.copy(out=V8[:, d * 4:(d + 1) * 4, :], in_=tr_ps[:, 4:8, :]), reads=[f"pb{2 + d}"], writes=["V8"])
                        yield
                        for d in two_:
                            S.op("pe", lambda e, d=d: e.matmul(PB[3][0:64, 384 + d * 4:388 + d * 4], lhsT=masks5[:, mIl[d], 0, :], rhs=g8[:, d * 4:(d + 1) * 4],
                                                               start=True, stop=True), reads=["c_masks", "gb8"], writes=["pb3"])
                            S.op("pe", lambda e, d=d: e.matmul(PB[3][0:64, 392 + d * 4:396 + d * 4], lhsT=masks5[:, mSl[d], 0, :], rhs=g8[:, d * 4:(d + 1) * 4],
                                                               start=True, stop=True), reads=["c_masks", "gb8"], writes=["pb3"])
                        S.op("pe", lambda e: e.matmul(PB[3][0:96, 400:408], lhsT=ones_f[:, :], rhs=g8, start=True, stop=True),
                             reads=["c_onesf", "gb8"], writes=["pb3"])
                        S.op("act", lambda e: e.activation(out=eg16[:], in_=PB[3][0:64, 384:400], func=AF.Exp), reads=["pb3"], writes=["eg16"])
                        S.op("act", lambda e: e.activation(out=dec96[:], in_=PB[3][0:96, 400:408], func=AF.Exp), reads=["pb3"], writes=["dec96"])
                        yield
                        kk_ps, qk_ps = v64(PB[4]), v64(PB[5])
                        for d in two_:
                            c2g, kc2, cs = info[d][0], info[d][2], info[d][4]
                            for h in range(4):
                                u = d * 4 + h
                                S.op("pe", lambda e, u=u, h=h, c2g=c2g, cs=cs: e.matmul(kk_ps[:, u, :], lhsT=c2g[:, 4 + h, cs], rhs=c2g[:, 4 + h, cs],
                                                                                        start=True, stop=True), reads=[kc2], writes=["pb4"])
                                S.op("pe", lambda e, u=u, h=h, c2g=c2g, cs=cs: e.matmul(qk_ps[:, u, :], lhsT=c2g[:, 4 + h, cs], rhs=c2g[:, h, cs],
                                                                                        start=True, stop=True), reads=[kc2], writes=["pb5"])
                        yield
                        S.op("dve", lambda e: e.tensor_tensor(out=R8[:], in0=mI2, in1=b8(g8, 64), op=ALU.mult), reads=["c_masks", "gb8"], writes=["R8"])
                        S.op("dve", lambda e: e.tensor_tensor(out=kkN[:], in0=kk_ps, in1=mS2, op=ALU.mult), reads=["pb4", "c_masks"], writes=["kkN"])
                        S.op("dve", lambda e: e.tensor_tensor(out=kkM[:], in0=kk_ps, in1=mSo2, op=ALU.mult), reads=["pb4", "c_masks"], writes=["kkM"])
                        S.op("dve", lambda e: e.tensor_tensor(out=qkI[:], in0=qk_ps, in1=mI2, op=ALU.mult), reads=["pb5", "c_masks"], writes=["qkI"])
                        Rflat = R8[:].rearrange("p u c -> p (u c)")
                        yield
                        for d in two_:
                            S.op("pe", lambda e, d=d: e.matmul(PB[2][0:64, d * 256:(d + 1) * 256], lhsT=masks5[:, mSl[d], 0, :], rhs=Rflat[:, d * 256:(d + 1) * 256],
                                                               start=True, stop=True), reads=["c_masks", "R8"], writes=["pb2"])
                        for u in range(8):
                            S.op("pe", lambda e, u=u: e.matmul(PB[3][0:64, u * 64:(u + 1) * 64], lhsT=R8[:, u, :], rhs=masks5[:, mSl[u // 4], 0, :],
                                                               start=True, stop=True), reads=["c_masks", "R8"], writes=["pb3"])
                        S.op("act", lambda e: e.activation(out=decT[:], in_=v64(PB[2]), func=AF.Exp), reads=["pb2"], writes=["decT"])
                        S.op("act", lambda e: e.activation(out=decD[:], in_=v64(PB[3]), func=AF.Exp), reads=["pb3"], writes=["decD"])
                        yield
                        S.op("pe", lambda e: e.matmul(PB[6][0:96, :], lhsT=ones_f[:, :], rhs=Rflat, start=True, stop=True),
                             reads=["c_onesf", "R8"], writes=["pb6"])
                        S.op("act", lambda e: e.activation(out=EGB8[:], in_=PB[6][0:96, :].rearrange("p (u c) -> p u c", u=8), func=AF.Exp),
                             reads=["pb6"], writes=["EGB8"])
                        yield
                        S.op("pool", lambda e: e.tensor_tensor(out=DB8[:], in0=idb8, in1=b8(beta8, 64), op=ALU.mult), reads=["c_idf", "gb8"], writes=["DB8"])
                        S.op("pe", lambda e: e.matmul(PB[7][0:64, :], lhsT=ones_bf[0:64, 0:64], rhs=DB8[:].rearrange("p u c -> p (u c)"), start=True, stop=True),
                             reads=["c_onesbf", "DB8"], writes=["pb7"])
                        yield
                        S.op("dve", lambda e: e.tensor_tensor(out=kkN[:], in0=kkN[:], in1=decD[:], op=ALU.mult), reads=["kkN", "decD"], writes=["kkN"])
                        S.op("pool", lambda e: e.tensor_tensor(out=Nm[0][:], in0=kkN[:], in1=b8(beta8, 64), op=ALU.mult), reads=["kkN", "gb8"], writes=[("Nm", 0)])
                        S.op("dve", lambda e: e.tensor_tensor(out=kkM[:], in0=kkM[:], in1=decT[:], op=ALU.mult), reads=["kkM", "decT"], writes=["kkM"])
                        S.op("dve", lambda e: e.tensor_tensor(out=Mm[0][:], in0=kkM[:], in1=v64(PB[7]), op=ALU.mult), reads=["kkM", "pb7"], writes=[("Mm", 0)])
                        S.op("pool", lambda e: e.tensor_tensor(out=Pm8[:], in0=idb8, in1=Mm[0][:], op=ALU.subtract), reads=[("Mm", 0), "c_idf"], writes=["Pm8"])
                        S.op("pool", lambda e: e.tensor_tensor(out=qkm8[0:64, :, :], in0=qkI[:], in1=decT[:], op=ALU.mult), reads=["qkI", "decT"], writes=["qkm8"])
                        yield
                        S.op("pool", lambda e: e.tensor_tensor(out=vb8[:], in0=V8[:], in1=b8(beta8, 96), op=ALU.mult), reads=["V8", "gb8"], writes=["vb8"])
                        S.op("dve", lambda e: e.tensor_tensor(out=bg8[:], in0=beta8, in1=eg16[:, 0:8], op=ALU.mult), reads=["gb8", "eg16"], writes=["bg8"])
                        S.op("pool", lambda e: e.tensor_tensor(out=kbg8[:], in0=K8[:], in1=b8(bg8[:], 96), op=ALU.mult), reads=["K8", "bg8"], writes=["kbg8"])
                        S.op("pool", lambda e: e.tensor_tensor(out=kte8[:], in0=K8[:], in1=b8(eg16[:, 8:16], 96), op=ALU.mult), reads=["K8", "eg16"], writes=["kte8"])
                        for d in two_:
                            c2g, kc2, cs = info[d][0], info[d][2], info[d][4]
                            S.op("pool", lambda e, d=d, c2g=c2g, cs=cs: e.tensor_tensor(out=qdT8[:, d * 4:(d + 1) * 4, :], in0=c2g[:, 0:4, cs],
                                                                                       in1=EGB8[:, d * 4:(d + 1) * 4, :], op=ALU.mult),
                                 reads=[kc2, "EGB8"], writes=["qdT8"])
                        yield
                        sqM, sqN, pd = v64(PB[4]), v64(PB[5]), v64(PB[6])

                        def emit_sq(lev, cur, nxt):
                            lastlev = (lev == 4)
                            for u in range(8):
                                if not lastlev:
                                    S.op("pe", lambda e, u=u: e.matmul(sqM[:, u, :], lhsT=Nm[cur][:, u, :], rhs=Mm[cur][:, u, :], start=True, stop=True),
                                         reads=[("Nm", cur), ("Mm", cur)], writes=["pb4"])
                                S.op("pe", lambda e, u=u: e.matmul(sqN[:, u, :], lhsT=Mm[cur][:, u, :], rhs=Nm[cur][:, u, :], start=True, stop=True),
                                     reads=[("Nm", cur), ("Mm", cur)], writes=["pb5"])
                            S.op("act", lambda e: e.copy(out=Nm[nxt][:], in_=sqN), reads=["pb5"], writes=[("Nm", nxt)])
                            if not lastlev:
                                S.op("dve", lambda e: e.tensor_copy(out=Mm[nxt][:], in_=sqM), reads=["pb4"], writes=[("Mm", nxt)])

                        def emit_pd(lev, nb_):
                            for u in range(8):
                                S.op("pe", lambda e, u=u: e.matmul(pd[:, u, :], lhsT=Nm[nb_][:, u, :], rhs=Pm8[:, u, :], start=True, stop=True),
                                     reads=[("Nm", nb_), "Pm8"], writes=["pb6"])
                            if lev < 4:
                                S.op("dve", lambda e: e.tensor_tensor(out=Pm8[:], in0=Pm8[:], in1=pd, op=ALU.add), reads=["Pm8", "pb6"], writes=["Pm8"])
                            else:
                                S.op("dve", lambda e: e.tensor_tensor(out=TiT8[:], in0=Pm8[:], in1=pd, op=ALU.add), reads=["Pm8", "pb6"], writes=["TiT8"])

                        emit_sq(0, 0, 1)
                        yield
                        for lev in range(5):
                            if lev < 4:
                                emit_sq(lev + 1, (lev + 1) % 2, (lev + 2) % 2)
                            emit_pd(lev, (lev + 1) % 2)
                            yield

                        wT_ps = PB[7][0:96, :].rearrange("p (u c) -> p u c", u=8)
                        for d in two_:
                            u_ps = PB[2 + d][0:64, 0:384].rearrange("p (h v) -> p h v", h=4)
                            for h in range(4):
                                u = d * 4 + h
                                S.op("pe", lambda e, u=u, h=h, u_ps=u_ps: e.matmul(u_ps[:, h, :], lhsT=TiT8[:, u, :], rhs=vb8[:, u, :], start=True, stop=True),
                                     reads=["TiT8", "vb8"], writes=[f"pb{2 + d}"])
                            S.op("act", lambda e, d=d, u_ps=u_ps: e.copy(out=u8[:, d * 4:(d + 1) * 4, :], in_=u_ps), reads=[f"pb{2 + d}"], writes=["u8"])
                        for u in range(8):
                            S.op("pe", lambda e, u=u: e.matmul(wT_ps[:, u, :], lhsT=kbg8[:, u, :], rhs=TiT8[:, u, :], start=True, stop=True),
                                 reads=["TiT8", "kbg8"], writes=["pb7"])
                        S.op("act", lambda e: e.copy(out=wT8[:], in_=wT_ps), reads=["pb7"], writes=["wT8"])
                        yield
                        for d in two_:
                            ws_ps = PB[4 + d][0:64, 0:384].rearrange("p (h v) -> p h v", h=4)
                            for h in range(4):
                                u = d * 4 + h
                                S.op("pe", lambda e, u=u, h=h, ws_ps=ws_ps: e.matmul(ws_ps[:, h, :], lhsT=wT8[:, u, :], rhs=dSbf8[:, u, :], start=True, stop=True),
                                     reads=["wT8", "dSbf8"], writes=[f"pb{4 + d}"])
                            S.op("dve", lambda e, d=d, ws_ps=ws_ps: e.tensor_tensor(out=vnew8[0:64, d * 4:(d + 1) * 4, :], in0=u8[:, d * 4:(d + 1) * 4, :], in1=ws_ps,
                                                                                    op=ALU.subtract), reads=["u8", f"pb{4 + d}"], writes=["vnew8"])
                        yield
                        for d in two_:
                            o_ps = PB[2 + d][0:64, 0:384].rearrange("p (h v) -> p h v", h=4)
                            for h in range(4):
                                u = d * 4 + h
                                S.op("pe", lambda e, u=u, h=h, o_ps=o_ps: e.matmul(o_ps[:, h, :], lhsT=qdT8[:, u, :], rhs=dSbf8[:, u, :], start=True, stop=False),
                                     reads=["qdT8", "dSbf8"], writes=[f"pb{2 + d}"])
                                S.op("pe", lambda e, u=u, h=h, o_ps=o_ps: e.matmul(o_ps[:, h, :], lhsT=qkm8[:, u, :], rhs=vnew8[:, u, :], start=False, stop=True),
                                     reads=["qkm8", "vnew8"], writes=[f"pb{2 + d}"])
                            S.op("act", lambda e, d=d: e.copy(out=do_sb[d][:], in_=PB[2 + d][0:64, 0:384]), reads=[f"pb{2 + d}"], writes=[("do_sb", d)])
                            S.dma("pool", lambda e, d=d: e.dma_start(out=OFB[d][ns[d] * 64:(ns[d] + 1) * 64, 384:768], in_=do_sb[d][:]),
                                  reads=[("do_sb", d)], writes=[("OFB", d)])
                        for d in two_:
                            kv_ps = PB[6 + d][0:96, 0:384].rearrange("p (h v) -> p h v", h=4)
                            for h in range(4):
                                u = d * 4 + h
                                S.op("pe", lambda e, u=u, h=h, kv_ps=kv_ps: e.matmul(kv_ps[:, h, :], lhsT=kte8[:, u, :], rhs=vnew8[0:64, u, :], start=True, stop=True),
                                     reads=["kte8", "vnew8"], writes=[f"pb{6 + d}"])
                            for h in range(4):
                                u = d * 4 + h
                                S.op("dve", lambda e, u=u, h=h, kv_ps=kv_ps: e.scalar_tensor_tensor(
                                    out=dS328[:, u, :], in0=dS328[:, u, :], scalar=dec96[:, u:u + 1], in1=kv_ps[:, h, :], op0=ALU.mult, op1=ALU.add),
                                    reads=["dS328", "dec96", f"pb{6 + d}"], writes=["dS328"])
                        S.op("act", lambda e: e.copy(out=dSbf8[:], in_=dS328[:]), reads=["dS328"], writes=["dSbf8"])


                    load_group(0, 0)
                    load_group(1, 15)
                    import os as _os
                    for j in range(int(_os.environ.get('K_NSTEPS', NCH))):
                        nf, nbk = j, NCH - 1 - j
                        if j % 4 == 0 and j + 4 < NCH:
                            load_group(0, j // 4 + 1)
                            load_group(1, 15 - (j // 4 + 1))
                        fillers = [gla_step((nf, nbk))] if "gla" in parts else []
                        main = gdn_step((nf, nbk)) if "gdn" in parts else iter(())
                        fi = 0
                        if not _osm.environ.get('K_ILV'):
                            for _ in range(int(_osm.environ.get('K_PRE', 0))):
                                next(main, "done")
                            for f_ in fillers:
                                for _ in f_:
                                    pass
                            fillers = []
                        while True:
                            main_alive = next(main, "done") != "done"
                            adv = False
                            while fi < len(fillers):
                                if next(fillers[fi], "done") != "done":
                                    adv = True
                                    break
                                fi += 1
                            if not main_alive and not adv:
                                break
                    S.barrier()

            mark(f'L{l} p3 done')
            if 4 in passes:
                with contextlib.ExitStack() as ps:
                    sbp = lambda n, s, dt: ps.enter_context(nc.sbuf_tensor(f"sbD{l}_" + n, s, dt))
                    two = range(2)
                    of_t = [sbp(f"of_t{i}", [128, 768], F32) for i in two]
                    ob_t = [sbp(f"ob_t{i}", [128, 768], F32) for i in two]
                    gt_t = [sbp(f"gt_t{i}", [128, 768], BF16) for i in two]
                    oc_all = [sbp(f"oc_all{i}", [128, 1024], BF16) for i in two]
                    x_t = [sbp(f"x_t{i}", [128, D], F32) for i in two]
                    xn = [sbp(f"xn{i}", [128, D], F32) for i in two]
                    osq = [sbp(f"osq{i}", [128, 768], F32) for i in two]
                    ss8 = [sbp(f"ss8{i}", [128, 8], F32) for i in two]
                    ocT = [sbp(f"ocT{i}", [128, 8, 128], BF16) for i in two]
                    junk4 = [sbp(f"junk4{i}", [128, D], BF16) for i in two]
                    ss4 = [sbp(f"ss4{i}", [128, 1], F32) for i in two]
                    def p4_tile(tt):
                        r0 = tt * 128
                        sl = tt % 2
                        pbase = 3 * (tt % 2)
                        S.dma("sp", lambda e, sl=sl, r0=r0: e.dma_start(out=of_t[sl][:], in_=OFB[0][r0:r0 + 128, :]), reads=[("OFB", 0)], writes=[("of_t", sl)])
                        S.dma("sp", lambda e, sl=sl, r0=r0: e.dma_start(out=ob_t[sl][:], in_=OFB[1][r0:r0 + 128, :]), reads=[("OFB", 1)], writes=[("ob_t", sl)])
                        S.dma("sp", lambda e, sl=sl, r0=r0: e.dma_start(out=gt_t[sl][:], in_=GATE[r0:r0 + 128, 0:768]), reads=["GATE"], writes=[("gt_t", sl)])
                        S.dma("sp", lambda e, sl=sl, r0=r0: e.dma_start(out=oc_all[sl][:, 768:1024], in_=OC3[r0:r0 + 128, :]), reads=["OC3"], writes=[("oc_all", sl)])
                        S.dma("sp", lambda e, sl=sl, r0=r0: e.dma_start(out=x_t[sl][:], in_=xsrc[r0:r0 + 128, :]), reads=["XSRC%d" % l], writes=[("x_t", sl)])
                        yield
                        S.op("dve", lambda e, sl=sl: e.tensor_tensor(out=of_t[sl][:], in0=of_t[sl][:], in1=ob_t[sl][:], op=ALU.add),
                             reads=[("of_t", sl), ("ob_t", sl)], writes=[("of_t", sl)])
                        S.op("pool", lambda e, sl=sl: e.tensor_tensor(out=osq[sl][:], in0=of_t[sl][:], in1=of_t[sl][:], op=ALU.mult),
                             reads=[("of_t", sl)], writes=[("osq", sl)])
                        S.op("dve", lambda e, sl=sl: e.tensor_reduce(out=ss8[sl][:], in_=osq[sl][:].rearrange("p (h v) -> p h v", h=8), axis=AX.X, op=ALU.add),
                             reads=[("osq", sl)], writes=[("ss8", sl)])
                        yield
                        rms_rstd(ss8[sl][:], ss8[sl][:], 96, ("ss8", sl), ("ss8", sl))
                        yield
                        S.op("dve", lambda e, sl=sl: e.tensor_tensor(out=of_t[sl][:].rearrange("p (h v) -> p h v", h=8),
                                                                     in0=of_t[sl][:].rearrange("p (h v) -> p h v", h=8),
                                                                     in1=ss8[sl][:].unsqueeze(2).broadcast_to([128, 8, 96]), op=ALU.mult),
                             reads=[("of_t", sl), ("ss8", sl)], writes=[("of_t", sl)])
                        S.op("pool", lambda e, sl=sl: e.tensor_tensor(out=of_t[sl][:], in0=of_t[sl][:], in1=ohnorm[:], op=ALU.mult),
                             reads=[("of_t", sl), "ohnorm"], writes=[("of_t", sl)])
                        S.op("dve", lambda e, sl=sl: e.tensor_tensor(out=oc_all[sl][:, 0:768], in0=of_t[sl][:], in1=gt_t[sl][:], op=ALU.mult),
                             reads=[("of_t", sl), ("gt_t", sl)], writes=[("oc_all", sl)])
                        yield
                        ptr = PB[pbase][:, 0:512].bitcast(BF16).rearrange("p (c t) -> p c t", c=8)
                        for c in range(8):
                            S.op("pe", lambda e, c=c, sl=sl: e.transpose(out=ptr[:, c, :], in_=oc_all[sl][:, c * 128:(c + 1) * 128], identity=ident_bf[:]),
                                 reads=[("oc_all", sl), "c_idbf"], writes=[f"pb{pbase}"])
                        S.op("act", lambda e, sl=sl: e.copy(out=ocT[sl][:], in_=ptr), reads=[f"pb{pbase}"], writes=[("ocT", sl)])
                        yield
                        for hf in range(2):
                            bk = pbase + 1 + hf
                            for c in range(8):
                                S.op("pe", lambda e, c=c, hf=hf, bk=bk, sl=sl: e.matmul(PB[bk][:, :], lhsT=ocT[sl][:, c, :], rhs=w_out_sb[:, c, hf * 512:(hf + 1) * 512],
                                                                                 start=(c == 0), stop=(c == 7)), reads=[("ocT", sl), "w_out_sb"], writes=[f"pb{bk}"])
                            S.op("dve", lambda e, hf=hf, bk=bk, sl=sl: e.tensor_tensor(out=xn[sl][:, hf * 512:(hf + 1) * 512], in0=x_t[sl][:, hf * 512:(hf + 1) * 512],
                                                                                       in1=PB[bk][:, :], op=ALU.add),
                                 reads=[("x_t", sl), f"pb{bk}"], writes=[("xn", sl)])
                        yield
                        if not last:
                            S.dma("pool", lambda e, sl=sl, r0=r0: e.dma_start(out=X1[r0:r0 + 128, :], in_=xn[sl][:]), reads=[("xn", sl)], writes=["XSRC%d" % (l + 1)])
                        else:
                            S.op("act", lambda e, sl=sl: e.activation(out=junk4[sl][:], in_=xn[sl][:], func=AF.Square, accum_out=ss4[sl][:]),
                                 reads=[("xn", sl)], writes=[("junk4", sl), ("ss4", sl)])
                            rms_rstd(ss4[sl][:], ss4[sl][:], D, ("ss4", sl), ("ss4", sl))
                            S.op("dve", lambda e, sl=sl: e.tensor_scalar(out=xn[sl][:], in0=xn[sl][:], scalar1=ss4[sl][:], scalar2=None, op0=ALU.mult),
                                 reads=[("xn", sl), ("ss4", sl)], writes=[("xn", sl)])
                            S.op("pool", lambda e, sl=sl: e.tensor_tensor(out=xn[sl][:], in0=xn[sl][:], in1=fnorm[:], op=ALU.mult),
                                 reads=[("xn", sl), "c_fnorm"], writes=[("xn", sl)])
                            S.dma("pool", lambda e, sl=sl, r0=r0: e.dma_start(out=out[r0:r0 + 128, :], in_=xn[sl][:]), reads=[("xn", sl)], writes=["out"])
                    active = []
                    nxt_tt = 0
                    while True:
                        while len(active) < int(_osm.environ.get('K_P4W', 2)) and nxt_tt < T // 128:
                            active.append(p4_tile(nxt_tt))
                            nxt_tt += 1
                        if not active:
                            break
                        for g_ in list(active):
                            if next(g_, "done") == "done":
                                active.remove(g_)
                    S.barrier()

        for l_ in range(nlayers):
            emit_layer(l_)
        S.barrier()
        S.finish()
        mark("end")
        print("MARKS", marks)
        print("ops", S.n_ops, "waits", S.n_waits, {e: len(S.ops[e]) for e in S.ENGS})
    return nc

def host_inputs(inputs, b):
    f32 = np.float32
    g = lambda k: np.asarray(inputs[k], dtype=f32)
    d = {}
    d["x"] = np.ascontiguousarray(g("x")[b])
    d["mem"] = np.ascontiguousarray(g("mem")[b])
    d["w_in"] = g("w_in")
    d["w_out"] = g("w_out")
    d["xa_w_kv"] = g("xa_w_kv")
    d["normw_pc"] = np.ascontiguousarray(g("norm_w").reshape(NL, 8, 128).transpose(0, 2, 1))
    d["memnormw_pc"] = np.ascontiguousarray(g("mem_norm_w").reshape(NL, 8, 128).transpose(0, 2, 1))
    d["w2"] = np.ascontiguousarray(g("gla_w2").transpose(0, 2, 1, 3))
    d["bias_bc"] = np.ascontiguousarray(np.broadcast_to(g("gla_b").reshape(NL, 1, 512), (NL, 128, 512)))
    ohn = np.concatenate([np.tile(g("gla_norm_w"), (1, 4)), np.tile(g("gdn_norm_w"), (1, 4))], axis=1)
    d["ohnorm_bc"] = np.ascontiguousarray(np.broadcast_to(ohn.reshape(NL, 1, 768), (NL, 128, 768)))
    d["xanorm_bc"] = np.ascontiguousarray(np.broadcast_to(np.tile(g("xa_norm_w"), (1, 4)).reshape(NL, 1, 256), (NL, 128, 256)))
    d["convw"] = np.ascontiguousarray(g("gdn_conv_w").reshape(NL, 12, 96, 5).transpose(0, 2, 1, 3))
    d["alog_bc"] = np.ascontiguousarray(np.broadcast_to(g("gdn_a_log").reshape(NL, 1, 8), (NL, 128, 8)))
    d["dtb_bc"] = np.ascontiguousarray(np.broadcast_to(g("gdn_dt_bias").reshape(NL, 1, 8), (NL, 128, 8)))
    d["fnorm_bc"] = np.ascontiguousarray(np.broadcast_to(g("final_norm_w").reshape(1, D), (128, D)))
    d["ident_bf"] = np.eye(128, dtype=f32).astype(ml_dtypes.bfloat16)
    d["ident_f"] = np.eye(64, dtype=f32)
    p = np.arange(64)[:, None]
    q = np.arange(64)[None, :]
    m4 = np.stack([(p <= q), (p >= q), (p > q), (p < q), (p > q)]).astype(f32)
    d["masks"] = np.ascontiguousarray(np.broadcast_to(m4.transpose(1, 0, 2)[:, :, None, :], (64, 5, 4, 64)))
    d["ones_f"] = np.ones((64, 96), f32)
    d["ones_bf"] = np.ones((96, 96), f32).astype(ml_dtypes.bfloat16)
    return d


_NC_CACHE = {}


def kernel(**inputs):
    if "nc" not in _NC_CACHE:
        _NC_CACHE["nc"] = build()
    nc = _NC_CACHE["nc"]
    in_maps = [host_inputs(inputs, b) for b in range(8)]
    res = run_bass_kernel_spmd(nc, in_maps, core_ids=list(range(8)))
    return np.stack([np.asarray(r["out"], dtype=np.float32) for r in res.results], axis=0)
```

```python
import contextlib
import numpy as np
import ml_dtypes
import concourse.bass as bass
import concourse.mybir as mybir
from concourse.bass_utils import run_bass_kernel_spmd

F32 = mybir.dt.float32
BF16 = mybir.dt.bfloat16
AF = mybir.ActivationFunctionType
ALU = mybir.AluOpType
AX = mybir.AxisListType

T = 4096
D = 1024
INW = 3376
NL = 2
MEM = 256
EPS = 1e-6
NCH = T // 64
import os as _osm
NDT = BF16 if _osm.environ.get('K_NDT', 'bf16') == 'bf16' else F32

EPOCH = 30000
N_DMA_SEMS = 32


class Sched:
    ENGS = ("pe", "act", "dve", "pool", "sp")

    def __init__(self, nc, stack):
        self.nc = nc
        self.stack = stack
        self.ops = {e: [] for e in self.ENGS}
        self.cnt = {e: 0 for e in self.ENGS}
        self.epoch = {e: 0 for e in self.ENGS}
        self.sems = {}
        for e in self.ENGS:
            self._new_eng_sem(e)
        self.dma_sems = []
        for i in range(2 * N_DMA_SEMS):
            s = stack.enter_context(nc.semaphore(f"dma{i}"))
            self.dma_sems.append([s, 0])
        self.dma_rr = {"sp": 0, "pool": 0}
        self.known = {e: {} for e in self.ENGS}
        self.last_w = {}
        self.readers = {}
        self.n_waits = 0
        self.n_ops = 0

    def _new_eng_sem(self, e):
        s = self.stack.enter_context(self.nc.semaphore(f"s_{e}_{self.epoch[e]}"))
        self.sems[(e, self.epoch[e])] = s

    def _deps(self, reads, writes):
        deps = []
        for b in reads:
            if b in self.last_w:
                deps.append(self.last_w[b])
        for b in writes:
            if b in self.last_w:
                deps.append(self.last_w[b])
            deps.extend(self.readers.get(b, {}).items())
        return deps

    def _emit_waits(self, eng, deps):
        need = {}
        for (sk, val) in deps:
            if sk[0] == eng and eng == "pe":
                continue
            if self.known[eng].get(sk, 0) >= val:
                continue
            if need.get(sk, 0) < val:
                need[sk] = val
        waits = []
        for sk, val in need.items():
            self.known[eng][sk] = val
            sem = self.sems[sk] if sk[0] in self.ENGS else self.dma_sems[sk[1]][0]
            waits.append((sem, val))
        return waits

    def _record(self, ev, reads, writes):
        for b in writes:
            self.last_w[b] = ev
            self.readers[b] = {}
        for b in reads:
            if b not in writes:
                d = self.readers.setdefault(b, {})
                if d.get(ev[0], 0) < ev[1]:
                    d[ev[0]] = ev[1]

    def op(self, eng, fn, reads=(), writes=()):
        waits = self._emit_waits(eng, self._deps(reads, writes))
        if self.cnt[eng] >= EPOCH:
            self.epoch[eng] += 1
            self.cnt[eng] = 0
            self._new_eng_sem(eng)
        self.cnt[eng] += 1
        sk = (eng, self.epoch[eng])
        sem = self.sems[sk]
        self.n_waits += len(waits)
        self.n_ops += 1

        def run(e, waits=waits, fn=fn, sem=sem):
            for (s, v) in waits:
                e.wait_ge(s, v)
            fn(e).then_inc(sem, 1)

        self.ops[eng].append(run)
        ev = (sk, self.cnt[eng])
        self._record(ev, reads, writes)
        return ev

    def dma(self, eng, fn, reads=(), writes=()):
        i = self.dma_rr[eng] + (0 if eng == "sp" else N_DMA_SEMS)
        self.dma_rr[eng] = (self.dma_rr[eng] + 1) % N_DMA_SEMS
        sem, cur = self.dma_sems[i]
        sk = ("dma", i)
        deps = self._deps(reads, writes)
        if cur > 0:
            deps.append((sk, cur))
        waits = self._emit_waits(eng, deps)
        val = cur + 16
        self.dma_sems[i][1] = val
        self.n_waits += len(waits)
        self.n_ops += 1

        def run(e, waits=waits, fn=fn, sem=sem):
            for (s, v) in waits:
                e.wait_ge(s, v)
            fn(e).then_inc(sem, 16)

        self.ops[eng].append(run)
        ev = (sk, val)
        self._record(ev, reads, writes)
        return ev

    def barrier(self):
        evs = [((e, self.epoch[e]), self.cnt[e]) for e in self.ENGS if self.cnt[e] > 0]
        evs += [(("dma", i), v) for i, (s, v) in enumerate(self.dma_sems) if v > 0]
        for eng in self.ENGS:
            waits = self._emit_waits(eng, evs)

            def run(e, waits=waits):
                for (s, v) in waits:
                    e.wait_ge(s, v)

            self.ops[eng].append(run)

    def wait_all(self, eng, bufs):
        deps = [self.last_w[b] for b in bufs if b in self.last_w]
        waits = self._emit_waits(eng, deps)

        def run(e, waits=waits):
            for (s, v) in waits:
                e.wait_ge(s, v)

        self.ops[eng].append(run)

    def finish(self):
        with self.nc.Block() as block:
            @block.tensor
            def _(e):
                for f in self.ops["pe"]:
                    f(e)

            @block.scalar
            def _(e):
                for f in self.ops["act"]:
                    f(e)

            @block.vector
            def _(e):
                for f in self.ops["dve"]:
                    f(e)

            @block.gpsimd
            def _(e):
                for f in self.ops["pool"]:
                    f(e)

            @block.sync
            def _(e):
                for f in self.ops["sp"]:
                    f(e)


def build(nlayers=NL, debug=False, passes=(0, 1, 2, 3, 4), parts=("gla", "gdn")):
    nc = bass.Bass("TRN2", target_bir_lowering=False)
    dram = lambda n, s, dt, k="ExternalInput": nc.dram_tensor(n, s, dt, kind=k).ap()
    SCR = "ExternalOutput" if debug else "Internal"
    x_in = dram("x", [T, D], F32)
    mem_in = dram("mem", [MEM, D], F32)
    w_in = dram("w_in", [NL, D, INW], F32)
    w_out = dram("w_out", [NL, D, D], F32)
    w_kv = dram("xa_w_kv", [NL, D, 512], F32)
    normw_pc = dram("normw_pc", [NL, 128, 8], F32)
    memnormw_pc = dram("memnormw_pc", [NL, 128, 8], F32)
    w2_in = dram("w2", [NL, 16, 2, 256], F32)
    bias_bc = dram("bias_bc", [NL, 128, 512], F32)
    ohnorm_bc = dram("ohnorm_bc", [NL, 128, 768], F32)
    xanorm_bc = dram("xanorm_bc", [NL, 128, 256], F32)
    convw_in = dram("convw", [NL, 96, 12, 5], F32)
    alog_bc = dram("alog_bc", [NL, 128, 8], F32)
    dtb_bc = dram("dtb_bc", [NL, 128, 8], F32)
    fnorm_bc = dram("fnorm_bc", [128, D], F32)
    ident_bf_in = dram("ident_bf", [128, 128], BF16)
    ident_f_in = dram("ident_f", [64, 64], F32)
    masks_in = dram("masks", [64, 5, 4, 64], F32)
    ones_f_in = dram("ones_f", [64, 96], F32)
    ones_bf_in = dram("ones_bf", [96, 96], BF16)
    out = dram("out", [T, D], F32, "ExternalOutput")
    KV = dram("s_kv", [T, 640], BF16, SCR)
    GATE = dram("s_gate", [T, 1024], BF16, SCR)
    GB = dram("s_gb", [T, 16], F32, SCR)
    LA = dram("s_la", [T, 512], F32, SCR)
    QKT = dram("s_qkt", [512, T], BF16, SCR)
    RAWT = dram("s_rawt", [12, 96, T], BF16, SCR)
    C2T = dram("s_c2t", [12, 96, T], BF16, SCR)
    OC3 = dram("s_oc3", [T, 256], BF16, SCR)
    OFB = [dram("s_of", [T, 768], F32, SCR), dram("s_ob", [T, 768], F32, SCR)]
    X1 = dram("s_x1", [T, D], F32, SCR)

    with contextlib.ExitStack() as st:
        S = Sched(nc, st)
        sb = lambda n, s, dt: st.enter_context(nc.sbuf_tensor("sb_" + n, s, dt))
        ident_bf = sb("ident_bf", [128, 128], BF16)
        ident_f = sb("ident_f", [64, 64], F32)
        masks = sb("masks", [64, 5, 4, 64], F32)
        masks5 = masks
        ones_f = sb("ones_f", [64, 96], F32)
        ones_bf = sb("ones_bf", [96, 96], BF16)
        fnorm = sb("fnorm", [128, D], F32)
        for (t_, d_, k_) in ((ident_bf, ident_bf_in, "c_idbf"), (ident_f, ident_f_in, "c_idf"),
                             (masks, masks_in, "c_masks"), (ones_f, ones_f_in, "c_onesf"),
                             (ones_bf, ones_bf_in, "c_onesbf"), (fnorm, fnorm_bc, "c_fnorm")):
            S.dma("sp", lambda e, t_=t_, d_=d_: e.dma_start(out=t_[:], in_=d_), writes=[k_])
        LE, GE, GT, LT = 0, 1, 2, 3

        w_out_sb = sb("w_out_sb", [128, 8, D], BF16)
        normw = sb("normw", [128, 8], F32)
        memnormw = sb("memnormw", [128, 8], F32)
        w2 = sb("w2sb", [16, 2, 256], F32)
        biasb = sb("biasb", [128, 512], F32)
        ohnorm = sb("ohnorm", [128, 768], F32)
        xanorm = sb("xanorm", [128, 256], F32)
        convw = sb("convw", [96, 12, 5], F32)
        nega = sb("nega", [128, 8], F32)
        dtb = sb("dtb", [128, 8], F32)
        mkT = sb("mkT", [64, 4, MEM], BF16)
        mv = sb("mv", [128, 2, 256], BF16)

        PB = [st.enter_context(nc.psum_tensor(f"pb{i}", [128, 512], F32)) for i in range(8)]

        def rms_rstd(eng_out, src_ap, n, key_src, key_out):
            S.op("act", lambda e: e.activation(out=eng_out, in_=src_ap, func=AF.Ln, scale=1.0 / n, bias=EPS),
                 reads=[key_src], writes=[key_out])
            S.op("act", lambda e: e.activation(out=eng_out, in_=eng_out, func=AF.Exp, scale=-0.5),
                 reads=[key_out], writes=[key_out])

        marks = []
        def mark(name):
            marks.append((name, len(S.ops['pe']), len(S.ops['act']), len(S.ops['dve']), len(S.ops['pool'])))
        def emit_layer(l):
            mark(f'L{l} start')
            xsrc = x_in if l == 0 else X1
            last = (l == nlayers - 1)
            if 0 in passes or 1 in passes:
                with contextlib.ExitStack() as ps:
                    sbp = lambda n, s, dt: ps.enter_context(nc.sbuf_tensor(f"sbA{l}_" + n, s, dt))
                    w_in_sb = sbp("w_in_sb", [128, 8, INW], BF16)
                    stage = [sbp(f"stage{i}", [128, 1688], F32) for i in range(2)]
                    xt = [sbp(f"xt{i}", [128, D], F32) for i in range(2)]
                    junk = sbp("junk", [128, D], BF16)
                    xs = [sbp(f"xs{i}", [128, D], BF16) for i in range(2)]
                    ss = sbp("ss", [128, 1], F32)
                    rstd = sbp("rstd", [128, 1], F32)
                    hTs = [sbp(f"hT{i}", [128, 8, 512], BF16) for i in range(2)]
                    hT = hTs[0]
                    kvo = [sbp(f"kvo{i}", [128, 640], BF16) for i in range(2)]
                    gate_o = [sbp(f"gateo{i}", [128, 1024], BF16) for i in range(2)]
                    gb_o = [sbp(f"gbo{i}", [128, 16], F32) for i in range(2)]
                    gtmp = sbp("gtmp", [128, 8], F32)
                    la_o = [sbp(f"lao{i}", [128, 512], F32) for i in range(2)]
                    fm_o = [sbp(f"fmo{i}", [128, 512], BF16) for i in range(2)]
                    lrT = sbp("lrT", [16, 2, 512], F32)
                    raw_o = sbp("raw_o", [96, 12, 512], BF16)
                    q3T = sbp("q3T", [64, 4, 512], BF16)
                    mx = [sbp(f"mx{i}", [128, 4], F32) for i in range(2)]
                    nb = [sbp(f"nb{i}", [128, 4], F32) for i in range(2)]
                    rs = [sbp(f"rs{i}", [128, 4], F32) for i in range(2)]
                    pexp = [sbp(f"pexp{i}", [128, 4, MEM], BF16) for i in range(2)]
                    pT = [sbp(f"pT{i}", [128, 8, 128], BF16) for i in range(2)]
                    o3 = [sbp(f"o3{i}", [128, 4, 64], F32) for i in range(2)]
                    o3sq = [sbp(f"o3sq{i}", [128, 4, 64], F32) for i in range(2)]
                    o3ss = [sbp(f"o3ss{i}", [128, 4], F32) for i in range(2)]
                    oc3_o = [sbp(f"oc3o{i}", [128, 256], BF16) for i in range(2)]
                    def token_tile_to_hT(src_dram, row0, slot, dst, dst_key, col0):
                        xk = ("xt", slot)
                        S.dma("sp", lambda e: e.dma_start(out=xt[slot][:], in_=src_dram[row0:row0 + 128, :]), writes=[xk])
                        S.op("act", lambda e: e.activation(out=junk[:], in_=xt[slot][:], func=AF.Square, accum_out=ss[:]),
                             reads=[xk], writes=["junk", "ss"])
                        rms_rstd(rstd[:], ss[:], D, "ss", "rstd")
                        S.op("dve", lambda e: e.tensor_scalar(out=xs[slot][:], in0=xt[slot][:], scalar1=rstd[:], scalar2=None,
                                                              op0=ALU.mult), reads=[xk, "rstd"], writes=[("xs", slot)])
                        ptr = PB[0][:, 0:512].bitcast(BF16).rearrange("p (c t) -> p c t", c=8)
                        for c in range(8):
                            S.op("pe", lambda e, c=c: e.transpose(out=ptr[:, c, :], in_=xs[slot][:, c * 128:(c + 1) * 128],
                                                                  identity=ident_bf[:]),
                                 reads=[("xs", slot), "c_idbf"], writes=["pb0"])
                        S.op("dve", lambda e: e.tensor_copy(out=dst[:, :, col0:col0 + 128], in_=ptr), reads=["pb0"],
                             writes=[dst_key])

                    if 0 in passes:
                        for (t_, d_, k_) in ((normw, normw_pc[l], "normw"), (memnormw, memnormw_pc[l], "memnormw"),
                                             (w2, w2_in[l], "w2"), (biasb, bias_bc[l], "biasb"),
                                             (ohnorm, ohnorm_bc[l], "ohnorm"), (xanorm, xanorm_bc[l], "xanorm"),
                                             (convw, convw_in[l], "convw"), (nega, alog_bc[l], "nega"),
                                             (dtb, dtb_bc[l], "dtb")):
                            S.dma("sp", lambda e, t_=t_, d_=d_: e.dma_start(out=t_[:], in_=d_), writes=[k_])
                        S.op("act", lambda e: e.activation(out=nega[:], in_=nega[:], func=AF.Exp), reads=["nega"], writes=["nega"])
                        S.op("dve", lambda e: e.tensor_scalar(out=nega[:], in0=nega[:], scalar1=-1.0, scalar2=None, op0=ALU.mult),
                             reads=["nega"], writes=["nega"])
                        for cc in range(16):
                            c, hf = divmod(cc, 2)
                            sl = cc % 2
                            S.dma("sp", lambda e, c=c, sl=sl, hf=hf: e.dma_start(out=stage[sl][:], in_=w_in[l, c * 128:(c + 1) * 128, hf * 1688:(hf + 1) * 1688]),
                                  writes=[("stage", sl)])
                            S.op("dve" if cc % 2 == 0 else "pool",
                                 lambda e, c=c, sl=sl, hf=hf: e.tensor_scalar(out=w_in_sb[:, c, hf * 1688:(hf + 1) * 1688], in0=stage[sl][:], scalar1=normw[:, c:c + 1],
                                                                       scalar2=None, op0=ALU.mult),
                                 reads=[("stage", sl), "normw"], writes=["w_in_sb"])
                        for c in range(8):
                            sl = c % 2
                            S.dma("sp", lambda e, c=c, sl=sl: e.dma_start(out=stage[sl][:, 0:D], in_=w_out[l, c * 128:(c + 1) * 128, :]),
                                  writes=[("stage", sl)])
                            S.op("act", lambda e, c=c, sl=sl: e.copy(out=w_out_sb[:, c, :], in_=stage[sl][:, 0:D]),
                                 reads=[("stage", sl)], writes=["w_out_sb"])
                        wkv = hT
                        for c in range(8):
                            sl = c % 2
                            S.dma("sp", lambda e, c=c, sl=sl: e.dma_start(out=stage[sl][:, 0:512], in_=w_kv[l, c * 128:(c + 1) * 128, :]),
                                  writes=[("stage", sl)])
                            S.op("dve", lambda e, c=c, sl=sl: e.tensor_scalar(out=wkv[:, c, :], in0=stage[sl][:, 0:512],
                                                                              scalar1=memnormw[:, c:c + 1], scalar2=None, op0=ALU.mult),
                                 reads=[("stage", sl), "memnormw"], writes=[("hT", 0)])
                        mT = raw_o[:, 0:4, :]
                        memT = sbp("memT", [128, 8, MEM], BF16)
                        for mt_ in range(2):
                            token_tile_to_hT(mem_in, mt_ * 128, mt_, memT, "memT", mt_ * 128)
                        for h in range(4):
                            pm = PB[1][0:64, 0:MEM]
                            for c in range(8):
                                S.op("pe", lambda e, c=c, h=h, pm=pm: e.matmul(pm, lhsT=wkv[:, c, h * 64:(h + 1) * 64], rhs=memT[:, c, :],
                                                                               start=(c == 0), stop=(c == 7)),
                                     reads=[("hT", 0), "memT"], writes=["pb1"])
                            S.op("act", lambda e, h=h, pm=pm: e.copy(out=mkT[:, h, :], in_=pm), reads=["pb1"], writes=["mkT"])
                        for mt_ in range(2):
                            pm = PB[1][:, 0:256]
                            for c in range(8):
                                S.op("pe", lambda e, c=c, mt_=mt_, pm=pm: e.matmul(pm, lhsT=memT[:, c, mt_ * 128:(mt_ + 1) * 128],
                                                                                   rhs=wkv[:, c, 256:512], start=(c == 0), stop=(c == 7)),
                                     reads=[("hT", 0), "memT"], writes=["pb1"])
                            S.op("act", lambda e, mt_=mt_, pm=pm: e.copy(out=mv[:, mt_, :], in_=pm), reads=["pb1"], writes=["mv"])

                    mark(f'L{l} p0 done')
                    if 1 in passes:
                        def do_mt(mt, hT, hk):
                            for sub in range(4):
                                tt = mt * 4 + sub
                                r0 = tt * 128
                                sl = tt % 2
                                if tt + 1 < T // 128:
                                    prep_tile(tt + 1)
                                hsl = lambda c: hT[:, c, sub * 128:(sub + 1) * 128]
                                groups = ((1, 256, 512), (2, 768, 512), (3, 2464, 400), (4, 3120, 256))
                                for (bk, c0, wd) in groups:
                                    for c in range(8):
                                        S.op("pe", lambda e, c=c, bk=bk, c0=c0, wd=wd, sub=sub: e.matmul(
                                            PB[bk][:, 0:wd], lhsT=hT[:, c, sub * 128:(sub + 1) * 128], rhs=w_in_sb[:, c, c0:c0 + wd],
                                            start=(c == 0), stop=(c == 7)), reads=[hk, "w_in_sb"], writes=[f"pb{bk}"])
                                S.op("act", lambda e, sl=sl: e.copy(out=kvo[sl][:, 0:512], in_=PB[1][:, 0:512]), reads=["pb1"],
                                     writes=[("kvo", sl)])
                                S.op("dve", lambda e, sl=sl: e.tensor_copy(out=kvo[sl][:, 512:640], in_=PB[2][:, 0:128]), reads=["pb2"],
                                     writes=[("kvo", sl)])
                                S.dma("pool", lambda e, sl=sl, r0=r0: e.dma_start(out=KV[r0:r0 + 128, :], in_=kvo[sl][:]),
                                      reads=[("kvo", sl)], writes=["KV"])
                                S.op("act", lambda e, sl=sl: e.activation(out=gate_o[sl][:, 0:384], in_=PB[2][:, 128:512], func=AF.Silu),
                                     reads=["pb2"], writes=[("gateo", sl)])
                                S.op("act", lambda e, sl=sl: e.activation(out=gate_o[sl][:, 384:768], in_=PB[3][:, 0:384], func=AF.Silu),
                                     reads=["pb3"], writes=[("gateo", sl)])
                                S.op("act", lambda e, sl=sl: e.activation(out=gate_o[sl][:, 768:1024], in_=PB[4][:, 0:256], func=AF.Silu),
                                     reads=["pb4"], writes=[("gateo", sl)])
                                S.dma("pool", lambda e, sl=sl, r0=r0: e.dma_start(out=GATE[r0:r0 + 128, :], in_=gate_o[sl][:]),
                                      reads=[("gateo", sl)], writes=["GATE"])
                                S.op("act", lambda e, sl=sl: e.activation(out=gb_o[sl][:, 0:8], in_=PB[3][:, 384:392], func=AF.Exp, scale=-1.0),
                                     reads=["pb3"], writes=[("gbo", sl)])
                                S.op("dve", lambda e, sl=sl: e.tensor_scalar(out=gb_o[sl][:, 0:8], in0=gb_o[sl][:, 0:8], scalar1=1.0, scalar2=None, op0=ALU.add),
                                     reads=[("gbo", sl)], writes=[("gbo", sl)])
                                S.op("dve", lambda e, sl=sl: e.reciprocal(out=gb_o[sl][:, 0:8], in_=gb_o[sl][:, 0:8]), reads=[("gbo", sl)], writes=[("gbo", sl)])
                                S.op("dve", lambda e: e.tensor_tensor(out=gtmp[:], in0=PB[3][:, 392:400], in1=dtb[:], op=ALU.add),
                                     reads=["pb3", "dtb"], writes=["gtmp"])
                                S.op("act", lambda e: e.activation(out=gtmp[:], in_=gtmp[:], func=AF.Exp), reads=["gtmp"], writes=["gtmp"])
                                S.op("act", lambda e: e.activation(out=gtmp[:], in_=gtmp[:], func=AF.Ln, bias=1.0), reads=["gtmp"],
                                     writes=["gtmp"])
                                S.op("dve", lambda e, sl=sl: e.tensor_tensor(out=gb_o[sl][:, 8:16], in0=gtmp[:], in1=nega[:], op=ALU.mult),
                                     reads=["gtmp", "nega"], writes=[("gbo", sl)])
                                S.dma("pool", lambda e, sl=sl, r0=r0: e.dma_start(out=GB[r0:r0 + 128, :], in_=gb_o[sl][:]),
                                      reads=[("gbo", sl)], writes=["GB"])
                            t0 = mt * 512
                            fmi = 0
                            for m in range(4):
                                bk = 5 + (m % 2)
                                for c in range(8):
                                    S.op("pe", lambda e, c=c, m=m, bk=bk: e.matmul(PB[bk][:, :], lhsT=w_in_sb[:, c, m * 128:(m + 1) * 128],
                                                                                   rhs=hT[:, c, :], start=(c == 0), stop=(c == 7)),
                                         reads=[hk, "w_in_sb"], writes=[f"pb{bk}"])
                                fs = fmi % 2
                                fmi += 1
                                S.op("act", lambda e, bk=bk, fs=fs, m=m: e.activation(out=fm_o[fs][:], in_=PB[bk][:, :], func=AF.Copy,
                                                                                      scale=(0.125 if m < 2 else 1.0)),
                                     reads=[f"pb{bk}"], writes=[("fmo", fs)])
                                S.dma("pool", lambda e, fs=fs, m=m, t0=t0: e.dma_start(out=QKT[m * 128:(m + 1) * 128, t0:t0 + 512],
                                                                                       in_=fm_o[fs][:]),
                                      reads=[("fmo", fs)], writes=["QKT"])
                            for z in range(2):
                                for c in range(8):
                                    S.op("pe", lambda e, c=c, z=z: e.matmul(PB[7][0:16, :], lhsT=w_in_sb[:, c, 1280 + z * 16:1296 + z * 16],
                                                                            rhs=hT[:, c, :], start=(c == 0), stop=(c == 7)),
                                         reads=[hk, "w_in_sb"], writes=["pb7"])
                                S.op("act", lambda e, z=z: e.copy(out=lrT[:, z, :], in_=PB[7][0:16, :]), reads=["pb7"], writes=["lrT"])
                            def gate_logits(sub):
                                tt = mt * 4 + sub
                                r0 = tt * 128
                                sl = tt % 2
                                for z in range(2):
                                    S.op("pe", lambda e, z=z, sub=sub: e.matmul(PB[7][:, z * 256:(z + 1) * 256],
                                                                                lhsT=lrT[:, z, sub * 128:(sub + 1) * 128], rhs=w2[:, z, :],
                                                                                start=True, stop=True),
                                         reads=["lrT", "w2"], writes=["pb7"])
                                S.op("dve", lambda e, sl=sl: e.tensor_tensor(out=la_o[sl][:], in0=PB[7][:, :], in1=biasb[:], op=ALU.add),
                                     reads=["pb7", "biasb"], writes=[("lao", sl)])
                                S.op("act", lambda e, sl=sl: e.activation(out=la_o[sl][:], in_=la_o[sl][:], func=AF.Exp, scale=-1.0),
                                     reads=[("lao", sl)], writes=[("lao", sl)])
                                S.op("act", lambda e, sl=sl: e.activation(out=la_o[sl][:], in_=la_o[sl][:], func=AF.Ln, bias=1.0),
                                     reads=[("lao", sl)], writes=[("lao", sl)])
                                S.op("dve", lambda e, sl=sl: e.tensor_scalar(out=la_o[sl][:], in0=la_o[sl][:], scalar1=-1.0 / 16.0,
                                                                             scalar2=None, op0=ALU.mult),
                                     reads=[("lao", sl)], writes=[("lao", sl)])
                                S.dma("pool", lambda e, sl=sl, r0=r0: e.dma_start(out=LA[r0:r0 + 128, :], in_=la_o[sl][:]),
                                      reads=[("lao", sl)], writes=["LA"])
                            for jj in range(12):
                                bk = 5 + (jj % 2)
                                for c in range(8):
                                    S.op("pe", lambda e, c=c, jj=jj, bk=bk: e.matmul(PB[bk][0:96, :],
                                                                                     lhsT=w_in_sb[:, c, 1312 + jj * 96:1312 + (jj + 1) * 96],
                                                                                     rhs=hT[:, c, :], start=(c == 0), stop=(c == 7)),
                                         reads=[hk, "w_in_sb"], writes=[f"pb{bk}"])
                                S.op("act" if jj % 2 == 0 else "dve",
                                     (lambda e, jj=jj, bk=bk: e.copy(out=raw_o[:, jj, :], in_=PB[bk][0:96, :])) if jj % 2 == 0 else
                                     (lambda e, jj=jj, bk=bk: e.tensor_copy(out=raw_o[:, jj, :], in_=PB[bk][0:96, :])),
                                     reads=[f"pb{bk}"], writes=["raw_o"])
                                if jj % 3 == 2:
                                    gate_logits(jj // 3)
                            S.dma("pool", lambda e, t0=t0: e.dma_start(out=RAWT[:, :, t0:t0 + 512].rearrange("j p c -> p j c"), in_=raw_o[:]),
                                  reads=["raw_o"], writes=["RAWT"])
                            for h in range(4):
                                bk = 5 + (h % 2)
                                for c in range(8):
                                    S.op("pe", lambda e, c=c, h=h, bk=bk: e.matmul(PB[bk][0:64, :],
                                                                                   lhsT=w_in_sb[:, c, 2864 + h * 64:2864 + (h + 1) * 64],
                                                                                   rhs=hT[:, c, :], start=(c == 0), stop=(c == 7)),
                                         reads=[hk, "w_in_sb"], writes=[f"pb{bk}"])
                                S.op("act", lambda e, h=h, bk=bk: e.copy(out=q3T[:, h, :], in_=PB[bk][0:64, :]), reads=[f"pb{bk}"],
                                     writes=["q3T"])
                            def xa_sub(sub):
                                tt = mt * 4 + sub
                                r0 = tt * 128
                                sl = tt % 2
                                xb = 5 if sl == 0 else 1
                                for h in range(4):
                                    bk = xb + (h // 2)
                                    S.op("pe", lambda e, h=h, bk=bk, sub=sub: e.matmul(PB[bk][:, (h % 2) * 256:(h % 2 + 1) * 256],
                                                                                       lhsT=q3T[:, h, sub * 128:(sub + 1) * 128], rhs=mkT[:, h, :],
                                                                                       start=True, stop=True),
                                         reads=["q3T", "mkT"], writes=[f"pb{bk}"])
                                yield
                                for hp in range(2):
                                    S.op("dve", lambda e, hp=hp: e.tensor_reduce(out=mx[sl][:, hp * 2:hp * 2 + 2],
                                                                                 in_=PB[xb + hp][:, :].rearrange("p (h m) -> p h m", h=2),
                                                                                 axis=AX.X, op=ALU.max),
                                         reads=[f"pb{xb + hp}"], writes=[("mx", sl)])
                                S.op("dve", lambda e: e.tensor_scalar(out=nb[sl][:], in0=mx[sl][:], scalar1=-0.125, scalar2=None, op0=ALU.mult),
                                     reads=[("mx", sl)], writes=[("nb", sl)])
                                for h in range(4):
                                    bk = xb + (h // 2)
                                    S.op("act", lambda e, h=h, bk=bk: e.activation(out=pexp[sl][:, h, :], in_=PB[bk][:, (h % 2) * 256:(h % 2 + 1) * 256],
                                                                                   func=AF.Exp, scale=0.125, bias=nb[sl][:, h:h + 1],
                                                                                   accum_out=rs[sl][:, h:h + 1]),
                                         reads=[f"pb{bk}", ("nb", sl)], writes=[("pexp", sl), ("rs", sl)])
                                yield
                                ptp = PB[xb + 2][:, :].bitcast(BF16).rearrange("p (c t) -> p c t", c=8)
                                for h in range(4):
                                    for mb in range(2):
                                        S.op("pe", lambda e, h=h, mb=mb: e.transpose(out=ptp[:, h * 2 + mb, :],
                                                                                     in_=pexp[sl][:, h, mb * 128:(mb + 1) * 128], identity=ident_bf[:]),
                                             reads=[("pexp", sl), "c_idbf"], writes=[f"pb{xb + 2}"])
                                S.op("dve", lambda e: e.tensor_copy(out=pT[sl][:], in_=ptp), reads=[f"pb{xb + 2}"], writes=[("pT", sl)])
                                yield
                                po = PB[xb][:, 0:256].rearrange("p (h d) -> p h d", h=4)
                                for h in range(4):
                                    for mb in range(2):
                                        S.op("pe", lambda e, h=h, mb=mb: e.matmul(po[:, h, :], lhsT=pT[sl][:, h * 2 + mb, :],
                                                                                  rhs=mv[:, mb, h * 64:(h + 1) * 64], start=(mb == 0), stop=(mb == 1)),
                                             reads=[("pT", sl), "mv"], writes=[f"pb{xb}"])
                                yield
                                S.op("dve", lambda e: e.reciprocal(out=rs[sl][:], in_=rs[sl][:]), reads=[("rs", sl)], writes=[("rs", sl)])
                                S.op("dve", lambda e: e.tensor_tensor(out=o3[sl][:], in0=po, in1=rs[sl][:].unsqueeze(2).broadcast_to([128, 4, 64]),
                                                                      op=ALU.mult), reads=[f"pb{xb}", ("rs", sl)], writes=[("o3", sl)])
                                S.op("pool", lambda e: e.tensor_tensor(out=o3sq[sl][:], in0=o3[sl][:], in1=o3[sl][:], op=ALU.mult), reads=[("o3", sl)],
                                     writes=[("o3sq", sl)])
                                S.op("dve", lambda e: e.tensor_reduce(out=o3ss[sl][:], in_=o3sq[sl][:], axis=AX.X, op=ALU.add), reads=[("o3sq", sl)],
                                     writes=[("o3ss", sl)])
                                yield
                                rms_rstd(o3ss[sl][:], o3ss[sl][:], 64, ("o3ss", sl), ("o3ss", sl))
                                S.op("dve", lambda e: e.tensor_tensor(out=o3[sl][:], in0=o3[sl][:], in1=o3ss[sl][:].unsqueeze(2).broadcast_to([128, 4, 64]),
                                                                      op=ALU.mult), reads=[("o3", sl), ("o3ss", sl)], writes=[("o3", sl)])
                                S.op("pool", lambda e: e.tensor_tensor(out=o3[sl][:], in0=o3[sl][:], in1=xanorm[:].rearrange("p (h d) -> p h d", h=4),
                                                                       op=ALU.mult), reads=[("o3", sl), "xanorm"], writes=[("o3", sl)])
                                S.dma("sp", lambda e, sl=sl, r0=r0: e.dma_start(out=oc3_o[sl][:], in_=GATE[r0:r0 + 128, 768:1024]),
                                      reads=["GATE"], writes=[("oc3o", sl)])
                                S.op("dve", lambda e, sl=sl: e.tensor_tensor(out=oc3_o[sl][:], in0=o3[sl][:].rearrange("p h d -> p (h d)"),
                                                                             in1=oc3_o[sl][:], op=ALU.mult),
                                     reads=[("o3", sl), ("oc3o", sl)], writes=[("oc3o", sl)])
                                S.dma("pool", lambda e, sl=sl, r0=r0: e.dma_start(out=OC3[r0:r0 + 128, :], in_=oc3_o[sl][:]),
                                      reads=[("oc3o", sl)], writes=["OC3"])
                            xa_act = []
                            xa_n = 0
                            while True:
                                while len(xa_act) < 2 and xa_n < 4:
                                    xa_act.append(xa_sub(xa_n))
                                    xa_n += 1
                                if not xa_act:
                                    break
                                for g_ in list(xa_act):
                                    if next(g_, "done") == "done":
                                        xa_act.remove(g_)

                        def prep_tile(tt_):
                            mt__, sub__ = divmod(tt_, 4)
                            token_tile_to_hT(xsrc, tt_ * 128, tt_ % 2, hTs[mt__ % 2], ("hT", mt__ % 2), sub__ * 128)
                        prep_tile(0)
                        for mt_ in range(T // 512):
                            do_mt(mt_, hTs[mt_ % 2], ("hT", mt_ % 2))
                    S.barrier()

            mark(f'L{l} p1 done')
            if 2 in passes:
                with contextlib.ExitStack() as ps:
                    sbp = lambda n, s, dt: ps.enter_context(nc.sbuf_tensor(f"sbB{l}_" + n, s, dt))
                    rawh = [sbp(f"rawh{i}", [96, 12, 516], BF16) for i in range(2)]
                    dg = sbp("dg", [96, 12, 5, 96], BF16)
                    for jj in range(12):
                        S.op("dve" if jj % 2 == 0 else "pool", lambda e, jj=jj: e.tensor_tensor(
                            out=dg[:, jj, :, :], in0=ident_bf[0:96, 0:96].unsqueeze(1).broadcast_to([96, 5, 96]),
                            in1=convw[:, jj, :].unsqueeze(2).broadcast_to([96, 5, 96]), op=ALU.mult),
                            reads=["c_idbf", "convw"], writes=["dg"])
                    sil = [sbp(f"sil{i}", [96, 512], F32) for i in range(8)]
                    sqb = [sbp(f"sqb{i}", [96, 512], BF16) for i in range(8)]
                    rsd = [sbp(f"rsd{i}", [96, 512], F32) for i in range(2)]
                    c2_o = [sbp(f"c2o{i}", [96, 12, 512], BF16) for i in range(2)]
                    for mt in range(T // 512):
                        t0 = mt * 512
                        rs_ = mt % 2
                        lo = max(t0 - 2, 0)
                        hi = min(t0 + 514, T)
                        rk = ("rawh", rs_)
                        if mt == 0:
                            S.op("pool", lambda e, rs_=rs_: e.memset(rawh[rs_][:, :, 0:2], 0.0), writes=[rk])
                        if mt == T // 512 - 1:
                            S.op("pool", lambda e, rs_=rs_: e.memset(rawh[rs_][:, :, 514:516], 0.0), writes=[rk])
                        S.dma("sp", lambda e, rs_=rs_, lo=lo, hi=hi, t0=t0: e.dma_start(
                            out=rawh[rs_][:, :, lo - (t0 - 2):hi - (t0 - 2)], in_=RAWT[:, :, lo:hi].rearrange("j p c -> p j c")),
                            reads=["RAWT"], writes=[rk])
                        for jj in range(12):
                            a_ = jj % 4
                            ak = f"pb{1 + a_}"
                            accp = PB[1 + a_][0:96, :]
                            for k in range(5):
                                S.op("pe", lambda e, jj=jj, a_=a_, rs_=rs_, k=k: e.matmul(PB[1 + a_][0:96, :], lhsT=dg[:, jj, k, :],
                                                                                         rhs=rawh[rs_][:, jj, k:k + 512], start=(k == 0), stop=(k == 4)),
                                     reads=[rk, "dg"], writes=[ak])
                            if jj >= 8:
                                S.op("act", lambda e, jj=jj, a_=a_, rs_=rs_: e.activation(out=c2_o[rs_][:, jj, :], in_=PB[1 + a_][0:96, :], func=AF.Silu),
                                     reads=[ak], writes=[("c2o", rs_)])
                            else:
                                s_ = jj
                                S.op("act", lambda e, a_=a_, s_=s_: e.activation(out=sil[s_][:], in_=PB[1 + a_][0:96, :], func=AF.Silu),
                                     reads=[ak], writes=[("sil", s_)])
                                S.op("pool", lambda e, s_=s_: e.tensor_tensor(out=sqb[s_][:], in0=sil[s_][:], in1=sil[s_][:], op=ALU.mult),
                                     reads=[("sil", s_)], writes=[("sqb", s_)])
                        for jj in range(8):
                            s_ = jj
                            r_ = jj % 2
                            bk = (5, 6, 7, 0)[jj % 4]
                            S.op("pe", lambda e, s_=s_, bk=bk: e.matmul(PB[bk][0:96, :], lhsT=ones_bf[:], rhs=sqb[s_][:], start=True, stop=True),
                                 reads=[("sqb", s_), "c_onesbf"], writes=[f"pb{bk}"])
                            S.op("act", lambda e, r_=r_, bk=bk: e.activation(out=rsd[r_][:], in_=PB[bk][0:96, :], func=AF.Ln, bias=EPS),
                                 reads=[f"pb{bk}"], writes=[("rsd", r_)])
                            S.op("act", lambda e, r_=r_: e.activation(out=rsd[r_][:], in_=rsd[r_][:], func=AF.Exp, scale=-0.5),
                                 reads=[("rsd", r_)], writes=[("rsd", r_)])
                            qs = (96.0 ** -0.5) if jj < 4 else 1.0
                            S.op("dve", lambda e, s_=s_, r_=r_, jj=jj, qs=qs, rs_=rs_: e.scalar_tensor_tensor(
                                out=c2_o[rs_][:, jj, :], in0=sil[s_][:], scalar=qs, in1=rsd[r_][:], op0=ALU.mult, op1=ALU.mult),
                                reads=[("sil", s_), ("rsd", r_)], writes=[("c2o", rs_)])
                        S.dma("pool", lambda e, rs_=rs_, t0=t0: e.dma_start(out=C2T[:, :, t0:t0 + 512].rearrange("j p c -> p j c"),
                                                                          in_=c2_o[rs_][:]), reads=[("c2o", rs_)], writes=["C2T"])
                    S.barrier()

            mark(f'L{l} p2 done')
            if 3 in passes:
                with contextlib.ExitStack() as ps:
                    sbp = lambda n, s, dt: ps.enter_context(nc.sbuf_tensor(f"sbC{l}_" + n, s, dt))
                    two = range(2)
                    glaq = [[sbp(f"glaq{d}{i}", [64, 4, 256], BF16) for i in two] for d in two]
                    glak = [[sbp(f"glak{d}{i}", [64, 4, 256], BF16) for i in two] for d in two]
                    glakv = [[sbp(f"glakv{d}{i}", [64, 4, 640], BF16) for i in two] for d in two]
                    glala = [[sbp(f"glala{d}{i}", [64, 4, 256], F32) for i in two] for d in two]
                    gdnc2 = [[sbp(f"gdnc2{d}{i}", [96, 12, 256], BF16) for i in two] for d in two]
                    gdngb = [[sbp(f"gdngb{d}{i}", [64, 4, 16], F32) for i in two] for d in two]
                    ebT = [sbp(f"ebT{d}", [64, 4, 64], F32) for d in two]
                    enbT = [sbp(f"enbT{d}", [64, 4, 64], F32) for d in two]
                    erem = [sbp(f"erem{d}", [64, 256], F32) for d in two]
                    qeT = [sbp(f"qeT{d}", [64, 4, 64], BF16) for d in two]
                    keT = [sbp(f"keT{d}", [64, 4, 64], BF16) for d in two]
                    ktl = [sbp(f"ktl{d}", [64, 256], BF16) for d in two]
                    ATm = [sbp(f"ATm{d}", [64, 4, 64], BF16) for d in two]
                    gS32 = [sbp(f"gS32{d}", [64, 4, 96], F32) for d in two]
                    gSbf = [sbp(f"gSbf{d}", [64, 4, 96], BF16) for d in two]
                    go_sb = [sbp(f"go_sb{d}", [64, 384], F32) for d in two]
                    ebT8 = sbp("ebT8", [64, 8, 64], F32)
                    enbT8 = sbp("enbT8", [64, 8, 64], F32)
                    erem8 = sbp("erem8", [64, 512], F32)
                    qeT8 = sbp("qeT8", [64, 8, 64], BF16)
                    keT8 = sbp("keT8", [64, 8, 64], BF16)
                    ktl8 = sbp("ktl8", [64, 512], BF16)
                    ATm8 = sbp("ATm8", [64, 8, 64], BF16)
                    gS328 = sbp("gS328", [64, 8, 96], F32)
                    gSbf8 = sbp("gSbf8", [64, 8, 96], BF16)
                    S.op("pool", lambda e: e.memset(gS328[:], 0.0), writes=["gS328"])
                    S.op("pool", lambda e: e.memset(gSbf8[:], 0.0), writes=["gSbf8"])

                    gb8 = sbp("gb8", [64, 16], F32)
                    K8 = sbp("K8", [64, 8, 96], BF16)
                    V8 = sbp("V8", [64, 8, 96], BF16)
                    eg16 = sbp("eg16", [64, 16], F32)
                    dec96 = sbp("dec96m", [96, 8], F32)
                    R8 = sbp("R8", [64, 8, 64], F32)
                    kkN = sbp("kkN", [64, 8, 64], F32)
                    kkM = sbp("kkM", [64, 8, 64], F32)
                    qkI = sbp("qkI", [64, 8, 64], F32)
                    decT = sbp("decT", [64, 8, 64], F32)
                    decD = sbp("decD", [64, 8, 64], F32)
                    EGB8 = sbp("EGB8", [96, 8, 64], F32)
                    DB8 = sbp("DB8", [64, 8, 64], BF16)
                    Nm = [sbp(f"Nm{i}", [64, 8, 64], NDT) for i in two]
                    Mm = [sbp(f"Mm{i}", [64, 8, 64], NDT) for i in two]
                    qkm8 = sbp("qkm8", [96, 8, 64], BF16)
                    Pm8 = sbp("Pm8", [64, 8, 64], NDT)
                    TiT8 = sbp("TiT8", [64, 8, 64], BF16)
                    vb8 = sbp("vb8", [64, 8, 96], BF16)
                    bg8 = sbp("bg8", [64, 8], F32)
                    kbg8 = sbp("kbg8", [64, 8, 96], BF16)
                    kte8 = sbp("kte8", [64, 8, 96], BF16)
                    qdT8 = sbp("qdT8", [96, 8, 64], BF16)
                    u8 = sbp("u8", [64, 8, 96], F32)
                    wT8 = sbp("wT8", [96, 8, 64], BF16)
                    vnew8 = sbp("vnew8", [96, 8, 96], BF16)
                    dS328 = sbp("dS328", [96, 8, 96], F32)
                    dSbf8 = sbp("dSbf8", [96, 8, 96], BF16)
                    do_sb = [sbp(f"do_sb{d}", [64, 384], F32) for d in two]
                    S.op("pool", lambda e: e.memset(dS328[:], 0.0), writes=["dS328"])
                    S.op("pool", lambda e: e.memset(qkm8[:], 0.0), writes=["qkm8"])
                    S.op("pool", lambda e: e.memset(vnew8[:], 0.0), writes=["vnew8"])
                    S.op("pool", lambda e: e.memset(dSbf8[:], 0.0), writes=["dSbf8"])

                    for d in two:
                        S.op("pool", lambda e, d=d: e.memset(gS32[d][:], 0.0), writes=[("gS32", d)])
                        S.op("pool", lambda e, d=d: e.memset(gSbf[d][:], 0.0), writes=[("gSbf", d)])

                    def load_group(d, grp):
                        gs = grp % 2
                        g0 = grp * 256
                        S.dma("sp", lambda e: e.dma_start(out=glaq[d][gs][:], in_=QKT[0:256, g0:g0 + 256].rearrange("(h p) c -> p h c", p=64)),
                              reads=["QKT"], writes=[("glaq", d, gs)])
                        S.dma("sp", lambda e: e.dma_start(out=glak[d][gs][:], in_=QKT[256:512, g0:g0 + 256].rearrange("(h p) c -> p h c", p=64)),
                              reads=["QKT"], writes=[("glak", d, gs)])
                        S.dma("sp", lambda e: e.dma_start(out=glakv[d][gs][:], in_=KV[g0:g0 + 256, :].rearrange("(n p) c -> p n c", p=64)),
                              reads=["KV"], writes=[("glakv", d, gs)])
                        S.dma("sp", lambda e: e.dma_start(out=glala[d][gs][:], in_=LA[g0:g0 + 256, d * 256:(d + 1) * 256].rearrange("(n p) c -> p n c", p=64)),
                              reads=["LA"], writes=[("glala", d, gs)])
                        S.dma("sp", lambda e: e.dma_start(out=gdnc2[d][gs][:], in_=C2T[:, :, g0:g0 + 256].rearrange("j p c -> p j c")),
                              reads=["C2T"], writes=[("gdnc2", d, gs)])
                        S.dma("sp", lambda e: e.dma_start(out=gdngb[d][gs][:], in_=GB[g0:g0 + 256, :].rearrange("(n p) c -> p n c", p=64)),
                              reads=["GB"], writes=[("gdngb", d, gs)])

                    def gla_chunk(d, n):
                        grp, ci = divmod(n, 4)
                        gs = grp % 2
                        qg, kg, kvg, lag = glaq[d][gs], glak[d][gs], glakv[d][gs], glala[d][gs]
                        kq_, kk_, kkv_, kla_ = ("glaq", d, gs), ("glak", d, gs), ("glakv", d, gs), ("glala", d, gs)
                        cs = slice(ci * 64, (ci + 1) * 64)
                        mI = LE if d == 0 else GE
                        mS = GT if d == 0 else LT
                        bT_ps = PB[0][0:64, 0:256].rearrange("p (h c) -> p h c", h=4)
                        rem_ps = PB[0][0:64, 256:512]
                        import os as _os
                        STG = int(_os.environ.get('K_STAGE', 99))
                        for h in range(4):
                            S.op("pe", lambda e, h=h: e.matmul(bT_ps[:, h, :], lhsT=lag[:, ci, h * 64:(h + 1) * 64], rhs=masks[:, mI, 0, :],
                                                               start=True, stop=True), reads=[kla_, "c_masks"], writes=["pb0"])
                        S.op("pe", lambda e: e.matmul(rem_ps, lhsT=masks[:, mS, 0, :], rhs=lag[:, ci, :], start=True, stop=True),
                             reads=[kla_, "c_masks"], writes=["pb0"])
                        yield
                        S.op("act", lambda e: e.activation(out=ebT[d][:], in_=bT_ps, func=AF.Exp), reads=["pb0"], writes=[("ebT", d)])
                        S.op("act", lambda e: e.activation(out=enbT[d][:], in_=bT_ps, func=AF.Exp, scale=-1.0), reads=["pb0"], writes=[("enbT", d)])
                        S.op("act", lambda e: e.activation(out=erem[d][:], in_=rem_ps, func=AF.Exp), reads=["pb0"], writes=[("erem", d)])
                        yield
                        S.op("dve", lambda e: e.tensor_tensor(out=qeT[d][:], in0=qg[:, :, cs], in1=ebT[d][:], op=ALU.mult),
                             reads=[kq_, ("ebT", d)], writes=[("qeT", d)])
                        S.op("pool", lambda e: e.tensor_tensor(out=keT[d][:], in0=kg[:, :, cs], in1=enbT[d][:], op=ALU.mult),
                             reads=[kk_, ("enbT", d)], writes=[("keT", d)])
                        S.op("dve", lambda e: e.tensor_tensor(out=ktl[d][:], in0=kvg[:, ci, 0:256], in1=erem[d][:], op=ALU.mult),
                             reads=[kkv_, ("erem", d)], writes=[("ktl", d)])
                        yield
                        AT_ps = PB[1][0:64, 0:256].rearrange("p (h c) -> p h c", h=4)
                        for h in range(4):
                            S.op("pe", lambda e, h=h: e.matmul(AT_ps[:, h, :], lhsT=keT[d][:, h, :], rhs=qeT[d][:, h, :], start=True, stop=True),
                                 reads=[("keT", d), ("qeT", d)], writes=["pb1"])
                        yield
                        S.op("dve", lambda e: e.tensor_tensor(out=ATm[d][:], in0=AT_ps, in1=masks[:, mI, :, :], op=ALU.mult),
                             reads=["pb1", "c_masks"], writes=[("ATm", d)])
                        yield
                        o_ps = PB[0][0:64, 0:384].rearrange("p (h v) -> p h v", h=4)
                        for h in range(4):
                            S.op("pe", lambda e, h=h: e.matmul(o_ps[:, h, :], lhsT=ATm[d][:, h, :], rhs=kvg[:, ci, 256 + h * 96:256 + (h + 1) * 96],
                                                               start=True, stop=False), reads=[("ATm", d), kkv_], writes=["pb0"])
                            S.op("pe", lambda e, h=h: e.matmul(o_ps[:, h, :], lhsT=qeT[d][:, h, :], rhs=gSbf[d][:, h, :], start=False, stop=True),
                                 reads=[("qeT", d), ("gSbf", d)], writes=["pb0"])
                        S.op("act", lambda e: e.copy(out=go_sb[d][:], in_=PB[0][0:64, 0:384]), reads=["pb0"], writes=[("go_sb", d)])
                        S.dma("pool", lambda e: e.dma_start(out=OFB[d][n * 64:(n + 1) * 64, 0:384], in_=go_sb[d][:]),
                              reads=[("go_sb", d)], writes=[("OFB", d)])
                        yield
                        kv_ps = PB[1][0:64, 0:384].rearrange("p (h v) -> p h v", h=4)
                        for h in range(4):
                            S.op("pe", lambda e, h=h: e.matmul(kv_ps[:, h, :], lhsT=ktl[d][:, h * 64:(h + 1) * 64],
                                                               rhs=kvg[:, ci, 256 + h * 96:256 + (h + 1) * 96], start=True, stop=True),
                                 reads=[("ktl", d), kkv_], writes=["pb1"])
                        dcol = 63 if d == 0 else 0
                        S.op("pool", lambda e: e.tensor_tensor(out=gS32[d][:], in0=gS32[d][:],
                                                               in1=ebT[d][:, :, dcol:dcol + 1].broadcast_to([64, 4, 96]), op=ALU.mult),
                             reads=[("gS32", d), ("ebT", d)], writes=[("gS32", d)])
                        S.op("dve", lambda e: e.tensor_tensor(out=gS32[d][:], in0=gS32[d][:], in1=kv_ps, op=ALU.add),
                             reads=[("gS32", d), "pb1"], writes=[("gS32", d)])
                        S.op("act", lambda e: e.copy(out=gSbf[d][:], in_=gS32[d][:]), reads=[("gS32", d)], writes=[("gSbf", d)])

                    def gla_step(ns):
                        two_ = range(2)
                        info = []
                        for d in two_:
                            grp, ci = divmod(ns[d], 4)
                            gs = grp % 2
                            info.append((glaq[d][gs], glak[d][gs], glakv[d][gs], glala[d][gs], ("glaq", d, gs), ("glak", d, gs), ("glakv", d, gs),
                                         ("glala", d, gs), slice(ci * 64, (ci + 1) * 64), ci))
                        mIl, mSl = (LE, GE), (GT, LT)
                        mI2 = masks5[:, 0:2, :, :].rearrange("p t r c -> p (t r) c")
                        v64 = lambda t_: t_[0:64, :].rearrange("p (u c) -> p u c", u=8)
                        bT_ps = v64(PB[0])
                        for d in two_:
                            lag, kla_, ci = info[d][3], info[d][7], info[d][9]
                            for h in range(4):
                                u = d * 4 + h
                                S.op("pe", lambda e, u=u, h=h, lag=lag, ci=ci, d=d: e.matmul(bT_ps[:, u, :], lhsT=lag[:, ci, h * 64:(h + 1) * 64],
                                                                                              rhs=masks5[:, mIl[d], 0, :], start=True, stop=True),
                                     reads=[kla_, "c_masks"], writes=["pb0"])
                            S.op("pe", lambda e, lag=lag, ci=ci, d=d: e.matmul(PB[1][0:64, d * 256:(d + 1) * 256], lhsT=masks5[:, mSl[d], 0, :], rhs=lag[:, ci, :],
                                                                              start=True, stop=True), reads=[kla_, "c_masks"], writes=["pb1"])
                        S.op("act", lambda e: e.activation(out=ebT8[:], in_=bT_ps, func=AF.Exp), reads=["pb0"], writes=["ebT8"])
                        S.op("act", lambda e: e.activation(out=enbT8[:], in_=bT_ps, func=AF.Exp, scale=-1.0), reads=["pb0"], writes=["enbT8"])
                        S.op("act", lambda e: e.activation(out=erem8[:], in_=PB[1][0:64, :], func=AF.Exp), reads=["pb1"], writes=["erem8"])
                        yield
                        for d in two_:
                            qg, kg, kvg, kq_, kk_, kkv_, cs, ci = info[d][0], info[d][1], info[d][2], info[d][4], info[d][5], info[d][6], info[d][8], info[d][9]
                            S.op("dve", lambda e, d=d, qg=qg, cs=cs: e.tensor_tensor(out=qeT8[:, d * 4:(d + 1) * 4, :], in0=qg[:, :, cs],
                                                                                     in1=ebT8[:, d * 4:(d + 1) * 4, :], op=ALU.mult),
                                 reads=[kq_, "ebT8"], writes=["qeT8"])
                            S.op("pool", lambda e, d=d, kg=kg, cs=cs: e.tensor_tensor(out=keT8[:, d * 4:(d + 1) * 4, :], in0=kg[:, :, cs],
                                                                                      in1=enbT8[:, d * 4:(d + 1) * 4, :], op=ALU.mult),
                                 reads=[kk_, "enbT8"], writes=["keT8"])
                            S.op("dve", lambda e, d=d, kvg=kvg, ci=ci: e.tensor_tensor(out=ktl8[:, d * 256:(d + 1) * 256], in0=kvg[:, ci, 0:256],
                                                                                       in1=erem8[:, d * 256:(d + 1) * 256], op=ALU.mult),
                                 reads=[kkv_, "erem8"], writes=["ktl8"])
                        yield
                        AT_ps = v64(PB[0])
                        for u in range(8):
                            S.op("pe", lambda e, u=u: e.matmul(AT_ps[:, u, :], lhsT=keT8[:, u, :], rhs=qeT8[:, u, :], start=True, stop=True),
                                 reads=["keT8", "qeT8"], writes=["pb0"])
                        S.op("dve", lambda e: e.tensor_tensor(out=ATm8[:], in0=AT_ps, in1=mI2, op=ALU.mult), reads=["pb0", "c_masks"], writes=["ATm8"])
                        yield
                        for d in two_:
                            kvg, kkv_, ci = info[d][2], info[d][6], info[d][9]
                            o_ps = PB[d][0:64, 0:384].rearrange("p (h v) -> p h v", h=4)
                            for h in range(4):
                                u = d * 4 + h
                                S.op("pe", lambda e, u=u, h=h, kvg=kvg, ci=ci, o_ps=o_ps: e.matmul(o_ps[:, h, :], lhsT=ATm8[:, u, :],
                                                                                                 rhs=kvg[:, ci, 256 + h * 96:256 + (h + 1) * 96],
                                                                                                 start=True, stop=False),
                                     reads=["ATm8", kkv_], writes=[f"pb{d}"])
                                S.op("pe", lambda e, u=u, h=h, o_ps=o_ps: e.matmul(o_ps[:, h, :], lhsT=qeT8[:, u, :], rhs=gSbf8[:, u, :], start=False, stop=True),
                                     reads=["qeT8", "gSbf8"], writes=[f"pb{d}"])
                            S.op("act", lambda e, d=d: e.copy(out=go_sb[d][:], in_=PB[d][0:64, 0:384]), reads=[f"pb{d}"], writes=[("go_sb", d)])
                            S.dma("pool", lambda e, d=d: e.dma_start(out=OFB[d][ns[d] * 64:(ns[d] + 1) * 64, 0:384], in_=go_sb[d][:]),
                                  reads=[("go_sb", d)], writes=[("OFB", d)])
                        yield
                        for d in two_:
                            kvg, kkv_, ci = info[d][2], info[d][6], info[d][9]
                            dcol = 63 if d == 0 else 0
                            kv_ps = PB[d][0:64, 0:384].rearrange("p (h v) -> p h v", h=4)
                            for h in range(4):
                                S.op("pe", lambda e, d=d, h=h, kvg=kvg, ci=ci, kv_ps=kv_ps: e.matmul(kv_ps[:, h, :], lhsT=ktl8[:, d * 256 + h * 64:d * 256 + (h + 1) * 64],
                                                                                                   rhs=kvg[:, ci, 256 + h * 96:256 + (h + 1) * 96], start=True, stop=True),
                                     reads=["ktl8", kkv_], writes=[f"pb{d}"])
                            for h in range(4):
                                u = d * 4 + h
                                S.op("dve", lambda e, u=u, h=h, dcol=dcol, kv_ps=kv_ps: e.scalar_tensor_tensor(
                                    out=gS328[:, u, :], in0=gS328[:, u, :], scalar=ebT8[:, u, dcol:dcol + 1], in1=kv_ps[:, h, :], op0=ALU.mult, op1=ALU.add),
                                    reads=["gS328", "ebT8", f"pb{d}"], writes=["gS328"])
                        S.op("act", lambda e: e.copy(out=gSbf8[:], in_=gS328[:]), reads=["gS328"], writes=["gSbf8"])
                        yield


                    def gdn_step(ns):
                        import os as _os
                        two_ = range(2)
                        info = []
                        for d in two_:
                            grp, ci = divmod(ns[d], 4)
                            gs = grp % 2
                            info.append((gdnc2[d][gs], gdngb[d][gs], ("gdnc2", d, gs), ("gdngb", d, gs), slice(ci * 64, (ci + 1) * 64), ci))
                        m8 = lambda a, b_: masks5[:, a:b_, :, :].rearrange("p t r c -> p (t r) c")
                        mI2, mS2, mSo2 = m8(0, 2), m8(2, 4), m8(3, 5)
                        mIl, mSl = (LE, GE), (GT, LT)
                        b8 = lambda ap, w, p=64: ap.unsqueeze(2).broadcast_to([p, 8, w])
                        idb8 = ident_f[:].unsqueeze(1).broadcast_to([64, 8, 64])
                        v64 = lambda t_: t_[0:64, :].rearrange("p (u c) -> p u c", u=8)
                        for d in two_:
                            gbg, kgb, ci = info[d][1], info[d][3], info[d][5]
                            S.op("pool", lambda e, d=d, gbg=gbg, ci=ci: e.tensor_copy(
                                out=gb8[:].rearrange("p (t z h) -> p t z h", t=2, z=2)[:, :, d, :],
                                in_=gbg[:, ci, :].rearrange("p (t z h) -> p t z h", t=2, z=2)[:, :, d, :]), reads=[kgb], writes=["gb8"])
                        beta8, g8 = gb8[:, 0:8], gb8[:, 8:16]
                        for d in two_:
                            c2g, kc2, cs = info[d][0], info[d][2], info[d][4]
                            tr_ps = PB[2 + d][0:64, 0:384].bitcast(BF16).rearrange("p (u x) -> p u x", u=8)
                            for u in range(8):
                                S.op("pe", lambda e, u=u, c2g=c2g, cs=cs, tr_ps=tr_ps: e.transpose(out=tr_ps[:, u, :], in_=c2g[:, 4 + u, cs],
                                                                                                 identity=ident_bf[0:96, 0:96]),
                                     reads=[kc2, "c_idbf"], writes=[f"pb{2 + d}"])
                            S.op("act", lambda e, d=d, tr_ps=tr_ps: e.copy(out=K8[:, d * 4:(d + 1) * 4, :], in_=tr_ps[:, 0:4, :]), reads=[f"pb{2 + d}"], writes=["K8"])
                            S.op("act", lambda e, d=d, tr_ps=tr_ps: e.copy(out=V8[:, d * 4:(d + 1) * 4, :], in_=tr_ps[:, 4:8, :]), reads=[f"pb{2 + d}"], writes=["V8"])
                        yield
                        for d in two_:
                            S.op("pe", lambda e, d=d: e.matmul(PB[3][0:64, 384 + d * 4:388 + d * 4], lhsT=masks5[:, mIl[d], 0, :], rhs=g8[:, d * 4:(d + 1) * 4],
                                                               start=True, stop=True), reads=["c_masks", "gb8"], writes=["pb3"])
                            S.op("pe", lambda e, d=d: e.matmul(PB[3][0:64, 392 + d * 4:396 + d * 4], lhsT=masks5[:, mSl[d], 0, :], rhs=g8[:, d * 4:(d + 1) * 4],
                                                               start=True, stop=True), reads=["c_masks", "gb8"], writes=["pb3"])
                        S.op("pe", lambda e: e.matmul(PB[3][0:96, 400:408], lhsT=ones_f[:, :], rhs=g8, start=True, stop=True),
                             reads=["c_onesf", "gb8"], writes=["pb3"])
                        S.op("act", lambda e: e.activation(out=eg16[:], in_=PB[3][0:64, 384:400], func=AF.Exp), reads=["pb3"], writes=["eg16"])
                        S.op("act", lambda e: e.activation(out=dec96[:], in_=PB[3][0:96, 400:408], func=AF.Exp), reads=["pb3"], writes=["dec96"])
                        yield
                        kk_ps, qk_ps = v64(PB[4]), v64(PB[5])
                        for d in two_:
                            c2g, kc2, cs = info[d][0], info[d][2], info[d][4]
                            for h in range(4):
                                u = d * 4 + h
                                S.op("pe", lambda e, u=u, h=h, c2g=c2g, cs=cs: e.matmul(kk_ps[:, u, :], lhsT=c2g[:, 4 + h, cs], rhs=c2g[:, 4 + h, cs],
                                                                                        start=True, stop=True), reads=[kc2], writes=["pb4"])
                                S.op("pe", lambda e, u=u, h=h, c2g=c2g, cs=cs: e.matmul(qk_ps[:, u, :], lhsT=c2g[:, 4 + h, cs], rhs=c2g[:, h, cs],
                                                                                        start=True, stop=True), reads=[kc2], writes=["pb5"])
                        yield
                        S.op("dve", lambda e: e.tensor_tensor(out=R8[:], in0=mI2, in1=b8(g8, 64), op=ALU.mult), reads=["c_masks", "gb8"], writes=["R8"])
                        S.op("dve", lambda e: e.tensor_tensor(out=kkN[:], in0=kk_ps, in1=mS2, op=ALU.mult), reads=["pb4", "c_masks"], writes=["kkN"])
                        S.op("dve", lambda e: e.tensor_tensor(out=kkM[:], in0=kk_ps, in1=mSo2, op=ALU.mult), reads=["pb4", "c_masks"], writes=["kkM"])
                        S.op("dve", lambda e: e.tensor_tensor(out=qkI[:], in0=qk_ps, in1=mI2, op=ALU.mult), reads=["pb5", "c_masks"], writes=["qkI"])
                        Rflat = R8[:].rearrange("p u c -> p (u c)")
                        yield
                        for d in two_:
                            S.op("pe", lambda e, d=d: e.matmul(PB[2][0:64, d * 256:(d + 1) * 256], lhsT=masks5[:, mSl[d], 0, :], rhs=Rflat[:, d * 256:(d + 1) * 256],
                                                               start=True, stop=True), reads=["c_masks", "R8"], writes=["pb2"])
                        for u in range(8):
                            S.op("pe", lambda e, u=u: e.matmul(PB[3][0:64, u * 64:(u + 1) * 64], lhsT=R8[:, u, :], rhs=masks5[:, mSl[u // 4], 0, :],
                                                               start=True, stop=True), reads=["c_masks", "R8"], writes=["pb3"])
                        S.op("act", lambda e: e.activation(out=decT[:], in_=v64(PB[2]), func=AF.Exp), reads=["pb2"], writes=["decT"])
                        S.op("act", lambda e: e.activation(out=decD[:], in_=v64(PB[3]), func=AF.Exp), reads=["pb3"], writes=["decD"])
                        yield
                        S.op("pe", lambda e: e.matmul(PB[6][0:96, :], lhsT=ones_f[:, :], rhs=Rflat, start=True, stop=True),
                             reads=["c_onesf", "R8"], writes=["pb6"])
                        S.op("act", lambda e: e.activation(out=EGB8[:], in_=PB[6][0:96, :].rearrange("p (u c) -> p u c", u=8), func=AF.Exp),
                             reads=["pb6"], writes=["EGB8"])
                        yield
                        S.op("pool", lambda e: e.tensor_tensor(out=DB8[:], in0=idb8, in1=b8(beta8, 64), op=ALU.mult), reads=["c_idf", "gb8"], writes=["DB8"])
                        S.op("pe", lambda e: e.matmul(PB[7][0:64, :], lhsT=ones_bf[0:64, 0:64], rhs=DB8[:].rearrange("p u c -> p (u c)"), start=True, stop=True),
                             reads=["c_onesbf", "DB8"], writes=["pb7"])
                        yield
                        S.op("dve", lambda e: e.tensor_tensor(out=kkN[:], in0=kkN[:], in1=decD[:], op=ALU.mult), reads=["kkN", "decD"], writes=["kkN"])
                        S.op("pool", lambda e: e.tensor_tensor(out=Nm[0][:], in0=kkN[:], in1=b8(beta8, 64), op=ALU.mult), reads=["kkN", "gb8"], writes=[("Nm", 0)])
                        S.op("dve", lambda e: e.tensor_tensor(out=kkM[:], in0=kkM[:], in1=decT[:], op=ALU.mult), reads=["kkM", "decT"], writes=["kkM"])
                        S.op("dve", lambda e: e.tensor_tensor(out=Mm[0][:], in0=kkM[:], in1=v64(PB[7]), op=ALU.mult), reads=["kkM", "pb7"], writes=[("Mm", 0)])
                        S.op("pool", lambda e: e.tensor_tensor(out=Pm8[:], in0=idb8, in1=Mm[0][:], op=ALU.subtract), reads=[("Mm", 0), "c_idf"], writes=["Pm8"])
                        S.op("pool", lambda e: e.tensor_tensor(out=qkm8[0:64, :, :], in0=qkI[:], in1=decT[:], op=ALU.mult), reads=["qkI", "decT"], writes=["qkm8"])
                        yield
                        S.op("pool", lambda e: e.tensor_tensor(out=vb8[:], in0=V8[:], in1=b8(beta8, 96), op=ALU.mult), reads=["V8", "gb8"], writes=["vb8"])
                        S.op("dve", lambda e: e.tensor_tensor(out=bg8[:], in0=beta8, in1=eg16[:, 0:8], op=ALU.mult), reads=["gb8", "eg16"], writes=["bg8"])
                        S.op("pool", lambda e: e.tensor_tensor(out=kbg8[:], in0=K8[:], in1=b8(bg8[:], 96), op=ALU.mult), reads=["K8", "bg8"], writes=["kbg8"])
                        S.op("pool", lambda e: e.tensor_tensor(out=kte8[:], in0=K8[:], in1=b8(eg16[:, 8:16], 96), op=ALU.mult), reads=["K8", "eg16"], writes=["kte8"])
                        for d in two_:
                            c2g, kc2, cs = info[d][0], info[d][2], info[d][4]
                            S.op("pool", lambda e, d=d, c2g=c2g, cs=cs: e.tensor_tensor(out=qdT8[:, d * 4:(d + 1) * 4, :], in0=c2g[:, 0:4, cs],
                                                                                       in1=EGB8[:, d * 4:(d + 1) * 4, :], op=ALU.mult),
                                 reads=[kc2, "EGB8"], writes=["qdT8"])
                        yield
                        sqM, sqN, pd = v64(PB[4]), v64(PB[5]), v64(PB[6])

                        def emit_sq(lev, cur, nxt):
                            lastlev = (lev == 4)
                            for u in range(8):
                                if not lastlev:
                                    S.op("pe", lambda e, u=u: e.matmul(sqM[:, u, :], lhsT=Nm[cur][:, u, :], rhs=Mm[cur][:, u, :], start=True, stop=True),
                                         reads=[("Nm", cur), ("Mm", cur)], writes=["pb4"])
                                S.op("pe", lambda e, u=u: e.matmul(sqN[:, u, :], lhsT=Mm[cur][:, u, :], rhs=Nm[cur][:, u, :], start=True, stop=True),
                                     reads=[("Nm", cur), ("Mm", cur)], writes=["pb5"])
                            S.op("act", lambda e: e.copy(out=Nm[nxt][:], in_=sqN), reads=["pb5"], writes=[("Nm", nxt)])
                            if not lastlev:
                                S.op("dve", lambda e: e.tensor_copy(out=Mm[nxt][:], in_=sqM), reads=["pb4"], writes=[("Mm", nxt)])

                        def emit_pd(lev, nb_):
                            for u in range(8):
                                S.op("pe", lambda e, u=u: e.matmul(pd[:, u, :], lhsT=Nm[nb_][:, u, :], rhs=Pm8[:, u, :], start=True, stop=True),
                                     reads=[("Nm", nb_), "Pm8"], writes=["pb6"])
                            if lev < 4:
                                S.op("dve", lambda e: e.tensor_tensor(out=Pm8[:], in0=Pm8[:], in1=pd, op=ALU.add), reads=["Pm8", "pb6"], writes=["Pm8"])
                            else:
                                S.op("dve", lambda e: e.tensor_tensor(out=TiT8[:], in0=Pm8[:], in1=pd, op=ALU.add), reads=["Pm8", "pb6"], writes=["TiT8"])

                        emit_sq(0, 0, 1)
                        yield
                        for lev in range(5):
                            if lev < 4:
                                emit_sq(lev + 1, (lev + 1) % 2, (lev + 2) % 2)
                            emit_pd(lev, (lev + 1) % 2)
                            yield

                        wT_ps = PB[7][0:96, :].rearrange("p (u c) -> p u c", u=8)
                        for d in two_:
                            u_ps = PB[2 + d][0:64, 0:384].rearrange("p (h v) -> p h v", h=4)
                            for h in range(4):
                                u = d * 4 + h
                                S.op("pe", lambda e, u=u, h=h, u_ps=u_ps: e.matmul(u_ps[:, h, :], lhsT=TiT8[:, u, :], rhs=vb8[:, u, :], start=True, stop=True),
                                     reads=["TiT8", "vb8"], writes=[f"pb{2 + d}"])
                            S.op("act", lambda e, d=d, u_ps=u_ps: e.copy(out=u8[:, d * 4:(d + 1) * 4, :], in_=u_ps), reads=[f"pb{2 + d}"], writes=["u8"])
                        for u in range(8):
                            S.op("pe", lambda e, u=u: e.matmul(wT_ps[:, u, :], lhsT=kbg8[:, u, :], rhs=TiT8[:, u, :], start=True, stop=True),
                                 reads=["TiT8", "kbg8"], writes=["pb7"])
                        S.op("act", lambda e: e.copy(out=wT8[:], in_=wT_ps), reads=["pb7"], writes=["wT8"])
                        yield
                        for d in two_:
                            ws_ps = PB[4 + d][0:64, 0:384].rearrange("p (h v) -> p h v", h=4)
                            for h in range(4):
                                u = d * 4 + h
                                S.op("pe", lambda e, u=u, h=h, ws_ps=ws_ps: e.matmul(ws_ps[:, h, :], lhsT=wT8[:, u, :], rhs=dSbf8[:, u, :], start=True, stop=True),
                                     reads=["wT8", "dSbf8"], writes=[f"pb{4 + d}"])
                            S.op("dve", lambda e, d=d, ws_ps=ws_ps: e.tensor_tensor(out=vnew8[0:64, d * 4:(d + 1) * 4, :], in0=u8[:, d * 4:(d + 1) * 4, :], in1=ws_ps,
                                                                                    op=ALU.subtract), reads=["u8", f"pb{4 + d}"], writes=["vnew8"])
                        yield
                        for d in two_:
                            o_ps = PB[2 + d][0:64, 0:384].rearrange("p (h v) -> p h v", h=4)
                            for h in range(4):
                                u = d * 4 + h
                                S.op("pe", lambda e, u=u, h=h, o_ps=o_ps: e.matmul(o_ps[:, h, :], lhsT=qdT8[:, u, :], rhs=dSbf8[:, u, :], start=True, stop=False),
                                     reads=["qdT8", "dSbf8"], writes=[f"pb{2 + d}"])
                                S.op("pe", lambda e, u=u, h=h, o_ps=o_ps: e.matmul(o_ps[:, h, :], lhsT=qkm8[:, u, :], rhs=vnew8[:, u, :], start=False, stop=True),
                                     reads=["qkm8", "vnew8"], writes=[f"pb{2 + d}"])
                            S.op("act", lambda e, d=d: e.copy(out=do_sb[d][:], in_=PB[2 + d][0:64, 0:384]), reads=[f"pb{2 + d}"], writes=[("do_sb", d)])
                            S.dma("pool", lambda e, d=d: e.dma_start(out=OFB[d][ns[d] * 64:(ns[d] + 1) * 64, 384:768], in_=do_sb[d][:]),
                                  reads=[("do_sb", d)], writes=[("OFB", d)])
                        for d in two_:
                            kv_ps = PB[6 + d][0:96, 0:384].rearrange("p (h v) -> p h v", h=4)
                            for h in range(4):
                                u = d * 4 + h
                                S.op("pe", lambda e, u=u, h=h, kv_ps=kv_ps: e.matmul(kv_ps[:, h, :], lhsT=kte8[:, u, :], rhs=vnew8[0:64, u, :], start=True, stop=True),
                                     reads=["kte8", "vnew8"], writes=[f"pb{6 + d}"])
                            for h in range(4):
                                u = d * 4 + h
                                S.op("dve", lambda e, u=u, h=h, kv_ps=kv_ps: e.scalar_tensor_tensor(
                                    out=dS328[:, u, :], in0=dS328[:, u, :], scalar=dec96[:, u:u + 1], in1=kv_ps[:, h, :], op0=ALU.mult, op1=ALU.add),
                                    reads=["dS328", "dec96", f"pb{6 + d}"], writes=["dS328"])
                        S.op("act", lambda e: e.copy(out=dSbf8[:], in_=dS328[:]), reads=["dS328"], writes=["dSbf8"])


                    load_group(0, 0)
                    load_group(1, 15)
                    import os as _os
                    for j in range(int(_os.environ.get('K_NSTEPS', NCH))):
                        nf, nbk = j, NCH - 1 - j
                        if j % 4 == 0 and j + 4 < NCH:
                            load_group(0, j // 4 + 1)
                            load_group(1, 15 - (j // 4 + 1))
                        fillers = [gla_step((nf, nbk))] if "gla" in parts else []
                        main = gdn_step((nf, nbk)) if "gdn" in parts else iter(())
                        fi = 0
                        if not _osm.environ.get('K_ILV'):
                            for _ in range(int(_osm.environ.get('K_PRE', 0))):
                                next(main, "done")
                            for f_ in fillers:
                                for _ in f_:
                                    pass
                            fillers = []
                        while True:
                            main_alive = next(main, "done") != "done"
                            adv = False
                            while fi < len(fillers):
                                if next(fillers[fi], "done") != "done":
                                    adv = True
                                    break
                                fi += 1
                            if not main_alive and not adv:
                                break
                    S.barrier()

            mark(f'L{l} p3 done')
            if 4 in passes:
                with contextlib.ExitStack() as ps:
                    sbp = lambda n, s, dt: ps.enter_context(nc.sbuf_tensor(f"sbD{l}_" + n, s, dt))
                    two = range(2)
                    of_t = [sbp(f"of_t{i}", [128, 768], F32) for i in range(4)]
                    ob_t = [sbp(f"ob_t{i}", [128, 768], F32) for i in range(4)]
                    gt_t = [sbp(f"gt_t{i}", [128, 768], BF16) for i in range(4)]
                    oc_all = [sbp(f"oc_all{i}", [128, 1024], BF16) for i in range(4)]
                    x_t = [sbp(f"x_t{i}", [128, D], F32) for i in range(4)]
                    xn = [sbp(f"xn{i}", [128, D], F32) for i in range(4)]
                    osq = [sbp(f"osq{i}", [128, 768], F32) for i in range(4)]
                    ss8 = [sbp(f"ss8{i}", [128, 8], F32) for i in range(4)]
                    ocT = [sbp(f"ocT{i}", [128, 8, 128], BF16) for i in range(4)]
                    junk4 = [sbp(f"junk4{i}", [128, D], BF16) for i in range(4)]
                    ss4 = [sbp(f"ss4{i}", [128, 1], F32) for i in range(4)]
                    def p4_loads(tt):
                        r0 = tt * 128
                        sl = tt % 4
                        S.dma("sp", lambda e, sl=sl, r0=r0: e.dma_start(out=of_t[sl][:], in_=OFB[0][r0:r0 + 128, :]), reads=[("OFB", 0)], writes=[("of_t", sl)])
                        S.dma("sp", lambda e, sl=sl, r0=r0: e.dma_start(out=ob_t[sl][:], in_=OFB[1][r0:r0 + 128, :]), reads=[("OFB", 1)], writes=[("ob_t", sl)])
                        S.dma("sp", lambda e, sl=sl, r0=r0: e.dma_start(out=gt_t[sl][:], in_=GATE[r0:r0 + 128, 0:768]), reads=["GATE"], writes=[("gt_t", sl)])
                        S.dma("sp", lambda e, sl=sl, r0=r0: e.dma_start(out=oc_all[sl][:, 768:1024], in_=OC3[r0:r0 + 128, :]), reads=["OC3"], writes=[("oc_all", sl)])
                        S.dma("sp", lambda e, sl=sl, r0=r0: e.dma_start(out=x_t[sl][:], in_=xsrc[r0:r0 + 128, :]), reads=["XSRC%d" % l], writes=[("x_t", sl)])

                    def p4_tile(tt):
                        r0 = tt * 128
                        sl = tt % 4
                        pbase = 3 * (tt % 2)
                        if tt + 2 < T // 128:
                            p4_loads(tt + 2)
                        yield
                        S.op("dve", lambda e, sl=sl: e.tensor_tensor(out=of_t[sl][:], in0=of_t[sl][:], in1=ob_t[sl][:], op=ALU.add),
                             reads=[("of_t", sl), ("ob_t", sl)], writes=[("of_t", sl)])
                        S.op("pool", lambda e, sl=sl: e.tensor_tensor(out=osq[sl][:], in0=of_t[sl][:], in1=of_t[sl][:], op=ALU.mult),
                             reads=[("of_t", sl)], writes=[("osq", sl)])
                        S.op("dve", lambda e, sl=sl: e.tensor_reduce(out=ss8[sl][:], in_=osq[sl][:].rearrange("p (h v) -> p h v", h=8), axis=AX.X, op=ALU.add),
                             reads=[("osq", sl)], writes=[("ss8", sl)])
                        yield
                        rms_rstd(ss8[sl][:], ss8[sl][:], 96, ("ss8", sl), ("ss8", sl))
                        yield
                        S.op("dve", lambda e, sl=sl: e.tensor_tensor(out=of_t[sl][:].rearrange("p (h v) -> p h v", h=8),
                                                                     in0=of_t[sl][:].rearrange("p (h v) -> p h v", h=8),
                                                                     in1=ss8[sl][:].unsqueeze(2).broadcast_to([128, 8, 96]), op=ALU.mult),
                             reads=[("of_t", sl), ("ss8", sl)], writes=[("of_t", sl)])
                        S.op("pool", lambda e, sl=sl: e.tensor_tensor(out=of_t[sl][:], in0=of_t[sl][:], in1=ohnorm[:], op=ALU.mult),
                             reads=[("of_t", sl), "ohnorm"], writes=[("of_t", sl)])
                        S.op("dve", lambda e, sl=sl: e.tensor_tensor(out=oc_all[sl][:, 0:768], in0=of_t[sl][:], in1=gt_t[sl][:], op=ALU.mult),
                             reads=[("of_t", sl), ("gt_t", sl)], writes=[("oc_all", sl)])
                        yield
                        ptr = PB[pbase][:, 0:512].bitcast(BF16).rearrange("p (c t) -> p c t", c=8)
                        for c in range(8):
                            S.op("pe", lambda e, c=c, sl=sl: e.transpose(out=ptr[:, c, :], in_=oc_all[sl][:, c * 128:(c + 1) * 128], identity=ident_bf[:]),
                                 reads=[("oc_all", sl), "c_idbf"], writes=[f"pb{pbase}"])
                        S.op("act", lambda e, sl=sl: e.copy(out=ocT[sl][:], in_=ptr), reads=[f"pb{pbase}"], writes=[("ocT", sl)])
                        yield
                        for hf in range(2):
                            bk = pbase + 1 + hf
                            for c in range(8):
                                S.op("pe", lambda e, c=c, hf=hf, bk=bk, sl=sl: e.matmul(PB[bk][:, :], lhsT=ocT[sl][:, c, :], rhs=w_out_sb[:, c, hf * 512:(hf + 1) * 512],
                                                                                 start=(c == 0), stop=(c == 7)), reads=[("ocT", sl), "w_out_sb"], writes=[f"pb{bk}"])
                            S.op("dve", lambda e, hf=hf, bk=bk, sl=sl: e.tensor_tensor(out=xn[sl][:, hf * 512:(hf + 1) * 512], in0=x_t[sl][:, hf * 512:(hf + 1) * 512],
                                                                                       in1=PB[bk][:, :], op=ALU.add),
                                 reads=[("x_t", sl), f"pb{bk}"], writes=[("xn", sl)])
                        yield
                        if not last:
                            S.dma("pool", lambda e, sl=sl, r0=r0: e.dma_start(out=X1[r0:r0 + 128, :], in_=xn[sl][:]), reads=[("xn", sl)], writes=["XSRC%d" % (l + 1)])
                        else:
                            S.op("act", lambda e, sl=sl: e.activation(out=junk4[sl][:], in_=xn[sl][:], func=AF.Square, accum_out=ss4[sl][:]),
                                 reads=[("xn", sl)], writes=[("junk4", sl), ("ss4", sl)])
                            rms_rstd(ss4[sl][:], ss4[sl][:], D, ("ss4", sl), ("ss4", sl))
                            S.op("dve", lambda e, sl=sl: e.tensor_scalar(out=xn[sl][:], in0=xn[sl][:], scalar1=ss4[sl][:], scalar2=None, op0=ALU.mult),
                                 reads=[("xn", sl), ("ss4", sl)], writes=[("xn", sl)])
                            S.op("pool", lambda e, sl=sl: e.tensor_tensor(out=xn[sl][:], in0=xn[sl][:], in1=fnorm[:], op=ALU.mult),
                                 reads=[("xn", sl), "c_fnorm"], writes=[("xn", sl)])
                            S.dma("pool", lambda e, sl=sl, r0=r0: e.dma_start(out=out[r0:r0 + 128, :], in_=xn[sl][:]), reads=[("xn", sl)], writes=["out"])
                    p4_loads(0)
                    p4_loads(1)
                    active = []
                    nxt_tt = 0
                    while True:
                        while len(active) < int(_osm.environ.get('K_P4W', 2)) and nxt_tt < T // 128:
                            active.append(p4_tile(nxt_tt))
                            nxt_tt += 1
                        if not active:
                            break
                        for g_ in list(active):
                            if next(g_, "done") == "done":
                                active.remove(g_)
                    S.barrier()

        for l_ in range(nlayers):
            emit_layer(l_)
        S.barrier()
        S.finish()
        mark("end")
        print("MARKS", marks)
        print("ops", S.n_ops, "waits", S.n_waits, {e: len(S.ops[e]) for e in S.ENGS})
    return nc

def host_inputs(inputs, b):
    f32 = np.float32
    g = lambda k: np.asarray(inputs[k], dtype=f32)
    d = {}
    d["x"] = np.ascontiguousarray(g("x")[b])
    d["mem"] = np.ascontiguousarray(g("mem")[b])
    d["w_in"] = g("w_in")
    d["w_out"] = g("w_out")
    d["xa_w_kv"] = g("xa_w_kv")
    d["normw_pc"] = np.ascontiguousarray(g("norm_w").reshape(NL, 8, 128).transpose(0, 2, 1))
    d["memnormw_pc"] = np.ascontiguousarray(g("mem_norm_w").reshape(NL, 8, 128).transpose(0, 2, 1))
    d["w2"] = np.ascontiguousarray(g("gla_w2").transpose(0, 2, 1, 3))
    d["bias_bc"] = np.ascontiguousarray(np.broadcast_to(g("gla_b").reshape(NL, 1, 512), (NL, 128, 512)))
    ohn = np.concatenate([np.tile(g("gla_norm_w"), (1, 4)), np.tile(g("gdn_norm_w"), (1, 4))], axis=1)
    d["ohnorm_bc"] = np.ascontiguousarray(np.broadcast_to(ohn.reshape(NL, 1, 768), (NL, 128, 768)))
    d["xanorm_bc"] = np.ascontiguousarray(np.broadcast_to(np.tile(g("xa_norm_w"), (1, 4)).reshape(NL, 1, 256), (NL, 128, 256)))
    d["convw"] = np.ascontiguousarray(g("gdn_conv_w").reshape(NL, 12, 96, 5).transpose(0, 2, 1, 3))
    d["alog_bc"] = np.ascontiguousarray(np.broadcast_to(g("gdn_a_log").reshape(NL, 1, 8), (NL, 128, 8)))
    d["dtb_bc"] = np.ascontiguousarray(np.broadcast_to(g("gdn_dt_bias").reshape(NL, 1, 8), (NL, 128, 8)))
    d["fnorm_bc"] = np.ascontiguousarray(np.broadcast_to(g("final_norm_w").reshape(1, D), (128, D)))
    d["ident_bf"] = np.eye(128, dtype=f32).astype(ml_dtypes.bfloat16)
    d["ident_f"] = np.eye(64, dtype=f32)
    p = np.arange(64)[:, None]
    q = np.arange(64)[None, :]
    m4 = np.stack([(p <= q), (p >= q), (p > q), (p < q), (p > q)]).astype(f32)
    d["masks"] = np.ascontiguousarray(np.broadcast_to(m4.transpose(1, 0, 2)[:, :, None, :], (64, 5, 4, 64)))
    d["ones_f"] = np.ones((64, 96), f32)
    d["ones_bf"] = np.ones((96, 96), f32).astype(ml_dtypes.bfloat16)
    return d


_NC_CACHE = {}


def kernel(**inputs):
    if "nc" not in _NC_CACHE:
        _NC_CACHE["nc"] = build()
    nc = _NC_CACHE["nc"]
    in_maps = [host_inputs(inputs, b) for b in range(8)]
    res = run_bass_kernel_spmd(nc, in_maps, core_ids=list(range(8)))
    return np.stack([np.asarray(r["out"], dtype=np.float32) for r in res.results], axis=0)
```
